# Optimizing a Trainium2 kernel written in Bass

```python
import math
import jax, jax.numpy as jnp
from jax import lax
import numpy as np

D_MODEL = 1024
BATCH = 4
SEQ = 4096
DEPTH = 1
DEC_BATCH = 16
DEC_SEQ = 32
PAST_LEN = 2048

CHUNK = 64
RET_HEADS = 4
RET_DK = 128
RET_DV = 128
SB_HEADS = 8
SB_DH = 64
SB_BLOCK = 128
MIX_WIDTH = RET_HEADS * RET_DV + SB_HEADS * SB_DH
N_MEM = 256
MEM_HEADS = 4
MEM_DH = D_MODEL // MEM_HEADS
D_FF = 2816
CONV_W = 3
ROPE_BASE = 10000.0
LN_EPS = 1e-5
RMS_EPS = 1e-6
DN_ALPHA = (2.0 * DEPTH) ** 0.25
DN_BETA = (8.0 * DEPTH) ** -0.25
IN_SIZES = (RET_HEADS * RET_DK, RET_HEADS * RET_DK, RET_HEADS * RET_DV, RET_HEADS * RET_DV,
            SB_HEADS * SB_DH, SB_HEADS * SB_DH, SB_HEADS * SB_DH)
IN_WIDTH = sum(IN_SIZES)
IN_SPLITS = tuple(int(s) for s in np.cumsum(IN_SIZES)[:-1])

kernel_name = "retention_stickbreaking_hybrid_stream_step"


def layer_norm(x, g, b):
    xf = x.astype(jnp.float32)
    mu = jnp.mean(xf, -1, keepdims=True)
    var = jnp.mean(jnp.square(xf - mu), -1, keepdims=True)
    y = (xf - mu) * lax.rsqrt(var + LN_EPS) * g.astype(jnp.float32) + b.astype(jnp.float32)
    return y.astype(x.dtype)


def split_heads(t, n):
    b, s, w = t.shape
    return t.reshape(b, s, n, w // n).transpose(0, 2, 1, 3)


def merge_heads(t):
    b, h, s, d = t.shape
    return t.transpose(0, 2, 1, 3).reshape(b, s, h * d)


def rope(x, pos):
    half = x.shape[-1] // 2
    inv = 1.0 / (ROPE_BASE ** (jnp.arange(half, dtype=jnp.float32) / half))
    ang = pos.astype(jnp.float32)[:, None] * inv[None, :]
    cos, sin = jnp.cos(ang).astype(x.dtype), jnp.sin(ang).astype(x.dtype)
    x1, x2 = x[..., :half], x[..., half:]
    return jnp.concatenate([x1 * cos - x2 * sin, x1 * sin + x2 * cos], axis=-1)


def retention(q, k, v, s0, chunk):
    b, h, t, dk = q.shape
    dv = v.shape[-1]
    n = t // chunk
    log_g = jnp.log1p(-(2.0 ** (-5.0 - jnp.arange(h, dtype=jnp.float32))))
    idx = jnp.arange(chunk, dtype=jnp.float32)
    diff = idx[:, None] - idx[None, :]
    decay_mask = jnp.where(diff >= 0, jnp.exp(log_g[:, None, None] * jnp.maximum(diff, 0.0)), 0.0)
    q_dec = jnp.exp(log_g[:, None] * (idx + 1.0))
    k_dec = jnp.exp(log_g[:, None] * (chunk - 1.0 - idx))
    chunk_dec = jnp.exp(log_g * chunk)
    qc = q.astype(jnp.float32).reshape(b, h, n, chunk, dk)
    kc = k.astype(jnp.float32).reshape(b, h, n, chunk, dk)
    vc = v.astype(jnp.float32).reshape(b, h, n, chunk, dv)
    scores = jnp.einsum('bhnid,bhnjd->bhnij', qc, kc) * decay_mask[None, :, None]
    o_intra = jnp.einsum('bhnij,bhnje->bhnie', scores, vc)
    u = jnp.einsum('bhnjd,bhnje->nbhde', kc * k_dec[None, :, None, :, None], vc)

    def step(s, u_n):
        return chunk_dec[None, :, None, None] * s + u_n, s

    s_final, s_prev = lax.scan(step, s0.astype(jnp.float32), u)
    o_cross = jnp.einsum('bhnid,nbhde->bhnie', qc * q_dec[None, :, None, :, None], s_prev)
    return (o_intra + o_cross).reshape(b, h, t, dv), s_final


def stick_breaking_block(q, k, v, q_pos, k_pos):
    z = jnp.einsum('bhqd,bhkd->bhqk', q, k).astype(jnp.float32) * (SB_DH ** -0.5)
    valid = k_pos[None, :] < q_pos[:, None]
    log_fail = jnp.where(valid, jax.nn.log_sigmoid(-z), 0.0)
    after = lax.cumsum(log_fail, axis=3, reverse=True) - log_fail
    a = jnp.where(valid, jnp.exp(jax.nn.log_sigmoid(z) + after), 0.0)
    return jnp.einsum('bhqk,bhkd->bhqd', a, v.astype(jnp.float32)).astype(v.dtype)


def stick_breaking_attention(q, k_all, v_all, q_pos):
    b, h, t, d = q.shape
    qb = min(SB_BLOCK, t)
    nb = t // qb
    k_pos = jnp.arange(k_all.shape[2], dtype=jnp.int32)
    q_blocks = q.reshape(b, h, nb, qb, d).transpose(2, 0, 1, 3, 4)
    pos_blocks = q_pos.reshape(nb, qb)
    out = lax.map(lambda a: stick_breaking_block(a[0], k_all, v_all, a[1], k_pos), (q_blocks, pos_blocks))
    return out.transpose(1, 2, 0, 3, 4).reshape(b, h, t, d)


def memory_attention(x, mem_k, mem_v, w_q, w_o):
    q = split_heads(x @ w_q, MEM_HEADS)
    s = jnp.einsum('bhtd,bhmd->bhtm', q, mem_k.astype(q.dtype)).astype(jnp.float32) * (MEM_DH ** -0.5)
    p = jax.nn.softmax(s, axis=-1).astype(x.dtype)
    o = jnp.einsum('bhtm,bhmd->bhtd', p, mem_v.astype(x.dtype))
    return merge_heads(o) @ w_o


def conv_ffn(x, conv_past, w_up, conv_w, conv_b, w_down):
    t = x.shape[1]
    u = x @ w_up
    u_pad = jnp.concatenate([conv_past.astype(u.dtype), u], axis=1)
    c = conv_b
    for j in range(CONV_W):
        c = c + conv_w[j] * u_pad[:, j:j + t]
    a, g = jnp.split(c, 2, axis=-1)
    return (jax.nn.silu(a) * g) @ w_down, u_pad[:, -(CONV_W - 1):]


def trunk_layer(x, mem_k, mem_v, ret_s0, sb_k_past, sb_v_past, conv_past, past_len,
                w_in, w_o, ln1_g, ln1_b, w_q_mem, w_o_mem, ln2_g, ln2_b,
                w_up, conv_w, conv_b, w_down, ln3_g, ln3_b):
    t = x.shape[1]
    pos = past_len + jnp.arange(t, dtype=jnp.int32)
    rq, rk, rv, rg, sq, sk, sv = jnp.split(x @ w_in, IN_SPLITS, axis=-1)
    rq = rope(split_heads(rq, RET_HEADS), pos)
    rk = rope(split_heads(rk, RET_HEADS), pos) * (RET_DK ** -0.5)
    rv = split_heads(rv, RET_HEADS)
    ro, ret_state = retention(rq, rk, rv, ret_s0, min(CHUNK, t))
    ro = ro * lax.rsqrt(jnp.mean(jnp.square(ro), -1, keepdims=True) + RMS_EPS)
    ro = merge_heads(ro.astype(x.dtype)) * jax.nn.silu(rg)
    sq, sk, sv = split_heads(sq, SB_HEADS), split_heads(sk, SB_HEADS), split_heads(sv, SB_HEADS)
    k_all = jnp.concatenate([sb_k_past.astype(sk.dtype), sk], axis=2)
    v_all = jnp.concatenate([sb_v_past.astype(sv.dtype), sv], axis=2)
    so = merge_heads(stick_breaking_attention(sq, k_all, v_all, pos))
    mix = jnp.concatenate([ro, so], axis=-1) @ w_o
    x = layer_norm(DN_ALPHA * x + mix, ln1_g, ln1_b)
    x = layer_norm(DN_ALPHA * x + memory_attention(x, mem_k, mem_v, w_q_mem, w_o_mem), ln2_g, ln2_b)
    f, conv_state = conv_ffn(x, conv_past, w_up, conv_w, conv_b, w_down)
    x = layer_norm(DN_ALPHA * x + f, ln3_g, ln3_b)
    return x, sk, sv, ret_state.astype(ret_s0.dtype), conv_state


def setup_inputs(seed: int = 0) -> dict:
    key = jax.random.key(seed)
    ks = jax.random.split(key, 26)
    f32 = jnp.float32

    def nrm(k, shape, scale):
        return jax.random.normal(k, shape, f32) * scale

    col_scale = jnp.asarray(np.concatenate([
        np.full((s,), DN_BETA if i in (2, 6) else 1.0, np.float32) for i, s in enumerate(IN_SIZES)]))
    return {
        "x_prompt": nrm(ks[0], (BATCH, SEQ, D_MODEL), 1.0),
        "x_sample": nrm(ks[1], (DEC_BATCH, DEC_SEQ, D_MODEL), 1.0),
        "cache_sb_k": nrm(ks[2], (DEPTH, DEC_BATCH, SB_HEADS, PAST_LEN, SB_DH), 1.0),
        "cache_sb_v": nrm(ks[3], (DEPTH, DEC_BATCH, SB_HEADS, PAST_LEN, SB_DH), DN_BETA),
        "state_ret": nrm(ks[4], (DEPTH, DEC_BATCH, RET_HEADS, RET_DK, RET_DV), 0.1),
        "state_ffn_conv": nrm(ks[5], (DEPTH, DEC_BATCH, CONV_W - 1, 2 * D_FF), 1.0),
        "cache_mem_k": nrm(ks[6], (DEPTH, DEC_BATCH, MEM_HEADS, N_MEM, MEM_DH), 1.0),
        "cache_mem_v": nrm(ks[7], (DEPTH, DEC_BATCH, MEM_HEADS, N_MEM, MEM_DH), DN_BETA),
        "mem_prompt": nrm(ks[8], (BATCH, N_MEM, D_MODEL), 1.0),
        "w_in": nrm(ks[9], (DEPTH, D_MODEL, IN_WIDTH), D_MODEL ** -0.5) * col_scale,
        "w_o": nrm(ks[10], (DEPTH, MIX_WIDTH, D_MODEL), MIX_WIDTH ** -0.5 * DN_BETA),
        "ln1_g": 1.0 + nrm(ks[11], (DEPTH, D_MODEL), 0.01),
        "ln1_b": nrm(ks[12], (DEPTH, D_MODEL), 0.01),
        "w_q_mem": nrm(ks[13], (DEPTH, D_MODEL, D_MODEL), D_MODEL ** -0.5),
        "w_k_mem": nrm(ks[14], (DEPTH, D_MODEL, D_MODEL), D_MODEL ** -0.5),
        "w_v_mem": nrm(ks[15], (DEPTH, D_MODEL, D_MODEL), D_MODEL ** -0.5 * DN_BETA),
        "w_o_mem": nrm(ks[16], (DEPTH, D_MODEL, D_MODEL), D_MODEL ** -0.5 * DN_BETA),
        "ln2_g": 1.0 + nrm(ks[17], (DEPTH, D_MODEL), 0.01),
        "ln2_b": nrm(ks[18], (DEPTH, D_MODEL), 0.01),
        "w_up": nrm(ks[19], (DEPTH, D_MODEL, 2 * D_FF), D_MODEL ** -0.5),
        "conv_w": nrm(ks[20], (DEPTH, CONV_W, 2 * D_FF), CONV_W ** -0.5),
        "conv_b": nrm(ks[21], (DEPTH, 2 * D_FF), 0.01),
        "w_down": nrm(ks[22], (DEPTH, D_FF, D_MODEL), D_FF ** -0.5 * DN_BETA),
        "ln3_g": 1.0 + nrm(ks[23], (DEPTH, D_MODEL), 0.01),
        "ln3_b": nrm(ks[24], (DEPTH, D_MODEL), 0.01),
    }


def reference(x_prompt, x_sample, cache_sb_k, cache_sb_v, state_ret, state_ffn_conv,
              cache_mem_k, cache_mem_v, mem_prompt,
              w_in, w_o, ln1_g, ln1_b, w_q_mem, w_k_mem, w_v_mem, w_o_mem, ln2_g, ln2_b,
              w_up, conv_w, conv_b, w_down, ln3_g, ln3_b):
    xp, xs = x_prompt, x_sample
    bp = xp.shape[0]
    past_len = cache_sb_k.shape[3]
    p_k, p_v, p_s, p_c, p_mk, p_mv = [], [], [], [], [], []
    s_k, s_v, s_s, s_c = [], [], [], []
    for l in range(DEPTH):
        lw = (w_in[l], w_o[l], ln1_g[l], ln1_b[l], w_q_mem[l], w_o_mem[l], ln2_g[l], ln2_b[l],
              w_up[l], conv_w[l], conv_b[l], w_down[l], ln3_g[l], ln3_b[l])
        mk = split_heads(mem_prompt @ w_k_mem[l], MEM_HEADS)
        mv = split_heads(mem_prompt @ w_v_mem[l], MEM_HEADS)
        zero_s = jnp.zeros((bp, RET_HEADS, RET_DK, RET_DV), xp.dtype)
        zero_kv = jnp.zeros((bp, SB_HEADS, 0, SB_DH), xp.dtype)
        zero_c = jnp.zeros((bp, CONV_W - 1, 2 * D_FF), xp.dtype)
        xp, pk, pv, ps, pc = trunk_layer(xp, mk, mv, zero_s, zero_kv, zero_kv, zero_c, 0, *lw)
        p_k.append(pk); p_v.append(pv); p_s.append(ps); p_c.append(pc); p_mk.append(mk); p_mv.append(mv)
        xs, sk, sv, ss, sc = trunk_layer(xs, cache_mem_k[l], cache_mem_v[l], state_ret[l],
                                         cache_sb_k[l], cache_sb_v[l], state_ffn_conv[l], past_len, *lw)
        s_k.append(sk); s_v.append(sv); s_s.append(ss); s_c.append(sc)
    new_sb_k_prompt = jnp.stack(p_k)
    new_sb_v_prompt = jnp.stack(p_v)
    new_state_ret_prompt = jnp.stack(p_s)
    new_ffn_conv_prompt = jnp.stack(p_c)
    new_mem_k_prompt = jnp.stack(p_mk)
    new_mem_v_prompt = jnp.stack(p_mv)
    new_sb_k_sample = jnp.stack(s_k)
    new_sb_v_sample = jnp.stack(s_v)
    new_state_ret_sample = jnp.stack(s_s)
    new_ffn_conv_sample = jnp.stack(s_c)
    return (xp, xs, new_sb_k_prompt, new_sb_v_prompt, new_state_ret_prompt, new_ffn_conv_prompt,
            new_mem_k_prompt, new_mem_v_prompt, new_sb_k_sample, new_sb_v_sample,
            new_state_ret_sample, new_ffn_conv_sample)
```

```python
import contextlib
import math
import numpy as np
import concourse.bass as bass
import concourse.mybir as mybir
from concourse.bass_utils import run_bass_kernel_spmd

F32 = mybir.dt.float32
BF16 = mybir.dt.bfloat16
AF = mybir.ActivationFunctionType
ALU = mybir.AluOpType
AX = mybir.AxisListType

D = 1024
NH_TILES = 15
NO_TILES = 17
DFF = 2816
NCH = 22
ALPHA = (2.0) ** 0.25
LN_EPS = 1e-5
RMS_EPS = 1e-6
NEG = -30000.0


class Buf:
    __slots__ = ("name", "writers", "readers")

    def __init__(self, name):
        self.name = name
        self.writers = {}
        self.readers = {}


class Op:
    __slots__ = ("idx", "eng", "fn", "deps", "signal", "seq", "is_dma", "slot", "val", "prev_val")

    def __init__(self, idx, eng, fn, is_dma):
        self.idx = idx
        self.eng = eng
        self.fn = fn
        self.deps = set()
        self.signal = False
        self.seq = 0
        self.is_dma = is_dma
        self.slot = None
        self.val = 0
        self.prev_val = 0


COMPUTE = ("pe", "act", "dve", "pool")
NDMA_SLOTS = {"sp": 24, "pool": 16}


class Prog:
    def __init__(self):
        self.ops = []
        self.dma_count = {q: 0 for q in NDMA_SLOTS}
        self.dma_slot_val = {q: [0] * n for q, n in NDMA_SLOTS.items()}
        self.extra_reads = []

    def add(self, eng, fn, reads=(), writes=(), dma=False, extra=True):
        op = Op(len(self.ops), eng, fn, dma)
        key = ("dma", op.idx) if dma else eng
        reads = list(reads)
        if extra:
            reads += self.extra_reads
        writes = list(writes)
        for b in reads:
            for k, j in b.writers.items():
                op.deps.add(j)
        for b in writes:
            for k, j in b.readers.items():
                if not (k == eng and eng == "pe"):
                    op.deps.add(j)
            for k, j in b.writers.items():
                if not (k == eng and eng == "pe"):
                    op.deps.add(j)
        for b in writes:
            b.writers = {key: op.idx}
            b.readers = {}
        for b in reads:
            if b not in writes:
                b.readers[key] = op.idx
        op.deps.discard(op.idx)
        if dma:
            n = NDMA_SLOTS[eng]
            s = self.dma_count[eng] % n
            self.dma_count[eng] += 1
            op.slot = s
            op.prev_val = self.dma_slot_val[eng][s]
            op.val = op.prev_val + 16
            self.dma_slot_val[eng][s] = op.val
        self.ops.append(op)
        return op

    def pe(self, fn, reads=(), writes=()):
        return self.add("pe", fn, reads, writes)

    def act(self, fn, reads=(), writes=()):
        return self.add("act", fn, reads, writes)

    def dve(self, fn, reads=(), writes=()):
        return self.add("dve", fn, reads, writes)

    def chain(self, eng, fns, reads=(), writes=()):
        op = None
        for i, fn in enumerate(fns):
            op = self.add(eng, fn, list(reads) + (list(writes) if i > 0 else []), writes)
        return op

    def dma(self, q, fn, reads=(), writes=(), extra=True):
        return self.add(q, fn, reads, writes, dma=True, extra=extra)

    def emit(self, nc, stack):
        ops = self.ops
        for op in ops:
            for j in op.deps:
                if not ops[j].is_dma:
                    ops[j].signal = True
        cnt = {e: 0 for e in COMPUTE}
        for op in ops:
            if not op.is_dma and op.signal:
                cnt[op.eng] += 1
                op.seq = cnt[op.eng]
        sem = {e: stack.enter_context(nc.semaphore("s_" + e)) for e in COMPUTE}
        dsem = {q: [stack.enter_context(nc.semaphore("d_%s%d" % (q, i))) for i in range(n)]
                for q, n in NDMA_SLOTS.items()}
        block = stack.enter_context(nc.Block())
        engs = {"pe": block.tensor, "act": block.scalar, "dve": block.vector,
                "pool": block.gpsimd, "sp": block.sync}

        def make(ename):
            mine = [op for op in ops if op.eng == ename]

            def body(e):
                waited = {}

                def wait(s, v):
                    k = id(s)
                    if waited.get(k, 0) >= v:
                        return
                    waited[k] = v
                    e.wait_ge(s, v)

                for op in mine:
                    for j in sorted(op.deps):
                        d = ops[j]
                        if d.is_dma:
                            wait(dsem[d.eng][d.slot], d.val)
                        else:
                            wait(sem[d.eng], d.seq)
                    if op.is_dma:
                        if op.prev_val > 0:
                            wait(dsem[op.eng][op.slot], op.prev_val)
                        ins = op.fn(e)
                        ins.then_inc(dsem[op.eng][op.slot], 16)
                    else:
                        ins = op.fn(e)
                        if op.signal:
                            ins.then_inc(sem[op.eng], 1)
                if ename in NDMA_SLOTS:
                    for s, v in enumerate(self.dma_slot_val[ename]):
                        if v > 0:
                            wait(dsem[ename][s], v)
            return body

        for ename, deco in engs.items():
            deco(make(ename))


class Rot:
    def __init__(self, items):
        self.items = list(items)
        self.i = 0

    def next(self):
        x = self.items[self.i % len(self.items)]
        self.i += 1
        return x


DBG = {}


def build_program(stop_after=None):
    if stop_after and stop_after.startswith("nokv:"):
        DBG["nokv"] = True
        stop_after = stop_after[5:]
    if stop_after and stop_after.startswith("nodma:"):
        DBG["nodma"] = True
        stop_after = stop_after[6:]
    if stop_after and stop_after.startswith("sb:"):
        _, hh, nn = stop_after.split(":")
        DBG["heads"] = int(hh)
        DBG["nkb"] = int(nn)
        stop_after = "sb"
    nc = bass.Bass("TRN2", target_bir_lowering=False, dynamic_dma_scratch_size=8192)
    P = Prog()
    sb_used = [8192]
    st = contextlib.ExitStack()

    def din(name, shape):
        return nc.dram_tensor(name, list(shape), F32, kind="ExternalInput").ap()

    def dout(name, shape):
        return nc.dram_tensor(name, list(shape), F32, kind="ExternalOutput").ap()

    xh = din("xh", [NH_TILES * 128, D])
    xo = din("xo", [NO_TILES * 128, D])
    xs = din("xs", [2, 32, D])
    memp = din("memp", [256, D])
    ck = din("ck", [2, 8, 2048, 64])
    cv = din("cv", [2, 8, 2048, 64])
    sr = din("sr", [2, 4, 128, 128])
    scv = din("scv", [2, 2, 2 * DFF])
    cmk = din("cmk", [2, 4, 256, 256])
    cmv = din("cmv", [2, 4, 256, 256])
    w_in = din("w_in", [D, 3584])
    w_o = din("w_o", [D, D])
    w_q = din("w_q", [D, D])
    w_k = din("w_k", [D, D])
    w_v = din("w_v", [D, D])
    w_om = din("w_om", [D, D])
    w_up = din("w_up", [D, 2 * DFF])
    w_dn = din("w_dn", [DFF, D])
    lnp = din("lnp", [6, D])
    convp = din("convp", [4, 2 * DFF])
    flag = din("flag", [128, 1])
    ropeh = din("ropeh", [NH_TILES, 128, 256])
    ropeo = din("ropeo", [NO_TILES, 128, 256])
    ropes = din("ropes", [32, 256])
    cb16 = din("cb16", [128, 1536])
    cf32 = din("cf32", [128, 8])

    y_o = dout("y_o", [2048, D])
    y_s = dout("y_s", [2, 32, D])
    k_o = dout("k_o", [8, 2048, 64])
    v_o = dout("v_o", [8, 2048, 64])
    k_s = dout("k_s", [2, 8, 32, 64])
    v_s = dout("v_s", [2, 8, 32, 64])
    sret_o = dout("sret_o", [4, 128, 128])
    sret_s = dout("sret_s", [2, 4, 128, 128])
    conv_o = dout("conv_o", [2, 2 * DFF])
    conv_s = dout("conv_s", [2, 2, 2 * DFF])
    mk_o = dout("mk_o", [4, 256, 256])
    mv_o = dout("mv_o", [4, 256, 256])

    def T(name, shape, dt):
        n = 1
        for d_ in shape[1:]:
            n *= d_
        sb_used[0] += (n * (4 if dt == F32 else 2) + 31) // 32 * 32
        assert sb_used[0] <= 196608, ("SBUF over 192 KiB/partition", name, sb_used[0])
        return st.enter_context(nc.sbuf_tensor(name, list(shape), dt))

    KT = T("KT", [128, 4, 4096], BF16)
    VS = T("VS", [128, 32, 512], BF16)
    kt_b = [Buf("kt%d" % i) for i in range(32)]
    vs_b = [Buf("vs%d" % i) for i in range(32)]
    WS = [T("WS%d" % i, [128, 8, 512], BF16) for i in range(4)]
    ws_b = [Buf("ws%d" % i) for i in range(4)]
    ws_rot = Rot(range(4))
    X = T("X", [128, 5, D], F32)
    x_b = [Buf("x%d" % i) for i in range(5)]
    XT = T("XT", [128, 8, 640], BF16)
    xt_b = [Buf("xt%d" % i) for i in range(5)]
    XB = [T("XB%d" % i, [128, D], BF16) for i in range(2)]
    xb_b = [Buf("xb%d" % i) for i in range(2)]
    xb_rot = Rot(range(2))
    mkt_b = Buf("mkt")
    mv_b = Buf("mv")
    lng_b = Buf("lng")
    lnb_b = Buf("lnb")
    LN = {}
    MM = {}
    mko_b = Buf("mk_o_dram")
    ROPE = [T("ROPE%d" % i, [128, 256], F32) for i in range(2)]
    rope_b = [Buf("rope%d" % i) for i in range(2)]
    rope_rot = Rot(range(2))
    C16 = T("C16", [128, 1536], BF16)
    CF = T("CF", [128, 8], F32)
    c_b = Buf("consts")
    IDENT = C16[:, 0:128]
    NEGU = C16[:, 128:256]
    NEGO = C16[:, 256:384]
    NEGM = C16[:, 384:1024]
    M01 = C16[:, 1024:1536]
    GAM = [1.0 - 2.0 ** (-5.0 - h) for h in range(4)]
    GL128 = [g ** 128 for g in GAM]
    GL32 = [g ** 32 for g in GAM]
    KD = CF[:, 0:4]
    EPSC = CF[:, 4:8]
    CONVP = T("CONVP", [128, 4, 44], F32)
    FLAG = T("FLAG", [128, 1], F32)
    S = T("S", [128, 512], F32)
    S16 = T("S16", [128, 512], BF16)
    s_b = Buf("S")
    s16_b = Buf("S16")
    USTATE = T("USTATE", [128, 44, 2], F32)
    us_b = Buf("ustate")
    SMALL = T("SMALL", [128, 4 * 64 + 8], F32)
    ARENA_BYTES = 41 * 1024
    ARENA = T("ARENA", [128, ARENA_BYTES // 2], BF16)
    arena_b = Buf("arena")
    P.extra_reads = [arena_b]

    PS = st.enter_context(nc.psum_tensor("PS", [128, 8, 512], F32))
    PSB = PS.bitcast(BF16)
    ps_b = [Buf("ps%d" % i) for i in range(8)]

    class Arena:
        def __init__(self):
            self.off = 0

        def reset(self):
            self.off = 0

        def take(self, shape, dt):
            n = 1
            for s in shape[1:]:
                n *= s
            nb = n * (4 if dt == F32 else 2)
            nb = (nb + 63) // 64 * 64
            a = ARENA[:, self.off // 2:(self.off + nb) // 2]
            self.off += nb
            assert self.off <= ARENA_BYTES, ("arena overflow", self.off)
            if dt == F32:
                a = a.bitcast(F32)
                a = a[:, 0:n]
            else:
                a = a[:, 0:n]
            if len(shape) == 3:
                a = a.rearrange("p (a b) -> p a b", a=shape[1])
            return a

    AR = Arena()

    def fence():
        j = SMALL[:, 256:257]
        P.add("dve", lambda e: e.memset(j, 0.0), reads=[], writes=[arena_b], extra=False)

    WBF = {}
    wconv_b = {}
    WSPEC = {"w_k": (w_k, D, D), "w_v": (w_v, D, D), "w_in": (w_in, D, 3584), "w_o": (w_o, D, D),
             "w_q": (w_q, D, D), "w_om": (w_om, D, D), "w_up": (w_up, D, 2 * DFF), "w_dn": (w_dn, DFF, D)}
    for wname, (wap, rows, cols) in WSPEC.items():
        sc = nc.dram_tensor(wname + "_bf", [rows, cols], BF16, kind="Internal").ap()
        WBF[id(wap)] = (wname, sc)
        wconv_b[wname] = []

    conv_done = set()
    conv_pending = []
    conv_sched = {0: ["w_in", "w_o", "w_q", "w_om"], 1: ["w_up", "w_dn"]}
    hook_cnt = [0]

    def conv_hook():
        hook_cnt[0] += 1
        if hook_cnt[0] % 3 == 0:
            conv_step(1)

    def conv_pieces(names):
        for wname in names:
            wap, rows, cols = WSPEC[wname]
            sc = WBF[id(wap)][1]
            step = 128
            for r0 in range(0, rows, step):
                r1 = min(rows, r0 + step)
                b = Buf("conv_%s_%d" % (wname, r0))
                wconv_b[wname].append(b)

                def thunk(sc=sc, wap=wap, r0=r0, r1=r1, b=b):
                    P.dma("pool", lambda e: e.dma_start(out=sc[r0:r1, :], in_=wap[r0:r1, :]), writes=[b], extra=False)
                conv_pending.append(thunk)

    def conv_step(n=1):
        for _ in range(n):
            if conv_pending:
                conv_pending.pop(0)()

    def wload(src_view, nk=8, ncols=512, q="pool"):
        view, wname, fview = src_view
        i = ws_rot.next()
        dst = WS[i][:, 0:nk, 0:ncols]
        if wname not in conv_done:
            P.dma("pool", lambda e: e.dma_start(out=dst, in_=fview), writes=[ws_b[i]], extra=False)
        else:
            P.dma(q, lambda e: e.dma_start(out=dst, in_=view), reads=wconv_b[wname], writes=[ws_b[i]],
                  extra=False)
        return i

    def wview(w, cb, nk=8, k0=0, ncols=512):
        wname, sc = WBF[id(w)]
        return (sc.rearrange("(kc p) n -> p kc n", p=128)[:, k0:k0 + nk, cb:cb + ncols], wname,
                w.rearrange("(kc p) n -> p kc n", p=128)[:, k0:k0 + nk, cb:cb + ncols])

    def transposes(e, bank, src, ntok, nblk, src_step=128):
        ins = None
        for i in range(nblk):
            ins = e.transpose(PSB[:, bank, i * 128:i * 128 + ntok],
                              src[0:ntok, i * src_step:i * src_step + 128], IDENT[0:ntok, 0:ntok])
        return ins

    tp_rot = Rot([2, 3])
    pj_rot = Rot([0, 1])

    def to_xt(xb_i, t, ntok, col0):
        bk = tp_rot.next()
        src = XB[xb_i]
        P.pe(lambda e: transposes(e, bk, src, ntok, 8), reads=[xb_b[xb_i], c_b], writes=[ps_b[bk]])
        dst = XT[:, :, col0:col0 + ntok]
        srcp = PSB[:, bk, :].rearrange("p (a b) -> p a b", a=8)[:, :, 0:ntok]
        P.act(lambda e: e.activation(out=dst, in_=srcp, func=AF.Identity), reads=[ps_b[bk]], writes=[xt_b[t]])

    def load_ln(idx):
        LNG = AR.take([128, D], F32)
        LNB = AR.take([128, D], F32)
        LN["g"], LN["b"] = LNG, LNB
        g = lnp[idx:idx + 1, :].to_broadcast([128, D])
        b = lnp[idx + 1:idx + 2, :].to_broadcast([128, D])
        P.dma("sp", lambda e: e.dma_start(out=LNG, in_=g), writes=[lng_b])
        P.dma("sp", lambda e: e.dma_start(out=LNB, in_=b), writes=[lnb_b])

    def resid_add(t, ntok, cbk, bank):
        xs_ = X[0:ntok, t, cbk * 512:(cbk + 1) * 512]
        P.dve(lambda e: e.scalar_tensor_tensor(out=xs_, in0=xs_, scalar=ALPHA, in1=PS[0:ntok, bank, :],
                                               op0=ALU.mult, op1=ALU.add),
              reads=[ps_b[bank], x_b[t]], writes=[x_b[t]])

    sm_bufs = [Buf("small%d" % i) for i in range(4)]
    sm_rot = Rot(range(4))

    def small_next():
        r = sm_rot.next()
        return r * 64, sm_bufs[r]

    def layer_norm(t, ntok, want_xb=True):
        LNG, LNB = LN["g"], LN["b"]
        so, sm_b = small_next()
        xt_ = X[0:ntok, t, :]
        stats = SMALL[0:ntok, so + 0:so + 12]
        mv = SMALL[0:ntok, so + 12:so + 14]
        rstd = SMALL[0:ntok, so + 14:so + 15]

        def f_stats(e):
            e.bn_stats(out=SMALL[0:ntok, so + 0:so + 6], in_=X[0:ntok, t, 0:512])
            return e.bn_stats(out=SMALL[0:ntok, so + 6:so + 12], in_=X[0:ntok, t, 512:1024])
        P.chain("dve", [f_stats, lambda e: e.bn_aggr(out=mv, in_=stats)], reads=[x_b[t]], writes=[sm_b])
        P.chain("act", [lambda e: e.activation(out=rstd, in_=SMALL[0:ntok, so + 13:so + 14], func=AF.Ln,
                                               bias=EPSLN[0:ntok, :], scale=1.0),
                        lambda e: e.activation(out=rstd, in_=rstd, func=AF.Exp, scale=-0.5)],
                reads=[sm_b, c_b], writes=[sm_b])
        P.chain("dve", [lambda e: e.scalar_tensor_tensor(out=xt_, in0=xt_, scalar=SMALL[0:ntok, so + 12:so + 13],
                                                         in1=LNG[0:ntok, :], op0=ALU.subtract, op1=ALU.mult),
                        lambda e: e.scalar_tensor_tensor(out=xt_, in0=xt_, scalar=rstd, in1=LNB[0:ntok, :],
                                                         op0=ALU.mult, op1=ALU.add)],
                reads=[sm_b, x_b[t], lng_b, lnb_b], writes=[x_b[t]])
        if want_xb:
            i = xb_rot.next()
            P.dve(lambda e: e.tensor_copy(out=XB[i][0:ntok, :], in_=xt_), reads=[x_b[t]], writes=[xb_b[i]])
            return i
        return None

    EPSLN = T("EPSLN", [128, 1], F32)

    P.dma("pool", lambda e: e.dma_start(out=C16[:, :], in_=cb16), writes=[c_b], extra=False)
    P.dma("sp", lambda e: e.dma_start(out=CF[:, :], in_=cf32), writes=[c_b], extra=False)
    P.dma("sp", lambda e: e.dma_start(out=FLAG[:, :], in_=flag), writes=[c_b], extra=False)
    with nc.allow_non_contiguous_dma(reason="tiny per-partition conv parameter gather"):
        for r in range(4):
            P.dma("sp", lambda e, r=r: e.dma_start(out=CONVP[:, r, :],
                                                   in_=convp[r, :].rearrange("(c p) -> p c", p=128),
                                                   allow_slow_non_contiguous=True),
                  writes=[c_b], extra=False)
    P.dve(lambda e: e.memset(EPSLN[:, :], LN_EPS), writes=[c_b])
    P.dve(lambda e: e.memset(USTATE[:, :, :], 0.0), writes=[us_b])
    P.dve(lambda e: e.memset(S[:, :], 0.0), writes=[s_b])
    P.dve(lambda e: e.memset(S16[:, :], 0.0), writes=[s16_b])

    def build_mkt(mk_tm, mk_buf):
        for mc in range(2):
            bk = tp_rot.next()
            src = mk_tm[:, mc, :]
            P.pe(lambda e, src=src, bk=bk: transposes(e, bk, src, 128, 8), reads=[mk_buf, c_b], writes=[ps_b[bk]])
            dst = MM["MKT"][:, :, mc * 128:(mc + 1) * 128]
            srcp = PSB[:, bk, :].rearrange("p (a b) -> p a b", a=8)
            P.dve(lambda e, dst=dst, srcp=srcp: e.tensor_copy(out=dst, in_=srcp), reads=[ps_b[bk]], writes=[mkt_b])

    def mem_phase_prompt():
        fence()
        AR.reset()
        MEMB = AR.take([128, 2, D], BF16)
        MEMT = AR.take([128, 8, 256], BF16)
        STG = AR.take([128, D], F32)
        memb_b, memt_b, stg_b = Buf("memb"), Buf("memt"), Buf("stg")
        P.dma("pool", lambda e: e.dma_start(out=MEMB, in_=memp.rearrange("(c p) n -> p c n", p=128)),
              writes=[memb_b])
        for mc in range(2):
            bk = tp_rot.next()
            src = MEMB[:, mc, :]
            P.pe(lambda e, src=src, bk=bk: transposes(e, bk, src, 128, 8), reads=[memb_b, c_b], writes=[ps_b[bk]])
            dst = MEMT[:, :, mc * 128:(mc + 1) * 128]
            srcp = PSB[:, bk, :].rearrange("p (a b) -> p a b", a=8)
            P.dve(lambda e, dst=dst, srcp=srcp: e.tensor_copy(out=dst, in_=srcp), reads=[ps_b[bk]], writes=[memt_b])
        for wi, (w, outd) in enumerate(((w_k, mk_o), (w_v, mv_o))):
            for cbk in range(2):
                si = wload(wview(w, cbk * 512))
                for mc in range(2):
                    bk = pj_rot.next()

                    def f(e, si=si, mc=mc, bk=bk):
                        ins = None
                        for kc in range(8):
                            ins = e.matmul(PS[:, bk, :], MEMT[:, kc, mc * 128:(mc + 1) * 128], WS[si][:, kc, :],
                                           start=(kc == 0), stop=(kc == 7))
                        return ins
                    P.pe(f, reads=[memt_b, ws_b[si]], writes=[ps_b[bk]])
                    sg_ = STG[:, cbk * 512:(cbk + 1) * 512]
                    P.act(lambda e, sg_=sg_, bk=bk: e.activation(out=sg_, in_=PS[:, bk, :], func=AF.Identity),
                          reads=[ps_b[bk]], writes=[stg_b])
                    dsto = outd[cbk * 2:cbk * 2 + 2, mc * 128:(mc + 1) * 128, :].rearrange("h p d -> p h d")
                    srco = sg_.rearrange("p (h d) -> p h d", h=2)
                    P.dma("sp", lambda e, dsto=dsto, srco=srco: e.dma_start(out=dsto, in_=srco), reads=[stg_b],
                          writes=[mko_b])

    def mem_load(srck, srcv, MKTM, mktm_b, rd):
        MV = MM["MV"]
        for mc in range(2):
            P.dma("pool", lambda e, mc=mc: e.dma_start(
                out=MKTM[:, mc, :].rearrange("p (h d) -> p h d", h=4),
                in_=srck[:, mc * 128:(mc + 1) * 128, :].rearrange("h p d -> p h d")), reads=rd, writes=[mktm_b])
            P.dma("pool", lambda e, mc=mc: e.dma_start(
                out=MV[:, mc, :].rearrange("p (h d) -> p h d", h=4),
                in_=srcv[:, mc * 128:(mc + 1) * 128, :].rearrange("h p d -> p h d")), reads=rd, writes=[mv_b])
        build_mkt(MKTM, mktm_b)

    def proj_tok(si, t, ntok, col0):
        bk = pj_rot.next()

        def f(e):
            ins = None
            for kc in range(8):
                ins = e.matmul(PS[0:ntok, bk, :], XT[:, kc, col0:col0 + ntok], WS[si][:, kc, :],
                               start=(kc == 0), stop=(kc == 7))
            return ins
        P.pe(f, reads=[xt_b[t], ws_b[si]], writes=[ps_b[bk]])
        return bk

    def rope(bk, ntok, ri, dst_bf, dst_buf, T1, T2, t12_b, kscale):
        ps4 = PS[0:ntok, bk, :].rearrange("p (h c d) -> p h c d", h=4, c=2)
        c2 = ROPE[ri][0:ntok, 0:128].unsqueeze(1).to_broadcast([ntok, 4, 128])
        s_lo = ROPE[ri][0:ntok, 128:192].unsqueeze(1).to_broadcast([ntok, 4, 64])
        s_hi = ROPE[ri][0:ntok, 192:256].unsqueeze(1).to_broadcast([ntok, 4, 64])
        t1 = T1[0:ntok, :].rearrange("p (h d) -> p h d", h=4)
        t2 = T2[0:ntok, :].rearrange("p (h c d) -> p h c d", h=4, c=2)

        def f(e):
            e.tensor_tensor(out=t1, in0=PS[0:ntok, bk, :].rearrange("p (h d) -> p h d", h=4), in1=c2, op=ALU.mult)
            e.tensor_tensor(out=t2[:, :, 0, :], in0=ps4[:, :, 1, :], in1=s_lo, op=ALU.mult)
            return e.tensor_tensor(out=t2[:, :, 1, :], in0=ps4[:, :, 0, :], in1=s_hi, op=ALU.mult)
        P.dve(f, reads=[ps_b[bk], rope_b[ri]], writes=[t12_b])
        if not kscale:
            P.dve(lambda e: e.tensor_tensor(out=dst_bf, in0=T1[0:ntok, :], in1=T2[0:ntok, :], op=ALU.add),
                  reads=[t12_b], writes=[dst_buf])
        else:
            def g(e):
                ins = None
                for h in range(4):
                    ins = e.tensor_scalar(out=dst_bf[:, h * 128:(h + 1) * 128], in0=T1[0:ntok, h * 128:(h + 1) * 128],
                                          scalar1=KD[0:ntok, h:h + 1], scalar2=None, op0=ALU.mult)
                return ins
            P.chain("dve", [lambda e: e.tensor_tensor(out=T1[0:ntok, :], in0=T1[0:ntok, :], in1=T2[0:ntok, :],
                                                      op=ALU.add), g],
                    reads=[t12_b, c_b], writes=[dst_buf, t12_b])

    def ret_state_update(kbf, vbf, ntok, kv_bufs, GL):
        bk = 6

        def f(e):
            ins = None
            for h in range(4):
                ins = e.matmul(PS[:, bk, h * 128:(h + 1) * 128], kbf[0:ntok, h * 128:(h + 1) * 128],
                               vbf[0:ntok, h * 128:(h + 1) * 128], start=True, stop=True)
            return ins
        P.pe(f, reads=kv_bufs, writes=[ps_b[bk]])

        def g(e):
            ins = None
            for h in range(4):
                ins = e.tensor_scalar(out=S[:, h * 128:(h + 1) * 128], in0=S[:, h * 128:(h + 1) * 128],
                                      scalar1=float(GL[h]), scalar2=None, op0=ALU.mult)
            return ins
        P.chain("dve", [lambda e: e.tensor_tensor(out=S[:, :], in0=S[:, :], in1=PS[:, bk, :], op=ALU.add), g],
                reads=[ps_b[bk], s_b, c_b], writes=[s_b])
        P.dve(lambda e: e.tensor_copy(out=S16[:, :], in_=S[:, :]), reads=[s_b], writes=[s16_b])

    def history_phase():
        for g0 in range(0, NH_TILES, 5):
            tiles = list(range(g0, min(g0 + 5, NH_TILES)))
            fence()
            AR.reset()
            KBF = AR.take([128, 5, 512], BF16)
            VBF = AR.take([128, 5, 512], BF16)
            T1s = [AR.take([128, 512], F32) for _ in range(2)]
            T2s = [AR.take([128, 512], F32) for _ in range(2)]
            SKBs = [AR.take([128, 512], BF16) for _ in range(2)]
            kbf_b = [Buf("kbf%d" % i) for i in range(5)]
            vbf_b = [Buf("vbf%d" % i) for i in range(5)]
            t12s_b, skbs_b = [Buf("t12a"), Buf("t12b")], [Buf("skba"), Buf("skbb")]
            ropei = {}
            for li, t in enumerate(tiles):
                i = xb_rot.next()
                P.dma("pool", lambda e, i=i, t=t: e.dma_start(out=XB[i][:, :], in_=xh[t * 128:(t + 1) * 128, :]),
                      writes=[xb_b[i]], extra=False)
                to_xt(i, li, 128, li * 128)
            for cbk in (1, 2, 5, 6):
                si = wload(wview(w_in, cbk * 512))
                bks = {0: proj_tok(si, 0, 128, 0)}
                for li, t in enumerate(tiles):
                    if li + 1 < len(tiles):
                        bks[li + 1] = proj_tok(si, li + 1, 128, (li + 1) * 128)
                    bk = bks[li]
                    T1, T2, t12_b = T1s[li % 2], T2s[li % 2], t12s_b[li % 2]
                    SKB, skb_b = SKBs[li % 2], skbs_b[li % 2]
                    if cbk == 1:
                        ri = rope_rot.next()
                        P.dma("sp", lambda e, ri=ri, t=t: e.dma_start(out=ROPE[ri][:, :], in_=ropeh[t]),
                              writes=[rope_b[ri]], extra=False)
                        rope(bk, 128, ri, KBF[:, li, :], kbf_b[li], T1, T2, t12_b, True)
                    elif cbk == 2:
                        P.act(lambda e, li=li, bk=bk: e.activation(out=VBF[:, li, :], in_=PS[:, bk, :], func=AF.Identity),
                              reads=[ps_b[bk]], writes=[vbf_b[li]])
                        ret_state_update(KBF[:, li, :], VBF[:, li, :], 128, [kbf_b[li], vbf_b[li]], GL128)
                    elif cbk == 5:
                        P.dve(lambda e, bk=bk, SKB=SKB: e.tensor_copy(out=SKB[:, :], in_=PS[:, bk, :]),
                              reads=[ps_b[bk]], writes=[skb_b])
                        tb = tp_rot.next()
                        P.pe(lambda e, tb=tb, SKB=SKB: transposes(e, tb, SKB, 128, 4), reads=[skb_b, c_b],
                             writes=[ps_b[tb]])
                        dst = KT[:, :, t * 128:(t + 1) * 128]
                        srcp = PSB[:, tb, 0:512].rearrange("p (a b) -> p a b", a=4)
                        P.act(lambda e, dst=dst, srcp=srcp: e.activation(out=dst, in_=srcp, func=AF.Identity),
                              reads=[ps_b[tb]], writes=[kt_b[t]])
                    else:
                        P.act(lambda e, t=t, bk=bk: e.activation(out=VS[:, t, :], in_=PS[:, bk, :], func=AF.Identity),
                              reads=[ps_b[bk]], writes=[vs_b[t]])

    def sb_attention(SQT, sqt_bufs, qcol0, nq, keyblocks, SOT, sot_buf, tmp, hook=None):
        EXPZ, SP, AT, R = tmp["EXPZ"], tmp["SP"], tmp["AT"], tmp["R"]
        W = nq
        wide = (W == 512)
        gsz = 2 if wide else max(1, 512 // W)
        z_rot = Rot([2, 4, 6])
        o_rot = Rot([0, 1])
        kbl = list(reversed(keyblocks))
        if "nkb" in DBG:
            kbl = kbl[:DBG["nkb"]]
        groups_ = [kbl[i:i + gsz] for i in range(0, len(kbl), gsz)]
        items = []
        for h in range(DBG.get("heads", 8)):
            ob = o_rot.next()
            for ii, blks in enumerate(groups_):
                items.append({"h": h, "c": h // 2, "hp": h % 2, "ob": ob, "blks": blks, "first": ii == 0,
                              "last": ii == len(groups_) - 1, "xi": len(items) % 2})

        def zsl(zb, b, lo=0, hi=None):
            hi = W if hi is None else hi
            return PS[:, zb + b, lo:hi] if wide else PS[:, zb, b * W + lo:b * W + hi]

        def zall(zb, g):
            if wide:
                return PS[:, zb, :] if g == 1 else PS[:, zb:zb + g, :]
            return PS[:, zb, 0:g * W]

        def ball(buf, g):
            if wide and g > 1:
                return buf[:, 0:g * W].rearrange("p (g w) -> p g w", g=g)
            return buf[:, 0:g * W]

        def zbufs(zb, g):
            return [ps_b[zb + b] for b in range(g)] if wide else [ps_b[zb]]

        def stage1(it):
            blks, c, hp, xi = it["blks"], it["c"], it["hp"], it["xi"]
            g = len(blks)
            zb = z_rot.next()
            it["zb"] = zb
            qap = SQT[hp][:, c, qcol0:qcol0 + nq]

            def f_qk(e):
                ins = None
                for b, kb in enumerate(blks):
                    msk = kb["mask"]
                    st_ = True if wide else (b == 0)
                    ins = e.matmul(zsl(zb, b), kb["kt"](c, hp), qap, start=st_,
                                   stop=((wide or b == g - 1) and msk is None), skip_group_check=True)
                    if msk is not None:
                        ncol, moff = msk
                        ins = e.matmul(zsl(zb, b, 0, ncol), IDENT, NEGM[:, moff:moff + ncol],
                                       start=False, stop=(wide or b == g - 1), skip_group_check=True)
                return ins
            bufs = [x for kb in blks for x in kb["bufs"]]
            P.pe(f_qk, reads=bufs + sqt_bufs + [c_b], writes=zbufs(zb, g))
            P.chain("act", [lambda e: e.activation(out=ball(EXPZ[xi], g), in_=zall(zb, g),
                                                   func=AF.Exp, scale=0.125),
                            lambda e: e.activation(out=SP[xi][:, 0:g * W], in_=EXPZ[xi][:, 0:g * W], func=AF.Ln,
                                                   bias=ONEC[:, :], scale=1.0)],
                    reads=zbufs(zb, g) + [c_b], writes=[tmp["sp_b"][xi], tmp["ez_b"]])

        def stage2(it):
            blks, c, hp, xi, zb, ob = it["blks"], it["c"], it["hp"], it["xi"], it["zb"], it["ob"]
            g, first, last = len(blks), it["first"], it["last"]

            def f_cum(e):
                ins = None
                if wide:
                    for b in range(g):
                        ins = e.matmul(zsl(zb, b), NEGU8[:, :], SP[xi][:, b * W:(b + 1) * W], start=False,
                                       stop=False, skip_group_check=True)
                        if not first:
                            ins = e.matmul(zsl(zb, b), NEGO8[:, :], R[:, 0:W], start=False, stop=False,
                                           skip_group_check=True)
                        for j in range(b):
                            ins = e.matmul(zsl(zb, b), NEGO8[:, :], SP[xi][:, j * W:(j + 1) * W], start=False,
                                           stop=False, skip_group_check=True)
                    return ins
                nmm = 1 + (0 if first else 1) + (g - 1)
                k = 0
                ins = e.matmul(PS[:, zb, 0:g * W], NEGU8[:, :], SP[xi][:, 0:g * W], start=False, stop=(k == nmm - 1),
                               skip_group_check=True)
                k += 1
                if not first:
                    rb = R[:, 0:W] if g == 1 else R[:, 0:W].unsqueeze(1).to_broadcast([128, g, W])
                    ob_ = PS[:, zb, 0:g * W] if g == 1 else PS[:, zb, 0:g * W].rearrange("p (g w) -> p g w", g=g)
                    ins = e.matmul(ob_, NEGO8[:, :], rb, start=False, stop=(k == nmm - 1), skip_group_check=True)
                    k += 1
                for j in range(g - 1):
                    r = g - 1 - j
                    sb_ = SP[xi][:, j * W:(j + 1) * W]
                    o2 = PS[:, zb, (j + 1) * W:g * W]
                    if r > 1:
                        sb_ = sb_.unsqueeze(1).to_broadcast([128, r, W])
                        o2 = o2.rearrange("p (g w) -> p g w", g=r)
                    ins = e.matmul(o2, NEGO8[:, :], sb_, start=False, stop=(k == nmm - 1), skip_group_check=True)
                    k += 1
                return ins
            P.pe(f_cum, reads=[tmp["sp_b"][xi], tmp["r_b"], c_b], writes=zbufs(zb, g))
            P.act(lambda e: e.activation(out=ball(AT[xi], g), in_=zall(zb, g), func=AF.Exp, scale=0.125),
                  reads=zbufs(zb, g), writes=[tmp["at_b"][xi]])
            if not last:
                fr = []
                if wide:
                    if first:
                        if g == 1:
                            fr.append(lambda e: e.tensor_copy(out=R[:, 0:W], in_=SP[xi][:, 0:W]))
                        else:
                            fr.append(lambda e: e.tensor_tensor(out=R[:, 0:W], in0=SP[xi][:, 0:W],
                                                                in1=SP[xi][:, W:2 * W], op=ALU.add))
                    else:
                        for b in range(g):
                            fr.append(lambda e, b=b: e.tensor_tensor(out=R[:, 0:W], in0=R[:, 0:W],
                                                                     in1=SP[xi][:, b * W:(b + 1) * W], op=ALU.add))
                elif g == 1:
                    if first:
                        fr.append(lambda e: e.tensor_copy(out=R[:, 0:W], in_=SP[xi][:, 0:W]))
                    else:
                        fr.append(lambda e: e.tensor_tensor(out=R[:, 0:W], in0=R[:, 0:W], in1=SP[xi][:, 0:W],
                                                            op=ALU.add))
                else:
                    spv = SP[xi][:, 0:g * W].rearrange("p (g w) -> p w g", g=g)
                    RT = R[:, 256:512].bitcast(F32)
                    fr.append(lambda e: e.tensor_reduce(out=RT[:, 0:W], in_=spv, axis=AX.X, op=ALU.add))
                    if first:
                        fr.append(lambda e: e.tensor_copy(out=R[:, 0:W], in_=RT[:, 0:W]))
                    else:
                        fr.append(lambda e: e.tensor_tensor(out=R[:, 0:W], in0=R[:, 0:W], in1=RT[:, 0:W],
                                                            op=ALU.add))
                P.chain("dve", fr, reads=[tmp["sp_b"][xi]], writes=[tmp["r_b"]])

        def stage3(it):
            blks, c, hp, xi, ob = it["blks"], it["c"], it["hp"], it["xi"], it["ob"]
            g, first, last = len(blks), it["first"], it["last"]

            def f_av(e):
                ins = None
                for b, kb in enumerate(blks):
                    ins = e.matmul(PS[:, ob, 0:W], kb["v"](c), AT[xi][:, b * W:(b + 1) * W],
                                   start=(first and b == 0), stop=(last and b == g - 1))
                return ins
            bufs = [x for kb in blks for x in kb["bufs"]]
            P.pe(f_av, reads=bufs + [tmp["at_b"][xi]], writes=[ps_b[ob]])
            if last:
                dst = SOT[hp * 64:(hp + 1) * 64, c, qcol0:qcol0 + nq]
                P.dve(lambda e: e.tensor_copy(out=dst, in_=PS[hp * 64:(hp + 1) * 64, ob, 0:nq]),
                      reads=[ps_b[ob]], writes=[sot_buf])

        n_it = len(items)
        for k in range(-2, n_it):
            if hook is not None:
                hook()
            if 0 <= k + 2 < n_it:
                stage1(items[k + 2])
            if 0 <= k + 1 < n_it:
                stage2(items[k + 1])
            if 0 <= k:
                stage3(items[k])

    ONEC = T("ONEC", [128, 1], F32)
    P.dve(lambda e: e.memset(ONEC[:, :], 1.0), writes=[c_b])
    NEGU8 = T("NEGU8", [128, 128], BF16)
    NEGO8 = T("NEGO8", [128, 128], BF16)
    P.dve(lambda e: e.tensor_scalar(out=NEGU8[:, :], in0=NEGU, scalar1=8.0, scalar2=None, op0=ALU.mult),
          reads=[c_b], writes=[c_b])
    P.dve(lambda e: e.tensor_scalar(out=NEGO8[:, :], in0=NEGO, scalar1=8.0, scalar2=None, op0=ALU.mult),
          reads=[c_b], writes=[c_b])

    def prefetch_sample_v(part):
        for n in range(part * 8, part * 8 + 8):
            sq_, blk = n // 16, n % 16
            gb = sq_ * 16 + blk
            P.dma("pool", lambda e, sq_=sq_, blk=blk, gb=gb: e.dma_start(
                out=VS[:, gb, :].rearrange("p (h d) -> p h d", h=8),
                in_=cv[sq_, :, blk * 128:(blk + 1) * 128, :].rearrange("h p d -> p h d")),
                writes=[vs_b[gb]], extra=False)

    def main_group(gi, tiles):
        nt = len(tiles)
        ntot = sum(t["ntok"] for t in tiles)
        is_s = tiles[0]["kind"] == "s"
        GL = GL32 if is_s else GL128
        fence()
        AR.reset()
        MIXT = AR.take([128, 8, 640], BF16)
        mixt_b = [Buf("mixt%d" % i) for i in range(nt)]
        sot_b = Buf("sot")
        mark = AR.off
        VBF = AR.take([128, 5, 512], BF16)
        KBF = AR.take([128, 5, 512], BF16)
        QKT = AR.take([128, 8, 640], BF16)
        SG = AR.take([128, 5, 512], BF16)
        T1 = AR.take([128, 512], F32)
        T2 = AR.take([128, 512], F32)
        SCB = AR.take([128, 512], BF16)
        ROB = AR.take([128, 512], BF16)
        QBF = ROB
        vbf_b = [Buf("vbf%d" % i) for i in range(nt)]
        kbf_b = [Buf("kbf%d" % i) for i in range(nt)]
        qkt_b = [Buf("qkt%d" % i) for i in range(nt)]
        sg_b = [Buf("sg%d" % i) for i in range(nt)]
        t12_b, scb_b, rob_b = Buf("t12"), Buf("scb"), Buf("rob")
        qbf_b = rob_b
        ropei = []
        for li, t in enumerate(tiles):
            ntok = t["ntok"]
            nr = t["nreal"]
            i = xb_rot.next()
            if nr < ntok:
                P.dve(lambda e, li=li: e.memset(X[:, li, :], 0.0), writes=[x_b[li]])
                P.dve(lambda e, i=i: e.memset(XB[i][:, :], 0.0), writes=[xb_b[i]])
            P.dma("sp", lambda e, li=li, t=t, nr=nr: e.dma_start(out=X[0:nr, li, :], in_=t["xsrc"]),
                  writes=[x_b[li]], extra=False)
            P.dma("pool", lambda e, i=i, t=t, nr=nr: e.dma_start(out=XB[i][0:nr, :], in_=t["xsrc"]),
                  writes=[xb_b[i]], extra=False)
            to_xt(i, li, ntok, t["col0"])
        for cbk in (1, 2, 0, 3):
            si = wload(wview(w_in, cbk * 512))
            bks = {0: proj_tok(si, 0, tiles[0]["ntok"], tiles[0]["col0"])}
            for li, t in enumerate(tiles):
                ntok, col0 = t["ntok"], t["col0"]
                if li + 1 < nt:
                    bks[li + 1] = proj_tok(si, li + 1, tiles[li + 1]["ntok"], tiles[li + 1]["col0"])
                bk = bks[li]
                if cbk == 1:
                    ri = rope_rot.next()
                    P.dma("sp", lambda e, ri=ri, t=t, ntok=ntok: e.dma_start(out=ROPE[ri][0:t["nreal"], :], in_=t["rope"]),
                          writes=[rope_b[ri]], extra=False)
                    t["ri"] = ri
                    rope(bk, ntok, ri, KBF[0:ntok, li, :], kbf_b[li], T1, T2, t12_b, True)
                    tb = tp_rot.next()
                    P.pe(lambda e, tb=tb, li=li, ntok=ntok: transposes(e, tb, KBF[:, li, :], ntok, 4),
                         reads=[kbf_b[li], c_b], writes=[ps_b[tb]])
                    dst = QKT[:, 4:8, col0:col0 + ntok]
                    srcp = PSB[:, tb, 0:512].rearrange("p (a b) -> p a b", a=4)[:, :, 0:ntok]
                    P.act(lambda e, dst=dst, srcp=srcp: e.activation(out=dst, in_=srcp, func=AF.Identity),
                          reads=[ps_b[tb]], writes=[qkt_b[li]])
                elif cbk == 2:
                    P.act(lambda e, li=li, bk=bk, ntok=ntok: e.activation(out=VBF[0:ntok, li, :], in_=PS[0:ntok, bk, :],
                                                                         func=AF.Identity),
                          reads=[ps_b[bk]], writes=[vbf_b[li]])
                elif cbk == 0:
                    ri = rope_rot.next()
                    P.dma("sp", lambda e, ri=ri, t=t, ntok=ntok: e.dma_start(out=ROPE[ri][0:t["nreal"], :], in_=t["rope"]),
                          writes=[rope_b[ri]], extra=False)
                    rope(bk, ntok, ri, QBF[0:ntok, :], qbf_b, T1, T2, t12_b, False)
                    tb = tp_rot.next()
                    P.pe(lambda e, tb=tb, ntok=ntok: transposes(e, tb, QBF, ntok, 4),
                         reads=[qbf_b, c_b], writes=[ps_b[tb]])
                    dst = QKT[:, 0:4, col0:col0 + ntok]
                    srcp = PSB[:, tb, 0:512].rearrange("p (a b) -> p a b", a=4)[:, :, 0:ntok]
                    P.act(lambda e, dst=dst, srcp=srcp: e.activation(out=dst, in_=srcp, func=AF.Identity),
                          reads=[ps_b[tb]], writes=[qkt_b[li]])
                else:
                    P.act(lambda e, li=li, bk=bk, ntok=ntok: e.activation(out=SG[0:ntok, li, :], in_=PS[0:ntok, bk, :],
                                                                         func=AF.Silu),
                          reads=[ps_b[bk]], writes=[sg_b[li]])
        def _ret_tile(li, t):
            ntok, col0 = t["ntok"], t["col0"]
            so, sm_b = small_next()
            if is_s:
                s = t["seq"]
                P.dma("sp", lambda e, s=s: e.dma_start(out=S[:, :].rearrange("p (h d) -> p h d", h=4),
                                                       in_=sr[s].rearrange("h p d -> p h d")),
                      writes=[s_b], extra=False)
                P.dve(lambda e: e.tensor_copy(out=S16[:, :], in_=S[:, :]), reads=[s_b], writes=[s16_b])

            def f_sc(e, ntok=ntok, col0=col0):
                ins = None
                for h in range(4):
                    ins = e.matmul(PS[0:ntok, 4, h * 128:h * 128 + ntok], QKT[:, 4 + h, col0:col0 + ntok],
                                   QKT[:, h, col0:col0 + ntok], start=True, stop=True)
                return ins
            P.pe(f_sc, reads=[qkt_b[li]], writes=[ps_b[4]])

            def f_msk(e, ntok=ntok):
                a = PS[0:ntok, 4, :].rearrange("p (h t) -> p h t", h=4)[:, :, 0:ntok]
                m = M01[0:ntok, :].rearrange("p (h t) -> p h t", h=4)[:, :, 0:ntok]
                o = SCB[0:ntok, :].rearrange("p (h t) -> p h t", h=4)[:, :, 0:ntok]
                return e.tensor_tensor(out=o, in0=a, in1=m, op=ALU.mult)
            P.dve(f_msk, reads=[ps_b[4], c_b], writes=[scb_b])

            def f_o(e, ntok=ntok, col0=col0, li=li):
                ins = None
                for h in range(4):
                    e.matmul(PS[0:ntok, 5, h * 128:(h + 1) * 128], SCB[0:ntok, h * 128:h * 128 + ntok],
                             VBF[0:ntok, li, h * 128:(h + 1) * 128], start=True, stop=False)
                    ins = e.matmul(PS[0:ntok, 5, h * 128:(h + 1) * 128], QKT[:, h, col0:col0 + ntok],
                                   S16[:, h * 128:(h + 1) * 128], start=False, stop=True)
                return ins
            P.pe(f_o, reads=[scb_b, vbf_b[li], qkt_b[li], s16_b], writes=[ps_b[5]])
            ss = SMALL[0:ntok, so + 16:so + 20]
            rs = SMALL[0:ntok, so + 20:so + 24]

            def f_ss(e, ntok=ntok):
                ins = None
                for h in range(4):
                    ins = e.activation(out=T1[0:ntok, h * 128:(h + 1) * 128], in_=PS[0:ntok, 5, h * 128:(h + 1) * 128],
                                       func=AF.Square, accum_out=SMALL[0:ntok, so + 16 + h:so + 17 + h])
                return ins
            P.act(f_ss, reads=[ps_b[5]], writes=[sm_b, t12_b])
            P.dve(lambda e, ntok=ntok, ss=ss: e.scalar_tensor_tensor(out=ss, in0=ss, scalar=1.0 / 128.0,
                                                                    in1=EPSC[0:ntok, :], op0=ALU.mult, op1=ALU.add),
                  reads=[sm_b, c_b], writes=[sm_b])

            P.chain("act", [lambda e, ss=ss, rs=rs: e.activation(out=rs, in_=ss, func=AF.Ln),
                            lambda e, rs=rs: e.activation(out=rs, in_=rs, func=AF.Exp, scale=-0.5)],
                    reads=[sm_b], writes=[sm_b])

            def f_ro(e, ntok=ntok, li=li):
                ins = None
                for h in range(4):
                    ins = e.scalar_tensor_tensor(out=ROB[0:ntok, h * 128:(h + 1) * 128],
                                                 in0=PS[0:ntok, 5, h * 128:(h + 1) * 128],
                                                 scalar=SMALL[0:ntok, so + 20 + h:so + 21 + h],
                                                 in1=SG[0:ntok, li, h * 128:(h + 1) * 128],
                                                 op0=ALU.mult, op1=ALU.mult)
                return ins
            P.dve(f_ro, reads=[ps_b[5], sm_b, sg_b[li]], writes=[rob_b])
            tb = tp_rot.next()
            P.pe(lambda e, tb=tb, ntok=ntok: transposes(e, tb, ROB, ntok, 4), reads=[rob_b, c_b], writes=[ps_b[tb]])
            dst = MIXT[:, 0:4, col0:col0 + ntok]
            srcp = PSB[:, tb, 0:512].rearrange("p (a b) -> p a b", a=4)[:, :, 0:ntok]
            P.act(lambda e, dst=dst, srcp=srcp: e.activation(out=dst, in_=srcp, func=AF.Identity),
                  reads=[ps_b[tb]], writes=[mixt_b[li]])
            ret_state_update(KBF[:, li, :], VBF[:, li, :], ntok, [kbf_b[li], vbf_b[li]], GL)
            if is_s:
                s = t["seq"]
                P.dma("sp", lambda e, s=s: e.dma_start(out=sret_s[s].rearrange("h p d -> p h d"),
                                                       in_=S[:, :].rearrange("p (h d) -> p h d", h=4)), reads=[s_b])
            elif t.get("last"):
                P.dma("sp", lambda e: e.dma_start(out=sret_o.rearrange("h p d -> p h d"),
                                                  in_=S[:, :].rearrange("p (h d) -> p h d", h=4)), reads=[s_b])
        for li, t in enumerate(tiles):
            _ret_tile(li, t)
        if stop_after == "ret":
            return
        fence()
        AR.off = mark
        SQTE = AR.take([128, 4, 640], BF16)
        SQTO = AR.take([128, 4, 640], BF16)
        SQT = (SQTE, SQTO)
        SQB = AR.take([128, 512], BF16)
        SKB = AR.take([128, 512], BF16)
        STK = AR.take([128, 512], F32)
        STV = AR.take([128, 512], F32)
        zw = 512 if is_s else 1024
        if is_s:
            KTOWN = AR.take([128, 4, 256], BF16)
            VOWN = AR.take([128, 2, 512], BF16)
            KSTG = [AR.take([128, 2, 512], BF16) for _ in range(2)]
        EXPZ = [AR.take([128, zw], F32)] * 2
        SPt = [AR.take([128, zw], BF16) for _ in range(2)]
        ATt = [AR.take([128, zw], BF16) for _ in range(2)]
        Rt = AR.take([128, 512], BF16)
        tmp = {"EXPZ": EXPZ, "SP": SPt, "AT": ATt, "R": Rt,
               "ez_b": Buf("expz"), "sp_b": [Buf("sp0"), Buf("sp1")], "at_b": [Buf("at0"), Buf("at1")], "r_b": Buf("R")}
        sqt_b = [Buf("sqt%d" % i) for i in range(nt)]
        P.dve(lambda e: e.memset(SQTE[:, :, :], 0.0), writes=sqt_b)
        P.dve(lambda e: e.memset(SQTO[:, :, :], 0.0), writes=sqt_b)
        sqb_b, skb_b, stk_b, stv_b = Buf("sqb"), Buf("skb"), Buf("stk"), Buf("stv")
        ktown_b = [Buf("ktown0"), Buf("ktown1")]
        vown_b = [Buf("vown0"), Buf("vown1")]
        kstg_b = [Buf("kstg0"), Buf("kstg1")]
        for cbk in (6, 5, 4):
            si = wload(wview(w_in, cbk * 512))
            bks = {0: proj_tok(si, 0, tiles[0]["ntok"], tiles[0]["col0"])}
            for li, t in enumerate(tiles):
                ntok, col0 = t["ntok"], t["col0"]
                if li + 1 < nt:
                    bks[li + 1] = proj_tok(si, li + 1, tiles[li + 1]["ntok"], tiles[li + 1]["col0"])
                bk = bks[li]
                if cbk == 6:
                    if is_s:
                        s = t["seq"]
                        P.dve(lambda e, s=s, bk=bk, ntok=ntok: e.tensor_copy(out=VOWN[0:ntok, s, :], in_=PS[0:ntok, bk, :]),
                              reads=[ps_b[bk]], writes=[vown_b[s]])
                    else:
                        blk = t["blk"]
                        P.dve(lambda e, blk=blk, bk=bk: e.tensor_copy(out=VS[:, blk, :], in_=PS[:, bk, :]),
                              reads=[ps_b[bk]], writes=[vs_b[blk]])
                    if (t["otile"] is not None or is_s) and not DBG.get("nokv"):
                        P.dve(lambda e, bk=bk, ntok=ntok: e.tensor_copy(out=STV[0:ntok, :], in_=PS[0:ntok, bk, :]),
                              reads=[ps_b[bk]], writes=[stv_b])
                        for hh in range(8):
                            if is_s:
                                dsto = v_s[t["seq"], hh]
                            else:
                                ot = t["otile"]
                                dsto = v_o[hh, ot * 128:(ot + 1) * 128, :]
                            if DBG.get("nodma"):
                                continue
                            P.dma("sp", lambda e, dsto=dsto, nr=t["nreal"], hh=hh: e.dma_start(
                                out=dsto, in_=STV[0:nr, hh * 64:(hh + 1) * 64]), reads=[stv_b])
                elif cbk == 5:
                    P.dve(lambda e, bk=bk, ntok=ntok: e.tensor_copy(out=SKB[0:ntok, :], in_=PS[0:ntok, bk, :]),
                          reads=[ps_b[bk]], writes=[skb_b])
                    tb = tp_rot.next()
                    P.pe(lambda e, tb=tb, ntok=ntok: transposes(e, tb, SKB, ntok, 4), reads=[skb_b, c_b], writes=[ps_b[tb]])
                    srcp = PSB[:, tb, 0:512].rearrange("p (a b) -> p a b", a=4)[:, :, 0:ntok]
                    if is_s:
                        s = t["seq"]
                        dst = KTOWN[:, :, s * 128:(s + 1) * 128]
                        P.dve(lambda e, dst=dst, srcp=srcp: e.tensor_copy(out=dst, in_=srcp),
                              reads=[ps_b[tb]], writes=[ktown_b[s]])
                    else:
                        blk = t["blk"]
                        dst = KT[:, :, blk * 128:(blk + 1) * 128]
                        P.dve(lambda e, dst=dst, srcp=srcp: e.tensor_copy(out=dst, in_=srcp),
                              reads=[ps_b[tb]], writes=[kt_b[blk]])
                    if (t["otile"] is not None or is_s) and not DBG.get("nokv"):
                        P.dve(lambda e, bk=bk, ntok=ntok: e.tensor_copy(out=STK[0:ntok, :], in_=PS[0:ntok, bk, :]),
                              reads=[ps_b[bk]], writes=[stk_b])
                        for hh in range(8):
                            if is_s:
                                dsto = k_s[t["seq"], hh]
                            else:
                                ot = t["otile"]
                                dsto = k_o[hh, ot * 128:(ot + 1) * 128, :]
                            if DBG.get("nodma"):
                                continue
                            P.dma("sp", lambda e, dsto=dsto, nr=t["nreal"], hh=hh: e.dma_start(
                                out=dsto, in_=STK[0:nr, hh * 64:(hh + 1) * 64]), reads=[stk_b])
                else:
                    P.dve(lambda e, bk=bk, ntok=ntok: e.tensor_copy(out=SQB[0:ntok, :], in_=PS[0:ntok, bk, :]),
                          reads=[ps_b[bk]], writes=[sqb_b])
                    tb = tp_rot.next()
                    P.pe(lambda e, tb=tb, ntok=ntok: transposes(e, tb, SQB, ntok, 4), reads=[sqb_b, c_b], writes=[ps_b[tb]])
                    srcp = PSB[:, tb, 0:512].rearrange("p (a b) -> p a b", a=4)[:, :, 0:ntok]
                    def f_sq(e, srcp=srcp, col0=col0, ntok=ntok):
                        e.tensor_copy(out=SQTE[0:64, :, col0:col0 + ntok], in_=srcp[0:64])
                        return e.tensor_copy(out=SQTO[64:128, :, col0:col0 + ntok], in_=srcp[64:128])
                    P.dve(f_sq, reads=[ps_b[tb]], writes=[sqt_b[li]])
        SOT = MIXT[:, 4:8, :]
        if stop_after == "sbproj":
            return
        if not is_s:
            b0 = tiles[0]["blk"]
            if gi in conv_sched:
                conv_pieces(conv_sched[gi])

            def mk_kb(blk, mask):
                return {"nk": 128,
                        "kt": (lambda c, hp, blk=blk: KT[:, c, blk * 128:(blk + 1) * 128]),
                        "v": (lambda c, blk=blk: VS[:, blk, c * 128:(c + 1) * 128]),
                        "bufs": [kt_b[blk], vs_b[blk]], "mask": mask}
            for (qa, qb) in ([(0, 4), (4, 5)] if nt == 5 else [(0, nt)]):
                kbs = []
                for blk in range(0, b0 + qb):
                    j = blk - b0
                    if j >= qa:
                        jj = j - qa
                        mask = ((jj + 1) * 128, (4 - jj) * 128)
                    else:
                        mask = None
                    kbs.append(mk_kb(blk, mask))
                sb_attention(SQT, [sqt_b[i] for i in range(qa, qb)], qa * 128, (qb - qa) * 128, kbs, SOT, sot_b, tmp,
                             hook=(conv_hook if gi in conv_sched else None))
            if gi in conv_sched:
                conv_step(1000)
                conv_done.update(conv_sched[gi])
            if gi == 3:
                prefetch_sample_v(0)
        else:
            kt_rot = Rot([3, 5])

            def k_steps(s):
                steps = []
                for q8 in range(8):
                    ki = q8 % 2

                    def dma_step(s=s, q8=q8, ki=ki):
                        for bb in range(2):
                            P.dma("pool", lambda e, bb=bb: e.dma_start(
                                out=KSTG[ki][:, bb, :].rearrange("p (h d) -> p h d", h=8),
                                in_=ck[s, :, (q8 * 2 + bb) * 128:(q8 * 2 + bb + 1) * 128, :].rearrange("h p d -> p h d")),
                                writes=[kstg_b[ki]])

                    def tr_step(s=s, q8=q8, ki=ki):
                        for bb in range(2):
                            blk = s * 16 + q8 * 2 + bb
                            tb = kt_rot.next()
                            P.pe(lambda e, tb=tb, bb=bb: transposes(e, tb, KSTG[ki][:, bb, :], 128, 4),
                                 reads=[kstg_b[ki], c_b], writes=[ps_b[tb]])
                            dst = KT[:, :, blk * 128:(blk + 1) * 128]
                            srcp = PSB[:, tb, 0:512].rearrange("p (a b) -> p a b", a=4)
                            P.dve(lambda e, dst=dst, srcp=srcp: e.tensor_copy(out=dst, in_=srcp),
                                  reads=[ps_b[tb]], writes=[kt_b[blk]])
                    steps.append(dma_step)
                    if q8 >= 1:
                        steps.append(prev_tr)
                    prev_tr = tr_step
                steps.append(prev_tr)
                return steps

            def kbs_for(s):
                kbs = []
                for blk in range(16):
                    gb = s * 16 + blk
                    kbs.append({"nk": 128,
                                "kt": (lambda c, hp, gb=gb: KT[:, c, gb * 128:(gb + 1) * 128]),
                                "v": (lambda c, gb=gb: VS[:, gb, c * 128:(c + 1) * 128]),
                                "bufs": [kt_b[gb], vs_b[gb]], "mask": None})
                kbs.append({"nk": 128,
                            "kt": (lambda c, hp, s=s: KTOWN[:, c, s * 128:(s + 1) * 128]),
                            "v": (lambda c, s=s: VOWN[:, s, c * 128:(c + 1) * 128]),
                            "bufs": [ktown_b[s], vown_b[s]], "mask": (128, 4 * 128)})
                return kbs
            for st_ in k_steps(tiles[0]["seq"]):
                st_()
            pend = k_steps(tiles[1]["seq"])
            cnt_ = [0]

            def k_hook():
                cnt_[0] += 1
                if cnt_[0] % 2 == 0 and pend:
                    pend.pop(0)()
            sb_attention(SQT, [sqt_b[0]], tiles[0]["col0"], 128, kbs_for(tiles[0]["seq"]), SOT, sot_b, tmp, hook=k_hook)
            while pend:
                pend.pop(0)()
            sb_attention(SQT, [sqt_b[1]], tiles[1]["col0"], 128, kbs_for(tiles[1]["seq"]), SOT, sot_b, tmp)
        if stop_after == "sb":
            return
        fence()
        AR.off = mark
        load_ln(0)
        sis = [wload(wview(w_o, cbk * 512)) for cbk in range(2)]
        xbi = {}

        def _wo_mm(li, t):
            ntok, col0 = t["ntok"], t["col0"]
            for cbk in range(2):
                bk = pj_rot.next()

                def f(e, bk=bk, cbk=cbk, ntok=ntok, col0=col0):
                    ins = None
                    for kc in range(8):
                        ins = e.matmul(PS[0:ntok, bk, :], MIXT[:, kc, col0:col0 + ntok], WS[sis[cbk]][:, kc, :],
                                       start=(kc == 0), stop=(kc == 7))
                    return ins
                P.pe(f, reads=[mixt_b[li], sot_b, ws_b[sis[cbk]]], writes=[ps_b[bk]])
                resid_add(li, ntok, cbk, bk)
        _wo_mm(0, tiles[0])
        for li, t in enumerate(tiles):
            if li + 1 < nt:
                _wo_mm(li + 1, tiles[li + 1])
            xbi[li] = layer_norm(li, t["ntok"])
            to_xt(xbi[li], li, t["ntok"], t["col0"])
        if stop_after == "ln1":
            return
        fence()
        AR.reset()
        QMT = AR.take([128, 8, 640], BF16)
        OTs = [AR.take([128, 8, 128], BF16)] * 2
        PF = AR.take([128, D], F32)
        MKTM_S = PF.bitcast(BF16).rearrange("p (a b) -> p a b", a=2)
        PBFs = [AR.take([128, D], BF16) for _ in range(2)]
        PTs = [AR.take([128, 8, 128], BF16) for _ in range(2)]
        MM["MKT"] = AR.take([128, 8, 256], BF16)
        MM["MV"] = AR.take([128, 2, D], BF16)
        MKT, MV = MM["MKT"], MM["MV"]
        qmt_b = Buf("qmt")
        ots_b = [Buf("ot0")] * 2
        pf_b = Buf("pf")
        pbfs_b = [Buf("pbf0"), Buf("pbf1")]
        pts_b = [Buf("pt0"), Buf("pt1")]
        mktm_s_b = pf_b
        load_ln(2)
        if not is_s:
            mem_load(mk_o, mv_o, MKTM_S, mktm_s_b, [mko_b])
        if gi == 3:
            prefetch_sample_v(1)
        pieces = [(c0, min(512, ntot - c0)) for c0 in range(0, ntot, 512)]
        for cbk in range(2):
            si = wload(wview(w_q, cbk * 512))
            for fc in range(4):
                fch = cbk * 4 + fc
                pb = [0, 1] if fc % 2 == 0 else [2, 3]

                def f(e, si=si, fc=fc, pb=pb):
                    ins = None
                    for pi, (c0, n) in enumerate(pieces):
                        for kc in range(8):
                            ins = e.matmul(PS[:, pb[pi], 0:n], WS[si][:, kc, fc * 128:(fc + 1) * 128],
                                           XT[:, kc, c0:c0 + n], start=(kc == 0), stop=(kc == 7))
                    return ins
                P.pe(f, reads=[xt_b[i] for i in range(nt)] + [ws_b[si]], writes=[ps_b[pb[i]] for i in range(len(pieces))])

                def g(e, fch=fch, pb=pb):
                    ins = None
                    for pi, (c0, n) in enumerate(pieces):
                        ins = e.activation(out=QMT[:, fch, c0:c0 + n], in_=PS[:, pb[pi], 0:n], func=AF.Identity)
                    return ins
                P.act(g, reads=[ps_b[pb[i]] for i in range(len(pieces))], writes=[qmt_b])
        sis = [wload(wview(w_om, cbk * 512)) for cbk in range(2)]

        def _b_tile(li, t):
            ntok, col0 = t["ntok"], t["col0"]
            so, sm_b = small_next()
            if is_s:
                mem_load(cmk[t["seq"]], cmv[t["seq"]], MKTM_S, mktm_s_b, [])
            oi = li % 2
            OT, ot_bb = OTs[oi], ots_b[oi]
            PBF, pbf_b, PT, pt_b = PBFs[oi], pbfs_b[oi], PTs[oi], pts_b[oi]

            def f_s(e, ntok=ntok, col0=col0):
                ins = None
                for h in range(4):
                    for hf in range(2):
                        ins = e.matmul(PS[0:ntok, 4 + h // 2, (h % 2) * 256:(h % 2) * 256 + 256],
                                       QMT[:, 2 * h + hf, col0:col0 + ntok], MKT[:, 2 * h + hf, :],
                                       start=(hf == 0), stop=(hf == 1))
                return ins
            P.pe(f_s, reads=[qmt_b, mkt_b], writes=[ps_b[4], ps_b[5]])
            sc3 = PS[0:ntok, 4:6, :].rearrange("p a (b m) -> p (a b) m", b=2)
            mx = SMALL[0:ntok, so + 24:so + 28]
            nmx = SMALL[0:ntok, so + 28:so + 32]

            P.chain("dve", [lambda e, sc3=sc3, mx=mx: e.tensor_reduce(out=mx, in_=sc3, axis=AX.X, op=ALU.max),
                            lambda e, mx=mx, nmx=nmx: e.tensor_scalar(out=nmx, in0=mx, scalar1=-1.0 / 16.0,
                                                                      scalar2=None, op0=ALU.mult)],
                    reads=[ps_b[4], ps_b[5]], writes=[sm_b])

            def f_e(e, ntok=ntok):
                ins = None
                for h in range(4):
                    ins = e.activation(out=PF[0:ntok, h * 256:(h + 1) * 256],
                                       in_=PS[0:ntok, 4 + h // 2, (h % 2) * 256:(h % 2) * 256 + 256],
                                       func=AF.Exp, bias=SMALL[0:ntok, so + 28 + h:so + 29 + h], scale=1.0 / 16.0,
                                       accum_out=SMALL[0:ntok, so + 32 + h:so + 33 + h])
                return ins
            P.act(f_e, reads=[ps_b[4], ps_b[5], sm_b], writes=[pf_b, sm_b])

            def f_n(e, ntok=ntok):
                ins = None
                for h in range(4):
                    ins = e.tensor_scalar(out=PBF[0:ntok, h * 256:(h + 1) * 256], in0=PF[0:ntok, h * 256:(h + 1) * 256],
                                          scalar1=SMALL[0:ntok, so + 36 + h:so + 37 + h], scalar2=None, op0=ALU.mult)
                return ins
            P.chain("dve", [lambda e, ntok=ntok: e.reciprocal(out=SMALL[0:ntok, so + 36:so + 40], in_=SMALL[0:ntok, so + 32:so + 36]),
                            f_n], reads=[pf_b, sm_b], writes=[pbf_b, sm_b])
            yield
            tb = tp_rot.next()
            P.pe(lambda e, tb=tb, ntok=ntok: transposes(e, tb, PBF, ntok, 8), reads=[pbf_b, c_b], writes=[ps_b[tb]])
            srcp = PSB[:, tb, :].rearrange("p (a b) -> p a b", a=8)[:, :, 0:ntok]
            P.act(lambda e, srcp=srcp, ntok=ntok, PT=PT: e.activation(out=PT[:, :, 0:ntok], in_=srcp, func=AF.Identity),
                  reads=[ps_b[tb]], writes=[pt_b])

            def f_pv(e, ntok=ntok):
                ins = None
                for h in range(4):
                    for hf in range(2):
                        och = 2 * h + hf
                        for mc in range(2):
                            ins = e.matmul(PS[:, 6 + och // 4, (och % 4) * 128:(och % 4) * 128 + ntok],
                                           MV[:, mc, h * 256 + hf * 128:h * 256 + hf * 128 + 128],
                                           PT[:, 2 * h + mc, 0:ntok], start=(mc == 0), stop=(mc == 1))
                return ins
            P.pe(f_pv, reads=[pt_b, mv_b], writes=[ps_b[6], ps_b[7]])
            dst = OT[:, :, 0:ntok]
            srcp2 = PS[:, 6:8, :].rearrange("p a (b m) -> p (a b) m", b=4)[:, :, 0:ntok]
            P.act(lambda e, dst=dst, srcp2=srcp2: e.activation(out=dst, in_=srcp2, func=AF.Identity),
                  reads=[ps_b[6], ps_b[7]], writes=[ot_bb])
            for cbk in range(2):
                bk = pj_rot.next()

                def f(e, bk=bk, cbk=cbk, ntok=ntok, OT=OT):
                    ins = None
                    for kc in range(8):
                        ins = e.matmul(PS[0:ntok, bk, :], OT[:, kc, 0:ntok], WS[sis[cbk]][:, kc, :],
                                       start=(kc == 0), stop=(kc == 7))
                    return ins
                P.pe(f, reads=[ot_bb, ws_b[sis[cbk]]], writes=[ps_b[bk]])
                resid_add(li, ntok, cbk, bk)
            xbi[li] = layer_norm(li, ntok)
            to_xt(xbi[li], li, ntok, col0)
        gens = [_b_tile(li, t) for li, t in enumerate(tiles)]
        if is_s:
            for g_ in gens:
                for _ in g_:
                    pass
        else:
            next(gens[0])
            for li in range(nt):
                if li + 1 < nt:
                    next(gens[li + 1])
                for _ in gens[li]:
                    pass
        if stop_after == "ln2":
            return
        fence()
        AR.reset()
        ncw_c = 640 if ntot > 512 else 512
        ndb = 1 if ntot > 512 else 2
        HT = AR.take([128, NCH, ncw_c], BF16)
        UB = [AR.take([128, ncw_c + 8], F32) for _ in range(2 * ndb)]
        CB = [AR.take([128, ncw_c], F32) for _ in range(2 * ndb)]
        SA = AR.take([128, ncw_c], F32)
        ht_b = [Buf("ht%d" % i) for i in range(NCH)]
        ub_b = [Buf("ub%d" % i) for i in range(2 * ndb)]
        cb_b = [Buf("cb%d" % i) for i in range(2 * ndb)]
        sa_b = Buf("sa")
        if is_s:
            segs = [(t["col0"], t["nreal"], 2 + li * 36) for li, t in enumerate(tiles)]
        else:
            segs = [(0, ntot, 2)]
        pair_rot = Rot([(0, 1), (2, 3), (4, 5), (6, 7)])
        if gi == 3:
            prefetch_sample_v(2)
        for pg in range(6):
            nch = 4 if pg < 5 else 2
            sa_i = wload(wview(w_up, pg * 512, ncols=nch * 128), ncols=nch * 128, q="sp")
            sg_i = wload(wview(w_up, DFF + pg * 512, ncols=nch * 128), ncols=nch * 128, q="sp")
            for ci in range(nch):
                chunk_a = pg * 4 + ci
                for which, (si, chunk) in enumerate(((sa_i, chunk_a), (sg_i, NCH + chunk_a))):
                    pb = pair_rot.next()

                    def f(e, si=si, ci=ci, pb=pb):
                        ins = None
                        for pi, (c0, n) in enumerate(pieces):
                            for kc in range(8):
                                ins = e.matmul(PS[:, pb[pi], 0:n], WS[si][:, kc, ci * 128:(ci + 1) * 128],
                                               XT[:, kc, c0:c0 + n], start=(kc == 0), stop=(kc == 7))
                        return ins
                    pbufs = [ps_b[pb[i]] for i in range(len(pieces))]
                    P.pe(f, reads=[xt_b[i] for i in range(nt)] + [ws_b[si]], writes=pbufs)
                    bi_ = which + 2 * (chunk_a % ndb)
                    U, Cb = UB[bi_], CB[bi_]

                    def f_cp(e, pb=pb, U=U, Cb=Cb, chunk=chunk):
                        ins = None
                        for (c0, n, uo) in segs:
                            for pi, (p0, pn) in enumerate(pieces):
                                lo, hi = max(c0, p0), min(c0 + n, p0 + pn)
                                if lo >= hi:
                                    continue
                                e.activation(out=U[:, uo + lo - c0:uo + hi - c0], in_=PS[:, pb[pi], lo - p0:hi - p0],
                                             func=AF.Identity)
                                ins = e.activation(out=Cb[:, lo:hi], in_=PS[:, pb[pi], lo - p0:hi - p0],
                                                   func=AF.Identity, scale=CONVP[:, 2, chunk:chunk + 1],
                                                   bias=CONVP[:, 3, chunk:chunk + 1])
                        return ins
                    P.act(f_cp, reads=pbufs + [c_b], writes=[ub_b[bi_], cb_b[bi_]])

                    def f_cv0(e, U=U, chunk=chunk):
                        ins = e.tensor_copy(out=U[:, 0:2], in_=USTATE[:, chunk, :])
                        if gi == 0:
                            ins = e.tensor_scalar(out=U[:, 2 + 126:2 + 128], in0=U[:, 2 + 126:2 + 128],
                                                  scalar1=FLAG[:, 0:1], scalar2=None, op0=ALU.mult)
                        return ins

                    def f_cv1(e, U=U, Cb=Cb, chunk=chunk):
                        ins = None
                        for sgi, (c0, n, uo) in enumerate(segs):
                            ins = e.scalar_tensor_tensor(out=Cb[:, c0:c0 + n], in0=U[:, uo - 1:uo - 1 + n],
                                                         scalar=CONVP[:, 1, chunk:chunk + 1], in1=Cb[:, c0:c0 + n],
                                                         op0=ALU.mult, op1=ALU.add)
                        return ins

                    def f_cv2(e, U=U, Cb=Cb, chunk=chunk):
                        ins = None
                        for sgi, (c0, n, uo) in enumerate(segs):
                            e.scalar_tensor_tensor(out=Cb[:, c0:c0 + n], in0=U[:, uo - 2:uo - 2 + n],
                                                   scalar=CONVP[:, 0, chunk:chunk + 1], in1=Cb[:, c0:c0 + n],
                                                   op0=ALU.mult, op1=ALU.add)
                            if is_s:
                                ins = e.tensor_copy(out=USTS[:, sgi, chunk, :], in_=U[:, uo + n - 2:uo + n])
                            else:
                                ins = e.tensor_copy(out=USTATE[:, chunk, :], in_=U[:, uo + n - 2:uo + n])
                        return ins
                    if is_s:
                        def f_st(e, U=U, chunk=chunk):
                            ins = None
                            for sgi, (c0, n, uo) in enumerate(segs):
                                ins = e.tensor_copy(out=U[:, uo - 2:uo], in_=USTS_IN[:, sgi, chunk, :])
                            return ins
                        P.dve(f_st, reads=[usin_b], writes=[ub_b[bi_]])
                    P.chain("dve", ([] if is_s else [f_cv0]) + [f_cv1, f_cv2],
                            reads=[ub_b[bi_], cb_b[bi_], us_b, c_b], writes=[cb_b[bi_], us_b, ub_b[bi_]])
                ia_, ig_ = 2 * (chunk_a % ndb), 1 + 2 * (chunk_a % ndb)
                P.act(lambda e, ia_=ia_: e.activation(out=SA[:, 0:ntot], in_=CB[ia_][:, 0:ntot], func=AF.Silu),
                      reads=[cb_b[ia_]], writes=[sa_b])
                P.dve(lambda e, chunk_a=chunk_a, ig_=ig_: e.tensor_tensor(out=HT[:, chunk_a, 0:ntot], in0=SA[:, 0:ntot],
                                                                         in1=CB[ig_][:, 0:ntot], op=ALU.mult),
                      reads=[sa_b, cb_b[ig_]], writes=[ht_b[chunk_a]])
        with nc.allow_non_contiguous_dma(reason="feature-major conv state scatter"):
            if is_s:
                for sgi, t in enumerate(tiles):
                    for r in range(2):
                        P.dma("sp", lambda e, sgi=sgi, t=t, r=r: e.dma_start(
                            out=conv_s[t["seq"], r, :].rearrange("(c p) -> p c", p=128), in_=USTS[:, sgi, :, r],
                            allow_slow_non_contiguous=True),
                            reads=[us_b])
            elif tiles[-1].get("last"):
                for r in range(2):
                    P.dma("sp", lambda e, r=r: e.dma_start(out=conv_o[r, :].rearrange("(c p) -> p c", p=128),
                                                           in_=USTATE[:, :, r], allow_slow_non_contiguous=True),
                          reads=[us_b])
        if stop_after == "conv":
            return
        if gi == 3:
            prefetch_sample_v(3)
        for cbk in range(2):
            sl = [wload(wview(w_dn, cbk * 512, nk=8, k0=0), q="sp"), wload(wview(w_dn, cbk * 512, nk=8, k0=8), q="sp"),
                  wload(wview(w_dn, cbk * 512, nk=6, k0=16), nk=6, q="sp")]
            for li, t in enumerate(tiles):
                ntok, col0 = t["ntok"], t["col0"]
                bk = pj_rot.next()

                def f(e, bk=bk, ntok=ntok, col0=col0, sl=sl):
                    ins = None
                    for ch in range(NCH):
                        ins = e.matmul(PS[0:ntok, bk, :], HT[:, ch, col0:col0 + ntok], WS[sl[ch // 8]][:, ch % 8, :],
                                       start=(ch == 0), stop=(ch == NCH - 1))
                    return ins
                P.pe(f, reads=ht_b + [ws_b[i] for i in sl], writes=[ps_b[bk]])
                resid_add(li, ntok, cbk, bk)
        fence()
        AR.reset()
        load_ln(4)
        for li, t in enumerate(tiles):
            ntok = t["ntok"]
            layer_norm(li, ntok, want_xb=False)
            if is_s:
                dsto = y_s[t["seq"]]
            elif t["otile"] is not None:
                dsto = y_o[t["otile"] * 128:(t["otile"] + 1) * 128, :]
            else:
                continue
            P.dma("sp", lambda e, dsto=dsto, li=li, nr=t["nreal"]: e.dma_start(out=dsto, in_=X[0:nr, li, :]),
                  reads=[x_b[li]])

    USTS_IN = T("USTS_IN", [128, 2, 44, 2], F32)
    USTS = T("USTS", [128, 2, 44, 2], F32)
    usin_b = Buf("usts_in")
    with nc.allow_non_contiguous_dma(reason="feature-major conv state gather"):
        for s in range(2):
            for r in range(2):
                P.dma("sp", lambda e, s=s, r=r: e.dma_start(out=USTS_IN[:, s, :, r],
                                                            in_=scv[s, r, :].rearrange("(c p) -> p c", p=128),
                                                            allow_slow_non_contiguous=True),
                      writes=[usin_b], extra=False)

    mem_phase_prompt()
    if stop_after != "mem":
        history_phase()
        groups = [[0, 1, 2, 3, 4], [5, 6, 7, 8], [9, 10, 11, 12], [13, 14, 15, 16]]
        for gi, g in enumerate(groups):
            tiles = []
            for li, ti in enumerate(g):
                tiles.append({"ntok": 128, "nreal": 128, "col0": li * 128, "kind": "p", "blk": NH_TILES + ti,
                              "otile": (ti - 1) if ti >= 1 else None,
                              "xsrc": xo[ti * 128:(ti + 1) * 128, :], "rope": ropeo[ti],
                              "last": ti == 16})
            main_group(gi, tiles)
            if stop_after in ("g0",) or (stop_after is not None and stop_after not in ("prompt",)):
                break
        if stop_after is None:
            tiles = [{"ntok": 128, "nreal": 32, "col0": s * 128, "kind": "s", "seq": s, "otile": None,
                      "xsrc": xs[s], "rope": ropes} for s in range(2)]
            main_group(4, tiles)
    P.emit(nc, st)
    st.close()
    return nc


def _rope_table(pos):
    half = 64
    inv = (1.0 / (np.float32(10000.0) ** (np.arange(half, dtype=np.float32) / np.float32(half)))).astype(np.float32)
    ang = pos.astype(np.float32)[:, None] * inv[None, :]
    c = np.cos(ang).astype(np.float32)
    s = np.sin(ang).astype(np.float32)
    return np.concatenate([c, c, -s, s], axis=1).astype(np.float32)


def _consts():
    ident = np.eye(128, dtype=np.float32)
    j = np.arange(128)[:, None]
    s_ = np.arange(128)[None, :]
    negu = -(j >= s_).astype(np.float32)
    nego = -np.ones((128, 128), np.float32)
    diag = np.where(j < s_, 0.0, NEG).astype(np.float32)
    negm = np.concatenate([np.full((128, 512), NEG, np.float32), diag], axis=1)
    m_ = np.arange(128)
    m01h = (m_[None, :] >= m_[:, None]).astype(np.float32)
    cb16 = np.concatenate([ident, negu, nego, negm, m01h, m01h, m01h, m01h], axis=1)
    g = (1.0 - 2.0 ** (-5.0 - np.arange(4, dtype=np.float64)))
    m = np.arange(128)
    kd = np.zeros((128, 4), np.float32)
    epsc = np.zeros((128, 4), np.float32)
    for h in range(4):
        kd[:, h] = (128.0 ** -0.5) * g[h] ** (-(m + 1.0))
        epsc[:, h] = RMS_EPS / (g[h] ** (2.0 * (m + 1.0)))
    cf32 = np.concatenate([kd, epsc], axis=1).astype(np.float32)
    return cb16, cf32


_NC_CACHE = {}


def kernel(x_prompt, x_sample, cache_sb_k, cache_sb_v, state_ret, state_ffn_conv,
           cache_mem_k, cache_mem_v, mem_prompt,
           w_in, w_o, ln1_g, ln1_b, w_q_mem, w_k_mem, w_v_mem, w_o_mem, ln2_g, ln2_b,
           w_up, conv_w, conv_b, w_down, ln3_g, ln3_b, _stop_after=None):
    f = lambda a: np.ascontiguousarray(np.asarray(a, dtype=np.float32))
    x_prompt, x_sample = f(x_prompt), f(x_sample)
    cache_sb_k, cache_sb_v = f(cache_sb_k), f(cache_sb_v)
    state_ret, state_ffn_conv = f(state_ret), f(state_ffn_conv)
    cache_mem_k, cache_mem_v, mem_prompt = f(cache_mem_k), f(cache_mem_v), f(mem_prompt)
    cb16, cf32 = _consts()
    lnp = np.stack([f(ln1_g)[0], f(ln1_b)[0], f(ln2_g)[0], f(ln2_b)[0], f(ln3_g)[0], f(ln3_b)[0]])
    convp = np.concatenate([f(conv_w)[0], f(conv_b)], axis=0)
    shared = {"w_in": f(w_in)[0], "w_o": f(w_o)[0], "w_q": f(w_q_mem)[0], "w_k": f(w_k_mem)[0],
              "w_v": f(w_v_mem)[0], "w_om": f(w_o_mem)[0], "w_up": f(w_up)[0], "w_dn": f(w_down)[0],
              "lnp": np.ascontiguousarray(lnp), "convp": np.ascontiguousarray(convp),
              "cb16": cb16, "cf32": cf32, "ropes": _rope_table(2048 + np.arange(32))}
    in_maps = []
    for c in range(8):
        b, half = c // 2, c % 2
        m = dict(shared)
        if half == 0:
            m["xh"] = np.zeros((NH_TILES * 128, D), np.float32)
            m["xo"] = np.concatenate([np.zeros((128, D), np.float32), x_prompt[b, 0:2048]], axis=0)
            pos_o = np.concatenate([np.zeros(128), np.arange(2048)])
            pos_h = np.zeros(NH_TILES * 128)
            m["flag"] = np.zeros((128, 1), np.float32)
        else:
            m["xh"] = x_prompt[b, 0:NH_TILES * 128]
            m["xo"] = x_prompt[b, NH_TILES * 128:4096]
            pos_o = NH_TILES * 128 + np.arange(NO_TILES * 128)
            pos_h = np.arange(NH_TILES * 128)
            m["flag"] = np.ones((128, 1), np.float32)
        m["ropeo"] = _rope_table(pos_o).reshape(NO_TILES, 128, 256)
        m["ropeh"] = _rope_table(pos_h).reshape(NH_TILES, 128, 256)
        m["xs"] = x_sample[2 * c:2 * c + 2]
        m["memp"] = mem_prompt[b]
        m["ck"] = cache_sb_k[0, 2 * c:2 * c + 2]
        m["cv"] = cache_sb_v[0, 2 * c:2 * c + 2]
        m["sr"] = state_ret[0, 2 * c:2 * c + 2]
        m["scv"] = state_ffn_conv[0, 2 * c:2 * c + 2]
        m["cmk"] = cache_mem_k[0, 2 * c:2 * c + 2]
        m["cmv"] = cache_mem_v[0, 2 * c:2 * c + 2]
        in_maps.append({k: np.ascontiguousarray(v, dtype=np.float32) for k, v in m.items()})
    key = _stop_after
    if key not in _NC_CACHE:
        _NC_CACHE[key] = build_program(_stop_after)
    nc = _NC_CACHE[key]
    res = run_bass_kernel_spmd(nc, in_maps, core_ids=list(range(8)))
    R = res.results
    y_prompt = np.zeros((4, 4096, D), np.float32)
    nk = np.zeros((1, 4, 8, 4096, 64), np.float32)
    nv = np.zeros((1, 4, 8, 4096, 64), np.float32)
    for c in range(8):
        b, half = c // 2, c % 2
        y_prompt[b, half * 2048:(half + 1) * 2048] = R[c]["y_o"]
        nk[0, b, :, half * 2048:(half + 1) * 2048] = R[c]["k_o"]
        nv[0, b, :, half * 2048:(half + 1) * 2048] = R[c]["v_o"]
    y_sample = np.concatenate([R[c]["y_s"] for c in range(8)], axis=0)
    sret_p = np.stack([R[2 * b + 1]["sret_o"] for b in range(4)])[None]
    conv_p = np.stack([R[2 * b + 1]["conv_o"] for b in range(4)])[None]
    mk_p = np.stack([R[2 * b]["mk_o"] for b in range(4)])[None]
    mv_p = np.stack([R[2 * b]["mv_o"] for b in range(4)])[None]
    ks = np.concatenate([R[c]["k_s"] for c in range(8)], axis=0)[None]
    vs = np.concatenate([R[c]["v_s"] for c in range(8)], axis=0)[None]
    sret_s = np.concatenate([R[c]["sret_s"] for c in range(8)], axis=0)[None]
    conv_s = np.concatenate([R[c]["conv_s"] for c in range(8)], axis=0)[None]
    return (y_prompt, y_sample, nk, nv, sret_p, conv_p, mk_p, mv_p, ks, vs, sret_s, conv_s)
```

```python
import contextlib
import math
import numpy as np
import concourse.bass as bass
import concourse.mybir as mybir
from concourse.bass_utils import run_bass_kernel_spmd

F32 = mybir.dt.float32
BF16 = mybir.dt.bfloat16
AF = mybir.ActivationFunctionType
ALU = mybir.AluOpType
AX = mybir.AxisListType

D = 1024
NH_TILES = 15
NO_TILES = 17
DFF = 2816
NCH = 22
ALPHA = (2.0) ** 0.25
LN_EPS = 1e-5
RMS_EPS = 1e-6
NEG = -30000.0


class Buf:
    __slots__ = ("name", "writers", "readers")

    def __init__(self, name):
        self.name = name
        self.writers = {}
        self.readers = {}


class Op:
    __slots__ = ("idx", "eng", "fn", "deps", "signal", "seq", "is_dma", "slot", "val", "prev_val")

    def __init__(self, idx, eng, fn, is_dma):
        self.idx = idx
        self.eng = eng
        self.fn = fn
        self.deps = set()
        self.signal = False
        self.seq = 0
        self.is_dma = is_dma
        self.slot = None
        self.val = 0
        self.prev_val = 0


COMPUTE = ("pe", "act", "dve", "pool")
NDMA_SLOTS = {"sp": 24, "pool": 16}


class Prog:
    def __init__(self):
        self.ops = []
        self.dma_count = {q: 0 for q in NDMA_SLOTS}
        self.dma_slot_val = {q: [0] * n for q, n in NDMA_SLOTS.items()}
        self.extra_reads = []

    def add(self, eng, fn, reads=(), writes=(), dma=False, extra=True):
        op = Op(len(self.ops), eng, fn, dma)
        key = ("dma", op.idx) if dma else eng
        reads = list(reads)
        if extra:
            reads += self.extra_reads
        writes = list(writes)
        for b in reads:
            for k, j in b.writers.items():
                op.deps.add(j)
        for b in writes:
            for k, j in b.readers.items():
                if not (k == eng and eng == "pe"):
                    op.deps.add(j)
            for k, j in b.writers.items():
                if not (k == eng and eng == "pe"):
                    op.deps.add(j)
        for b in writes:
            b.writers = {key: op.idx}
            b.readers = {}
        for b in reads:
            if b not in writes:
                b.readers[key] = op.idx
        op.deps.discard(op.idx)
        if dma:
            n = NDMA_SLOTS[eng]
            s = self.dma_count[eng] % n
            self.dma_count[eng] += 1
            op.slot = s
            op.prev_val = self.dma_slot_val[eng][s]
            op.val = op.prev_val + 16
            self.dma_slot_val[eng][s] = op.val
        self.ops.append(op)
        return op

    def pe(self, fn, reads=(), writes=()):
        return self.add("pe", fn, reads, writes)

    def act(self, fn, reads=(), writes=()):
        return self.add("act", fn, reads, writes)

    def dve(self, fn, reads=(), writes=()):
        return self.add("dve", fn, reads, writes)

    def chain(self, eng, fns, reads=(), writes=()):
        op = None
        for i, fn in enumerate(fns):
            op = self.add(eng, fn, list(reads) + (list(writes) if i > 0 else []), writes)
        return op

    def dma(self, q, fn, reads=(), writes=(), extra=True):
        return self.add(q, fn, reads, writes, dma=True, extra=extra)

    def emit(self, nc, stack):
        ops = self.ops
        for op in ops:
            for j in op.deps:
                if not ops[j].is_dma:
                    ops[j].signal = True
        cnt = {e: 0 for e in COMPUTE}
        for op in ops:
            if not op.is_dma and op.signal:
                cnt[op.eng] += 1
                op.seq = cnt[op.eng]
        sem = {e: stack.enter_context(nc.semaphore("s_" + e)) for e in COMPUTE}
        dsem = {q: [stack.enter_context(nc.semaphore("d_%s%d" % (q, i))) for i in range(n)]
                for q, n in NDMA_SLOTS.items()}
        block = stack.enter_context(nc.Block())
        engs = {"pe": block.tensor, "act": block.scalar, "dve": block.vector,
                "pool": block.gpsimd, "sp": block.sync}

        def make(ename):
            mine = [op for op in ops if op.eng == ename]

            def body(e):
                waited = {}

                def wait(s, v):
                    k = id(s)
                    if waited.get(k, 0) >= v:
                        return
                    waited[k] = v
                    e.wait_ge(s, v)

                for op in mine:
                    for j in sorted(op.deps):
                        d = ops[j]
                        if d.is_dma:
                            wait(dsem[d.eng][d.slot], d.val)
                        else:
                            wait(sem[d.eng], d.seq)
                    if op.is_dma:
                        if op.prev_val > 0:
                            wait(dsem[op.eng][op.slot], op.prev_val)
                        ins = op.fn(e)
                        ins.then_inc(dsem[op.eng][op.slot], 16)
                    else:
                        ins = op.fn(e)
                        if op.signal:
                            ins.then_inc(sem[op.eng], 1)
                if ename in NDMA_SLOTS:
                    for s, v in enumerate(self.dma_slot_val[ename]):
                        if v > 0:
                            wait(dsem[ename][s], v)
            return body

        for ename, deco in engs.items():
            deco(make(ename))


class Rot:
    def __init__(self, items):
        self.items = list(items)
        self.i = 0

    def next(self):
        x = self.items[self.i % len(self.items)]
        self.i += 1
        return x


DBG = {}


def build_program(stop_after=None):
    if stop_after and stop_after.startswith("nokv:"):
        DBG["nokv"] = True
        stop_after = stop_after[5:]
    if stop_after and stop_after.startswith("nodma:"):
        DBG["nodma"] = True
        stop_after = stop_after[6:]
    if stop_after and stop_after.startswith("sb:"):
        _, hh, nn = stop_after.split(":")
        DBG["heads"] = int(hh)
        DBG["nkb"] = int(nn)
        stop_after = "sb"
    nc = bass.Bass("TRN2", target_bir_lowering=False, dynamic_dma_scratch_size=8192)
    P = Prog()
    sb_used = [8192]
    st = contextlib.ExitStack()

    def din(name, shape):
        return nc.dram_tensor(name, list(shape), F32, kind="ExternalInput").ap()

    def dout(name, shape):
        return nc.dram_tensor(name, list(shape), F32, kind="ExternalOutput").ap()

    xh = din("xh", [NH_TILES * 128, D])
    xo = din("xo", [NO_TILES * 128, D])
    xs = din("xs", [2, 32, D])
    memp = din("memp", [256, D])
    ck = din("ck", [2, 8, 2048, 64])
    cv = din("cv", [2, 8, 2048, 64])
    sr = din("sr", [2, 4, 128, 128])
    scv = din("scv", [2, 2, 2 * DFF])
    cmk = din("cmk", [2, 4, 256, 256])
    cmv = din("cmv", [2, 4, 256, 256])
    w_in = din("w_in", [D, 3584])
    w_o = din("w_o", [D, D])
    w_q = din("w_q", [D, D])
    w_k = din("w_k", [D, D])
    w_v = din("w_v", [D, D])
    w_om = din("w_om", [D, D])
    w_up = din("w_up", [D, 2 * DFF])
    w_dn = din("w_dn", [DFF, D])
    lnp = din("lnp", [6, D])
    convp = din("convp", [4, 2 * DFF])
    flag = din("flag", [128, 1])
    ropeh = din("ropeh", [NH_TILES, 128, 256])
    ropeo = din("ropeo", [NO_TILES, 128, 256])
    ropes = din("ropes", [32, 256])
    cb16 = din("cb16", [128, 1536])
    cf32 = din("cf32", [128, 8])

    y_o = dout("y_o", [2048, D])
    y_s = dout("y_s", [2, 32, D])
    k_o = dout("k_o", [8, 2048, 64])
    v_o = dout("v_o", [8, 2048, 64])
    k_s = dout("k_s", [2, 8, 32, 64])
    v_s = dout("v_s", [2, 8, 32, 64])
    sret_o = dout("sret_o", [4, 128, 128])
    sret_s = dout("sret_s", [2, 4, 128, 128])
    conv_o = dout("conv_o", [2, 2 * DFF])
    conv_s = dout("conv_s", [2, 2, 2 * DFF])
    mk_o = dout("mk_o", [4, 256, 256])
    mv_o = dout("mv_o", [4, 256, 256])

    def T(name, shape, dt):
        n = 1
        for d_ in shape[1:]:
            n *= d_
        sb_used[0] += (n * (4 if dt == F32 else 2) + 31) // 32 * 32
        assert sb_used[0] <= 196608, ("SBUF over 192 KiB/partition", name, sb_used[0])
        return st.enter_context(nc.sbuf_tensor(name, list(shape), dt))

    KT = T("KT", [128, 4, 4096], BF16)
    VS = T("VS", [128, 32, 512], BF16)
    kt_b = [Buf("kt%d" % i) for i in range(32)]
    vs_b = [Buf("vs%d" % i) for i in range(32)]
    WS = [T("WS%d" % i, [128, 8, 512], BF16) for i in range(4)]
    ws_b = [Buf("ws%d" % i) for i in range(4)]
    ws_rot = Rot(range(4))
    X = T("X", [128, 5, D], F32)
    x_b = [Buf("x%d" % i) for i in range(5)]
    XT = T("XT", [128, 8, 640], BF16)
    xt_b = [Buf("xt%d" % i) for i in range(5)]
    XB = [T("XB%d" % i, [128, D], BF16) for i in range(2)]
    xb_b = [Buf("xb%d" % i) for i in range(2)]
    xb_rot = Rot(range(2))
    mkt_b = Buf("mkt")
    mv_b = Buf("mv")
    lng_b = Buf("lng")
    lnb_b = Buf("lnb")
    LN = {}
    MM = {}
    mko_b = Buf("mk_o_dram")
    ROPE = [T("ROPE%d" % i, [128, 256], F32) for i in range(2)]
    rope_b = [Buf("rope%d" % i) for i in range(2)]
    rope_rot = Rot(range(2))
    C16 = T("C16", [128, 1536], BF16)
    CF = T("CF", [128, 8], F32)
    c_b = Buf("consts")
    IDENT = C16[:, 0:128]
    NEGU = C16[:, 128:256]
    NEGO = C16[:, 256:384]
    NEGM = C16[:, 384:1024]
    M01 = C16[:, 1024:1536]
    GAM = [1.0 - 2.0 ** (-5.0 - h) for h in range(4)]
    GL128 = [g ** 128 for g in GAM]
    GL32 = [g ** 32 for g in GAM]
    KD = CF[:, 0:4]
    EPSC = CF[:, 4:8]
    CONVP = T("CONVP", [128, 4, 44], F32)
    FLAG = T("FLAG", [128, 1], F32)
    S = T("S", [128, 512], F32)
    S16 = T("S16", [128, 512], BF16)
    s_b = Buf("S")
    s16_b = Buf("S16")
    USTATE = T("USTATE", [128, 44, 2], F32)
    us_b = Buf("ustate")
    SMALL = T("SMALL", [128, 4 * 64 + 8], F32)
    ARENA_BYTES = 41 * 1024
    ARENA = T("ARENA", [128, ARENA_BYTES // 2], BF16)
    arena_b = Buf("arena")
    P.extra_reads = [arena_b]

    PS = st.enter_context(nc.psum_tensor("PS", [128, 8, 512], F32))
    PSB = PS.bitcast(BF16)
    ps_b = [Buf("ps%d" % i) for i in range(8)]

    class Arena:
        def __init__(self):
            self.off = 0

        def reset(self):
            self.off = 0

        def take(self, shape, dt):
            n = 1
            for s in shape[1:]:
                n *= s
            nb = n * (4 if dt == F32 else 2)
            nb = (nb + 63) // 64 * 64
            a = ARENA[:, self.off // 2:(self.off + nb) // 2]
            self.off += nb
            assert self.off <= ARENA_BYTES, ("arena overflow", self.off)
            if dt == F32:
                a = a.bitcast(F32)
                a = a[:, 0:n]
            else:
                a = a[:, 0:n]
            if len(shape) == 3:
                a = a.rearrange("p (a b) -> p a b", a=shape[1])
            return a

    AR = Arena()

    def fence():
        j = SMALL[:, 256:257]
        P.add("dve", lambda e: e.memset(j, 0.0), reads=[], writes=[arena_b], extra=False)

    WBF = {}
    wconv_b = {}
    WSPEC = {"w_k": (w_k, D, D), "w_v": (w_v, D, D), "w_in": (w_in, D, 3584), "w_o": (w_o, D, D),
             "w_q": (w_q, D, D), "w_om": (w_om, D, D), "w_up": (w_up, D, 2 * DFF), "w_dn": (w_dn, DFF, D)}
    for wname, (wap, rows, cols) in WSPEC.items():
        sc = nc.dram_tensor(wname + "_bf", [rows, cols], BF16, kind="Internal").ap()
        WBF[id(wap)] = (wname, sc)
        wconv_b[wname] = []

    conv_done = set()
    conv_pending = []
    conv_sched = {0: ["w_in", "w_o", "w_q", "w_om"], 1: ["w_up", "w_dn"]}
    hook_cnt = [0]

    def conv_hook():
        hook_cnt[0] += 1
        if hook_cnt[0] % 3 == 0:
            conv_step(1)

    def conv_pieces(names):
        for wname in names:
            wap, rows, cols = WSPEC[wname]
            sc = WBF[id(wap)][1]
            step = 128
            for r0 in range(0, rows, step):
                r1 = min(rows, r0 + step)
                b = Buf("conv_%s_%d" % (wname, r0))
                wconv_b[wname].append(b)

                def thunk(sc=sc, wap=wap, r0=r0, r1=r1, b=b):
                    P.dma("pool", lambda e: e.dma_start(out=sc[r0:r1, :], in_=wap[r0:r1, :]), writes=[b], extra=False)
                conv_pending.append(thunk)

    def conv_step(n=1):
        for _ in range(n):
            if conv_pending:
                conv_pending.pop(0)()

    def wload(src_view, nk=8, ncols=512):
        view, wname, fview = src_view
        i = ws_rot.next()
        dst = WS[i][:, 0:nk, 0:ncols]
        if wname not in conv_done:
            P.dma("pool", lambda e: e.dma_start(out=dst, in_=fview), writes=[ws_b[i]], extra=False)
        else:
            P.dma("pool", lambda e: e.dma_start(out=dst, in_=view), reads=wconv_b[wname], writes=[ws_b[i]],
                  extra=False)
        return i

    def wview(w, cb, nk=8, k0=0, ncols=512):
        wname, sc = WBF[id(w)]
        return (sc.rearrange("(kc p) n -> p kc n", p=128)[:, k0:k0 + nk, cb:cb + ncols], wname,
                w.rearrange("(kc p) n -> p kc n", p=128)[:, k0:k0 + nk, cb:cb + ncols])

    def transposes(e, bank, src, ntok, nblk, src_step=128):
        ins = None
        for i in range(nblk):
            ins = e.transpose(PSB[:, bank, i * 128:i * 128 + ntok],
                              src[0:ntok, i * src_step:i * src_step + 128], IDENT[0:ntok, 0:ntok])
        return ins

    tp_rot = Rot([2, 3])
    pj_rot = Rot([0, 1])

    def to_xt(xb_i, t, ntok, col0):
        bk = tp_rot.next()
        src = XB[xb_i]
        P.pe(lambda e: transposes(e, bk, src, ntok, 8), reads=[xb_b[xb_i], c_b], writes=[ps_b[bk]])
        dst = XT[:, :, col0:col0 + ntok]
        srcp = PSB[:, bk, :].rearrange("p (a b) -> p a b", a=8)[:, :, 0:ntok]
        P.act(lambda e: e.activation(out=dst, in_=srcp, func=AF.Identity), reads=[ps_b[bk]], writes=[xt_b[t]])

    def load_ln(idx):
        LNG = AR.take([128, D], F32)
        LNB = AR.take([128, D], F32)
        LN["g"], LN["b"] = LNG, LNB
        g = lnp[idx:idx + 1, :].to_broadcast([128, D])
        b = lnp[idx + 1:idx + 2, :].to_broadcast([128, D])
        P.dma("sp", lambda e: e.dma_start(out=LNG, in_=g), writes=[lng_b])
        P.dma("sp", lambda e: e.dma_start(out=LNB, in_=b), writes=[lnb_b])

    def resid_add(t, ntok, cbk, bank):
        xs_ = X[0:ntok, t, cbk * 512:(cbk + 1) * 512]
        P.dve(lambda e: e.scalar_tensor_tensor(out=xs_, in0=xs_, scalar=ALPHA, in1=PS[0:ntok, bank, :],
                                               op0=ALU.mult, op1=ALU.add),
              reads=[ps_b[bank], x_b[t]], writes=[x_b[t]])

    sm_bufs = [Buf("small%d" % i) for i in range(4)]
    sm_rot = Rot(range(4))

    def small_next():
        r = sm_rot.next()
        return r * 64, sm_bufs[r]

    def layer_norm(t, ntok, want_xb=True):
        LNG, LNB = LN["g"], LN["b"]
        so, sm_b = small_next()
        xt_ = X[0:ntok, t, :]
        stats = SMALL[0:ntok, so + 0:so + 12]
        mv = SMALL[0:ntok, so + 12:so + 14]
        rstd = SMALL[0:ntok, so + 14:so + 15]

        def f_stats(e):
            e.bn_stats(out=SMALL[0:ntok, so + 0:so + 6], in_=X[0:ntok, t, 0:512])
            return e.bn_stats(out=SMALL[0:ntok, so + 6:so + 12], in_=X[0:ntok, t, 512:1024])
        P.chain("dve", [f_stats, lambda e: e.bn_aggr(out=mv, in_=stats)], reads=[x_b[t]], writes=[sm_b])
        P.chain("act", [lambda e: e.activation(out=rstd, in_=SMALL[0:ntok, so + 13:so + 14], func=AF.Ln,
                                               bias=EPSLN[0:ntok, :], scale=1.0),
                        lambda e: e.activation(out=rstd, in_=rstd, func=AF.Exp, scale=-0.5)],
                reads=[sm_b, c_b], writes=[sm_b])
        P.chain("dve", [lambda e: e.scalar_tensor_tensor(out=xt_, in0=xt_, scalar=SMALL[0:ntok, so + 12:so + 13],
                                                         in1=LNG[0:ntok, :], op0=ALU.subtract, op1=ALU.mult),
                        lambda e: e.scalar_tensor_tensor(out=xt_, in0=xt_, scalar=rstd, in1=LNB[0:ntok, :],
                                                         op0=ALU.mult, op1=ALU.add)],
                reads=[sm_b, x_b[t], lng_b, lnb_b], writes=[x_b[t]])
        if want_xb:
            i = xb_rot.next()
            P.dve(lambda e: e.tensor_copy(out=XB[i][0:ntok, :], in_=xt_), reads=[x_b[t]], writes=[xb_b[i]])
            return i
        return None

    EPSLN = T("EPSLN", [128, 1], F32)

    P.dma("pool", lambda e: e.dma_start(out=C16[:, :], in_=cb16), writes=[c_b], extra=False)
    P.dma("sp", lambda e: e.dma_start(out=CF[:, :], in_=cf32), writes=[c_b], extra=False)
    P.dma("sp", lambda e: e.dma_start(out=FLAG[:, :], in_=flag), writes=[c_b], extra=False)
    with nc.allow_non_contiguous_dma(reason="tiny per-partition conv parameter gather"):
        for r in range(4):
            P.dma("sp", lambda e, r=r: e.dma_start(out=CONVP[:, r, :],
                                                   in_=convp[r, :].rearrange("(c p) -> p c", p=128),
                                                   allow_slow_non_contiguous=True),
                  writes=[c_b], extra=False)
    P.dve(lambda e: e.memset(EPSLN[:, :], LN_EPS), writes=[c_b])
    P.dve(lambda e: e.memset(USTATE[:, :, :], 0.0), writes=[us_b])
    P.dve(lambda e: e.memset(S[:, :], 0.0), writes=[s_b])
    P.dve(lambda e: e.memset(S16[:, :], 0.0), writes=[s16_b])

    def build_mkt(mk_tm, mk_buf):
        for mc in range(2):
            bk = tp_rot.next()
            src = mk_tm[:, mc, :]
            P.pe(lambda e, src=src, bk=bk: transposes(e, bk, src, 128, 8), reads=[mk_buf, c_b], writes=[ps_b[bk]])
            dst = MM["MKT"][:, :, mc * 128:(mc + 1) * 128]
            srcp = PSB[:, bk, :].rearrange("p (a b) -> p a b", a=8)
            P.dve(lambda e, dst=dst, srcp=srcp: e.tensor_copy(out=dst, in_=srcp), reads=[ps_b[bk]], writes=[mkt_b])

    def mem_phase_prompt():
        fence()
        AR.reset()
        MEMB = AR.take([128, 2, D], BF16)
        MEMT = AR.take([128, 8, 256], BF16)
        STG = AR.take([128, D], F32)
        memb_b, memt_b, stg_b = Buf("memb"), Buf("memt"), Buf("stg")
        P.dma("pool", lambda e: e.dma_start(out=MEMB, in_=memp.rearrange("(c p) n -> p c n", p=128)),
              writes=[memb_b])
        for mc in range(2):
            bk = tp_rot.next()
            src = MEMB[:, mc, :]
            P.pe(lambda e, src=src, bk=bk: transposes(e, bk, src, 128, 8), reads=[memb_b, c_b], writes=[ps_b[bk]])
            dst = MEMT[:, :, mc * 128:(mc + 1) * 128]
            srcp = PSB[:, bk, :].rearrange("p (a b) -> p a b", a=8)
            P.dve(lambda e, dst=dst, srcp=srcp: e.tensor_copy(out=dst, in_=srcp), reads=[ps_b[bk]], writes=[memt_b])
        for wi, (w, outd) in enumerate(((w_k, mk_o), (w_v, mv_o))):
            for cbk in range(2):
                si = wload(wview(w, cbk * 512))
                for mc in range(2):
                    bk = pj_rot.next()

                    def f(e, si=si, mc=mc, bk=bk):
                        ins = None
                        for kc in range(8):
                            ins = e.matmul(PS[:, bk, :], MEMT[:, kc, mc * 128:(mc + 1) * 128], WS[si][:, kc, :],
                                           start=(kc == 0), stop=(kc == 7))
                        return ins
                    P.pe(f, reads=[memt_b, ws_b[si]], writes=[ps_b[bk]])
                    sg_ = STG[:, cbk * 512:(cbk + 1) * 512]
                    P.act(lambda e, sg_=sg_, bk=bk: e.activation(out=sg_, in_=PS[:, bk, :], func=AF.Identity),
                          reads=[ps_b[bk]], writes=[stg_b])
                    dsto = outd[cbk * 2:cbk * 2 + 2, mc * 128:(mc + 1) * 128, :].rearrange("h p d -> p h d")
                    srco = sg_.rearrange("p (h d) -> p h d", h=2)
                    P.dma("sp", lambda e, dsto=dsto, srco=srco: e.dma_start(out=dsto, in_=srco), reads=[stg_b],
                          writes=[mko_b])

    def mem_load(srck, srcv, MKTM, mktm_b, rd):
        MV = MM["MV"]
        for mc in range(2):
            P.dma("pool", lambda e, mc=mc: e.dma_start(
                out=MKTM[:, mc, :].rearrange("p (h d) -> p h d", h=4),
                in_=srck[:, mc * 128:(mc + 1) * 128, :].rearrange("h p d -> p h d")), reads=rd, writes=[mktm_b])
            P.dma("pool", lambda e, mc=mc: e.dma_start(
                out=MV[:, mc, :].rearrange("p (h d) -> p h d", h=4),
                in_=srcv[:, mc * 128:(mc + 1) * 128, :].rearrange("h p d -> p h d")), reads=rd, writes=[mv_b])
        build_mkt(MKTM, mktm_b)

    def proj_tok(si, t, ntok, col0):
        bk = pj_rot.next()

        def f(e):
            ins = None
            for kc in range(8):
                ins = e.matmul(PS[0:ntok, bk, :], XT[:, kc, col0:col0 + ntok], WS[si][:, kc, :],
                               start=(kc == 0), stop=(kc == 7))
            return ins
        P.pe(f, reads=[xt_b[t], ws_b[si]], writes=[ps_b[bk]])
        return bk

    def rope(bk, ntok, ri, dst_bf, dst_buf, T1, T2, t12_b, kscale):
        ps4 = PS[0:ntok, bk, :].rearrange("p (h c d) -> p h c d", h=4, c=2)
        c2 = ROPE[ri][0:ntok, 0:128].unsqueeze(1).to_broadcast([ntok, 4, 128])
        s_lo = ROPE[ri][0:ntok, 128:192].unsqueeze(1).to_broadcast([ntok, 4, 64])
        s_hi = ROPE[ri][0:ntok, 192:256].unsqueeze(1).to_broadcast([ntok, 4, 64])
        t1 = T1[0:ntok, :].rearrange("p (h d) -> p h d", h=4)
        t2 = T2[0:ntok, :].rearrange("p (h c d) -> p h c d", h=4, c=2)

        def f(e):
            e.tensor_tensor(out=t1, in0=PS[0:ntok, bk, :].rearrange("p (h d) -> p h d", h=4), in1=c2, op=ALU.mult)
            e.tensor_tensor(out=t2[:, :, 0, :], in0=ps4[:, :, 1, :], in1=s_lo, op=ALU.mult)
            return e.tensor_tensor(out=t2[:, :, 1, :], in0=ps4[:, :, 0, :], in1=s_hi, op=ALU.mult)
        P.dve(f, reads=[ps_b[bk], rope_b[ri]], writes=[t12_b])
        if not kscale:
            P.dve(lambda e: e.tensor_tensor(out=dst_bf, in0=T1[0:ntok, :], in1=T2[0:ntok, :], op=ALU.add),
                  reads=[t12_b], writes=[dst_buf])
        else:
            def g(e):
                ins = None
                for h in range(4):
                    ins = e.tensor_scalar(out=dst_bf[:, h * 128:(h + 1) * 128], in0=T1[0:ntok, h * 128:(h + 1) * 128],
                                          scalar1=KD[0:ntok, h:h + 1], scalar2=None, op0=ALU.mult)
                return ins
            P.chain("dve", [lambda e: e.tensor_tensor(out=T1[0:ntok, :], in0=T1[0:ntok, :], in1=T2[0:ntok, :],
                                                      op=ALU.add), g],
                    reads=[t12_b, c_b], writes=[dst_buf, t12_b])

    def ret_state_update(kbf, vbf, ntok, kv_bufs, GL):
        bk = 6

        def f(e):
            ins = None
            for h in range(4):
                ins = e.matmul(PS[:, bk, h * 128:(h + 1) * 128], kbf[0:ntok, h * 128:(h + 1) * 128],
                               vbf[0:ntok, h * 128:(h + 1) * 128], start=True, stop=True)
            return ins
        P.pe(f, reads=kv_bufs, writes=[ps_b[bk]])

        def g(e):
            ins = None
            for h in range(4):
                ins = e.tensor_scalar(out=S[:, h * 128:(h + 1) * 128], in0=S[:, h * 128:(h + 1) * 128],
                                      scalar1=float(GL[h]), scalar2=None, op0=ALU.mult)
            return ins
        P.chain("dve", [lambda e: e.tensor_tensor(out=S[:, :], in0=S[:, :], in1=PS[:, bk, :], op=ALU.add), g],
                reads=[ps_b[bk], s_b, c_b], writes=[s_b])
        P.dve(lambda e: e.tensor_copy(out=S16[:, :], in_=S[:, :]), reads=[s_b], writes=[s16_b])

    def history_phase():
        for g0 in range(0, NH_TILES, 5):
            tiles = list(range(g0, min(g0 + 5, NH_TILES)))
            fence()
            AR.reset()
            KBF = AR.take([128, 5, 512], BF16)
            VBF = AR.take([128, 5, 512], BF16)
            T1s = [AR.take([128, 512], F32) for _ in range(2)]
            T2s = [AR.take([128, 512], F32) for _ in range(2)]
            SKBs = [AR.take([128, 512], BF16) for _ in range(2)]
            kbf_b = [Buf("kbf%d" % i) for i in range(5)]
            vbf_b = [Buf("vbf%d" % i) for i in range(5)]
            t12s_b, skbs_b = [Buf("t12a"), Buf("t12b")], [Buf("skba"), Buf("skbb")]
            ropei = {}
            for li, t in enumerate(tiles):
                i = xb_rot.next()
                P.dma("pool", lambda e, i=i, t=t: e.dma_start(out=XB[i][:, :], in_=xh[t * 128:(t + 1) * 128, :]),
                      writes=[xb_b[i]], extra=False)
                to_xt(i, li, 128, li * 128)
            for cbk in (1, 2, 5, 6):
                si = wload(wview(w_in, cbk * 512))
                bks = {0: proj_tok(si, 0, 128, 0)}
                for li, t in enumerate(tiles):
                    if li + 1 < len(tiles):
                        bks[li + 1] = proj_tok(si, li + 1, 128, (li + 1) * 128)
                    bk = bks[li]
                    T1, T2, t12_b = T1s[li % 2], T2s[li % 2], t12s_b[li % 2]
                    SKB, skb_b = SKBs[li % 2], skbs_b[li % 2]
                    if cbk == 1:
                        ri = rope_rot.next()
                        P.dma("sp", lambda e, ri=ri, t=t: e.dma_start(out=ROPE[ri][:, :], in_=ropeh[t]),
                              writes=[rope_b[ri]], extra=False)
                        rope(bk, 128, ri, KBF[:, li, :], kbf_b[li], T1, T2, t12_b, True)
                    elif cbk == 2:
                        P.act(lambda e, li=li, bk=bk: e.activation(out=VBF[:, li, :], in_=PS[:, bk, :], func=AF.Identity),
                              reads=[ps_b[bk]], writes=[vbf_b[li]])
                        ret_state_update(KBF[:, li, :], VBF[:, li, :], 128, [kbf_b[li], vbf_b[li]], GL128)
                    elif cbk == 5:
                        P.dve(lambda e, bk=bk, SKB=SKB: e.tensor_copy(out=SKB[:, :], in_=PS[:, bk, :]),
                              reads=[ps_b[bk]], writes=[skb_b])
                        tb = tp_rot.next()
                        P.pe(lambda e, tb=tb, SKB=SKB: transposes(e, tb, SKB, 128, 4), reads=[skb_b, c_b],
                             writes=[ps_b[tb]])
                        dst = KT[:, :, t * 128:(t + 1) * 128]
                        srcp = PSB[:, tb, 0:512].rearrange("p (a b) -> p a b", a=4)
                        P.act(lambda e, dst=dst, srcp=srcp: e.activation(out=dst, in_=srcp, func=AF.Identity),
                              reads=[ps_b[tb]], writes=[kt_b[t]])
                    else:
                        P.act(lambda e, t=t, bk=bk: e.activation(out=VS[:, t, :], in_=PS[:, bk, :], func=AF.Identity),
                              reads=[ps_b[bk]], writes=[vs_b[t]])

    def sb_attention(SQT, sqt_bufs, qcol0, nq, keyblocks, SOT, sot_buf, tmp, hook=None):
        EXPZ, SP, AT, R = tmp["EXPZ"], tmp["SP"], tmp["AT"], tmp["R"]
        W = nq
        wide = (W == 512)
        gsz = 2 if wide else max(1, 512 // W)
        z_rot = Rot([2, 4, 6])
        o_rot = Rot([0, 1])
        kbl = list(reversed(keyblocks))
        if "nkb" in DBG:
            kbl = kbl[:DBG["nkb"]]
        groups_ = [kbl[i:i + gsz] for i in range(0, len(kbl), gsz)]
        items = []
        for h in range(DBG.get("heads", 8)):
            ob = o_rot.next()
            for ii, blks in enumerate(groups_):
                items.append({"h": h, "c": h // 2, "hp": h % 2, "ob": ob, "blks": blks, "first": ii == 0,
                              "last": ii == len(groups_) - 1, "xi": len(items) % 2})

        def zsl(zb, b, lo=0, hi=None):
            hi = W if hi is None else hi
            return PS[:, zb + b, lo:hi] if wide else PS[:, zb, b * W + lo:b * W + hi]

        def zall(zb, g):
            if wide:
                return PS[:, zb, :] if g == 1 else PS[:, zb:zb + g, :]
            return PS[:, zb, 0:g * W]

        def ball(buf, g):
            if wide and g > 1:
                return buf[:, 0:g * W].rearrange("p (g w) -> p g w", g=g)
            return buf[:, 0:g * W]

        def zbufs(zb, g):
            return [ps_b[zb + b] for b in range(g)] if wide else [ps_b[zb]]

        def stage1(it):
            blks, c, hp, xi = it["blks"], it["c"], it["hp"], it["xi"]
            g = len(blks)
            zb = z_rot.next()
            it["zb"] = zb
            qap = SQT[hp][:, c, qcol0:qcol0 + nq]

            def f_qk(e):
                ins = None
                for b, kb in enumerate(blks):
                    msk = kb["mask"]
                    st_ = True if wide else (b == 0)
                    ins = e.matmul(zsl(zb, b), kb["kt"](c, hp), qap, start=st_,
                                   stop=((wide or b == g - 1) and msk is None), skip_group_check=True)
                    if msk is not None:
                        ncol, moff = msk
                        ins = e.matmul(zsl(zb, b, 0, ncol), IDENT, NEGM[:, moff:moff + ncol],
                                       start=False, stop=(wide or b == g - 1), skip_group_check=True)
                return ins
            bufs = [x for kb in blks for x in kb["bufs"]]
            P.pe(f_qk, reads=bufs + sqt_bufs + [c_b], writes=zbufs(zb, g))
            P.chain("act", [lambda e: e.activation(out=ball(EXPZ[xi], g), in_=zall(zb, g),
                                                   func=AF.Exp, scale=0.125),
                            lambda e: e.activation(out=SP[xi][:, 0:g * W], in_=EXPZ[xi][:, 0:g * W], func=AF.Ln,
                                                   bias=ONEC[:, :], scale=1.0)],
                    reads=zbufs(zb, g) + [c_b], writes=[tmp["sp_b"][xi], tmp["ez_b"]])

        def stage2(it):
            blks, c, hp, xi, zb, ob = it["blks"], it["c"], it["hp"], it["xi"], it["zb"], it["ob"]
            g, first, last = len(blks), it["first"], it["last"]

            def f_cum(e):
                ins = None
                if wide:
                    for b in range(g):
                        ins = e.matmul(zsl(zb, b), NEGU8[:, :], SP[xi][:, b * W:(b + 1) * W], start=False,
                                       stop=False, skip_group_check=True)
                        if not first:
                            ins = e.matmul(zsl(zb, b), NEGO8[:, :], R[:, 0:W], start=False, stop=False,
                                           skip_group_check=True)
                        for j in range(b):
                            ins = e.matmul(zsl(zb, b), NEGO8[:, :], SP[xi][:, j * W:(j + 1) * W], start=False,
                                           stop=False, skip_group_check=True)
                    return ins
                nmm = 1 + (0 if first else 1) + (g - 1)
                k = 0
                ins = e.matmul(PS[:, zb, 0:g * W], NEGU8[:, :], SP[xi][:, 0:g * W], start=False, stop=(k == nmm - 1),
                               skip_group_check=True)
                k += 1
                if not first:
                    rb = R[:, 0:W] if g == 1 else R[:, 0:W].unsqueeze(1).to_broadcast([128, g, W])
                    ob_ = PS[:, zb, 0:g * W] if g == 1 else PS[:, zb, 0:g * W].rearrange("p (g w) -> p g w", g=g)
                    ins = e.matmul(ob_, NEGO8[:, :], rb, start=False, stop=(k == nmm - 1), skip_group_check=True)
                    k += 1
                for j in range(g - 1):
                    r = g - 1 - j
                    sb_ = SP[xi][:, j * W:(j + 1) * W]
                    o2 = PS[:, zb, (j + 1) * W:g * W]
                    if r > 1:
                        sb_ = sb_.unsqueeze(1).to_broadcast([128, r, W])
                        o2 = o2.rearrange("p (g w) -> p g w", g=r)
                    ins = e.matmul(o2, NEGO8[:, :], sb_, start=False, stop=(k == nmm - 1), skip_group_check=True)
                    k += 1
                return ins
            P.pe(f_cum, reads=[tmp["sp_b"][xi], tmp["r_b"], c_b], writes=zbufs(zb, g))
            P.act(lambda e: e.activation(out=ball(AT[xi], g), in_=zall(zb, g), func=AF.Exp, scale=0.125),
                  reads=zbufs(zb, g), writes=[tmp["at_b"][xi]])
            if not last:
                fr = []
                if wide:
                    if first:
                        if g == 1:
                            fr.append(lambda e: e.tensor_copy(out=R[:, 0:W], in_=SP[xi][:, 0:W]))
                        else:
                            fr.append(lambda e: e.tensor_tensor(out=R[:, 0:W], in0=SP[xi][:, 0:W],
                                                                in1=SP[xi][:, W:2 * W], op=ALU.add))
                    else:
                        for b in range(g):
                            fr.append(lambda e, b=b: e.tensor_tensor(out=R[:, 0:W], in0=R[:, 0:W],
                                                                     in1=SP[xi][:, b * W:(b + 1) * W], op=ALU.add))
                elif g == 1:
                    if first:
                        fr.append(lambda e: e.tensor_copy(out=R[:, 0:W], in_=SP[xi][:, 0:W]))
                    else:
                        fr.append(lambda e: e.tensor_tensor(out=R[:, 0:W], in0=R[:, 0:W], in1=SP[xi][:, 0:W],
                                                            op=ALU.add))
                else:
                    spv = SP[xi][:, 0:g * W].rearrange("p (g w) -> p w g", g=g)
                    RT = R[:, 256:512].bitcast(F32)
                    fr.append(lambda e: e.tensor_reduce(out=RT[:, 0:W], in_=spv, axis=AX.X, op=ALU.add))
                    if first:
                        fr.append(lambda e: e.tensor_copy(out=R[:, 0:W], in_=RT[:, 0:W]))
                    else:
                        fr.append(lambda e: e.tensor_tensor(out=R[:, 0:W], in0=R[:, 0:W], in1=RT[:, 0:W],
                                                            op=ALU.add))
                P.chain("dve", fr, reads=[tmp["sp_b"][xi]], writes=[tmp["r_b"]])

        def stage3(it):
            blks, c, hp, xi, ob = it["blks"], it["c"], it["hp"], it["xi"], it["ob"]
            g, first, last = len(blks), it["first"], it["last"]

            def f_av(e):
                ins = None
                for b, kb in enumerate(blks):
                    ins = e.matmul(PS[:, ob, 0:W], kb["v"](c), AT[xi][:, b * W:(b + 1) * W],
                                   start=(first and b == 0), stop=(last and b == g - 1))
                return ins
            bufs = [x for kb in blks for x in kb["bufs"]]
            P.pe(f_av, reads=bufs + [tmp["at_b"][xi]], writes=[ps_b[ob]])
            if last:
                dst = SOT[hp * 64:(hp + 1) * 64, c, qcol0:qcol0 + nq]
                P.dve(lambda e: e.tensor_copy(out=dst, in_=PS[hp * 64:(hp + 1) * 64, ob, 0:nq]),
                      reads=[ps_b[ob]], writes=[sot_buf])

        n_it = len(items)
        for k in range(-2, n_it):
            if hook is not None:
                hook()
            if 0 <= k + 2 < n_it:
                stage1(items[k + 2])
            if 0 <= k + 1 < n_it:
                stage2(items[k + 1])
            if 0 <= k:
                stage3(items[k])

    ONEC = T("ONEC", [128, 1], F32)
    P.dve(lambda e: e.memset(ONEC[:, :], 1.0), writes=[c_b])
    NEGU8 = T("NEGU8", [128, 128], BF16)
    NEGO8 = T("NEGO8", [128, 128], BF16)
    P.dve(lambda e: e.tensor_scalar(out=NEGU8[:, :], in0=NEGU, scalar1=8.0, scalar2=None, op0=ALU.mult),
          reads=[c_b], writes=[c_b])
    P.dve(lambda e: e.tensor_scalar(out=NEGO8[:, :], in0=NEGO, scalar1=8.0, scalar2=None, op0=ALU.mult),
          reads=[c_b], writes=[c_b])

    def prefetch_sample_v(part):
        for n in range(part * 8, part * 8 + 8):
            sq_, blk = n // 16, n % 16
            gb = sq_ * 16 + blk
            P.dma("pool", lambda e, sq_=sq_, blk=blk, gb=gb: e.dma_start(
                out=VS[:, gb, :].rearrange("p (h d) -> p h d", h=8),
                in_=cv[sq_, :, blk * 128:(blk + 1) * 128, :].rearrange("h p d -> p h d")),
                writes=[vs_b[gb]], extra=False)

    def main_group(gi, tiles):
        nt = len(tiles)
        ntot = sum(t["ntok"] for t in tiles)
        is_s = tiles[0]["kind"] == "s"
        GL = GL32 if is_s else GL128
        fence()
        AR.reset()
        MIXT = AR.take([128, 8, 640], BF16)
        mixt_b = [Buf("mixt%d" % i) for i in range(nt)]
        sot_b = Buf("sot")
        mark = AR.off
        VBF = AR.take([128, 5, 512], BF16)
        KBF = AR.take([128, 5, 512], BF16)
        QKT = AR.take([128, 8, 640], BF16)
        SG = AR.take([128, 5, 512], BF16)
        T1 = AR.take([128, 512], F32)
        T2 = AR.take([128, 512], F32)
        SCB = AR.take([128, 512], BF16)
        ROB = AR.take([128, 512], BF16)
        QBF = ROB
        vbf_b = [Buf("vbf%d" % i) for i in range(nt)]
        kbf_b = [Buf("kbf%d" % i) for i in range(nt)]
        qkt_b = [Buf("qkt%d" % i) for i in range(nt)]
        sg_b = [Buf("sg%d" % i) for i in range(nt)]
        t12_b, scb_b, rob_b = Buf("t12"), Buf("scb"), Buf("rob")
        qbf_b = rob_b
        ropei = []
        for li, t in enumerate(tiles):
            ntok = t["ntok"]
            nr = t["nreal"]
            i = xb_rot.next()
            if nr < ntok:
                P.dve(lambda e, li=li: e.memset(X[:, li, :], 0.0), writes=[x_b[li]])
                P.dve(lambda e, i=i: e.memset(XB[i][:, :], 0.0), writes=[xb_b[i]])
            P.dma("sp", lambda e, li=li, t=t, nr=nr: e.dma_start(out=X[0:nr, li, :], in_=t["xsrc"]),
                  writes=[x_b[li]], extra=False)
            P.dma("pool", lambda e, i=i, t=t, nr=nr: e.dma_start(out=XB[i][0:nr, :], in_=t["xsrc"]),
                  writes=[xb_b[i]], extra=False)
            to_xt(i, li, ntok, t["col0"])
        for cbk in (1, 2, 0, 3):
            si = wload(wview(w_in, cbk * 512))
            bks = {0: proj_tok(si, 0, tiles[0]["ntok"], tiles[0]["col0"])}
            for li, t in enumerate(tiles):
                ntok, col0 = t["ntok"], t["col0"]
                if li + 1 < nt:
                    bks[li + 1] = proj_tok(si, li + 1, tiles[li + 1]["ntok"], tiles[li + 1]["col0"])
                bk = bks[li]
                if cbk == 1:
                    ri = rope_rot.next()
                    P.dma("sp", lambda e, ri=ri, t=t, ntok=ntok: e.dma_start(out=ROPE[ri][0:t["nreal"], :], in_=t["rope"]),
                          writes=[rope_b[ri]], extra=False)
                    t["ri"] = ri
                    rope(bk, ntok, ri, KBF[0:ntok, li, :], kbf_b[li], T1, T2, t12_b, True)
                    tb = tp_rot.next()
                    P.pe(lambda e, tb=tb, li=li, ntok=ntok: transposes(e, tb, KBF[:, li, :], ntok, 4),
                         reads=[kbf_b[li], c_b], writes=[ps_b[tb]])
                    dst = QKT[:, 4:8, col0:col0 + ntok]
                    srcp = PSB[:, tb, 0:512].rearrange("p (a b) -> p a b", a=4)[:, :, 0:ntok]
                    P.act(lambda e, dst=dst, srcp=srcp: e.activation(out=dst, in_=srcp, func=AF.Identity),
                          reads=[ps_b[tb]], writes=[qkt_b[li]])
                elif cbk == 2:
                    P.act(lambda e, li=li, bk=bk, ntok=ntok: e.activation(out=VBF[0:ntok, li, :], in_=PS[0:ntok, bk, :],
                                                                         func=AF.Identity),
                          reads=[ps_b[bk]], writes=[vbf_b[li]])
                elif cbk == 0:
                    ri = rope_rot.next()
                    P.dma("sp", lambda e, ri=ri, t=t, ntok=ntok: e.dma_start(out=ROPE[ri][0:t["nreal"], :], in_=t["rope"]),
                          writes=[rope_b[ri]], extra=False)
                    rope(bk, ntok, ri, QBF[0:ntok, :], qbf_b, T1, T2, t12_b, False)
                    tb = tp_rot.next()
                    P.pe(lambda e, tb=tb, ntok=ntok: transposes(e, tb, QBF, ntok, 4),
                         reads=[qbf_b, c_b], writes=[ps_b[tb]])
                    dst = QKT[:, 0:4, col0:col0 + ntok]
                    srcp = PSB[:, tb, 0:512].rearrange("p (a b) -> p a b", a=4)[:, :, 0:ntok]
                    P.act(lambda e, dst=dst, srcp=srcp: e.activation(out=dst, in_=srcp, func=AF.Identity),
                          reads=[ps_b[tb]], writes=[qkt_b[li]])
                else:
                    P.act(lambda e, li=li, bk=bk, ntok=ntok: e.activation(out=SG[0:ntok, li, :], in_=PS[0:ntok, bk, :],
                                                                         func=AF.Silu),
                          reads=[ps_b[bk]], writes=[sg_b[li]])
        def _ret_tile(li, t):
            ntok, col0 = t["ntok"], t["col0"]
            so, sm_b = small_next()
            if is_s:
                s = t["seq"]
                P.dma("sp", lambda e, s=s: e.dma_start(out=S[:, :].rearrange("p (h d) -> p h d", h=4),
                                                       in_=sr[s].rearrange("h p d -> p h d")),
                      writes=[s_b], extra=False)
                P.dve(lambda e: e.tensor_copy(out=S16[:, :], in_=S[:, :]), reads=[s_b], writes=[s16_b])

            def f_sc(e, ntok=ntok, col0=col0):
                ins = None
                for h in range(4):
                    ins = e.matmul(PS[0:ntok, 4, h * 128:h * 128 + ntok], QKT[:, 4 + h, col0:col0 + ntok],
                                   QKT[:, h, col0:col0 + ntok], start=True, stop=True)
                return ins
            P.pe(f_sc, reads=[qkt_b[li]], writes=[ps_b[4]])

            def f_msk(e, ntok=ntok):
                a = PS[0:ntok, 4, :].rearrange("p (h t) -> p h t", h=4)[:, :, 0:ntok]
                m = M01[0:ntok, :].rearrange("p (h t) -> p h t", h=4)[:, :, 0:ntok]
                o = SCB[0:ntok, :].rearrange("p (h t) -> p h t", h=4)[:, :, 0:ntok]
                return e.tensor_tensor(out=o, in0=a, in1=m, op=ALU.mult)
            P.dve(f_msk, reads=[ps_b[4], c_b], writes=[scb_b])

            def f_o(e, ntok=ntok, col0=col0, li=li):
                ins = None
                for h in range(4):
                    e.matmul(PS[0:ntok, 5, h * 128:(h + 1) * 128], SCB[0:ntok, h * 128:h * 128 + ntok],
                             VBF[0:ntok, li, h * 128:(h + 1) * 128], start=True, stop=False)
                    ins = e.matmul(PS[0:ntok, 5, h * 128:(h + 1) * 128], QKT[:, h, col0:col0 + ntok],
                                   S16[:, h * 128:(h + 1) * 128], start=False, stop=True)
                return ins
            P.pe(f_o, reads=[scb_b, vbf_b[li], qkt_b[li], s16_b], writes=[ps_b[5]])
            ss = SMALL[0:ntok, so + 16:so + 20]
            rs = SMALL[0:ntok, so + 20:so + 24]

            def f_ss(e, ntok=ntok):
                ins = None
                for h in range(4):
                    ins = e.activation(out=T1[0:ntok, h * 128:(h + 1) * 128], in_=PS[0:ntok, 5, h * 128:(h + 1) * 128],
                                       func=AF.Square, accum_out=SMALL[0:ntok, so + 16 + h:so + 17 + h])
                return ins
            P.act(f_ss, reads=[ps_b[5]], writes=[sm_b, t12_b])
            P.dve(lambda e, ntok=ntok, ss=ss: e.scalar_tensor_tensor(out=ss, in0=ss, scalar=1.0 / 128.0,
                                                                    in1=EPSC[0:ntok, :], op0=ALU.mult, op1=ALU.add),
                  reads=[sm_b, c_b], writes=[sm_b])

            P.chain("act", [lambda e, ss=ss, rs=rs: e.activation(out=rs, in_=ss, func=AF.Ln),
                            lambda e, rs=rs: e.activation(out=rs, in_=rs, func=AF.Exp, scale=-0.5)],
                    reads=[sm_b], writes=[sm_b])

            def f_ro(e, ntok=ntok, li=li):
                ins = None
                for h in range(4):
                    ins = e.scalar_tensor_tensor(out=ROB[0:ntok, h * 128:(h + 1) * 128],
                                                 in0=PS[0:ntok, 5, h * 128:(h + 1) * 128],
                                                 scalar=SMALL[0:ntok, so + 20 + h:so + 21 + h],
                                                 in1=SG[0:ntok, li, h * 128:(h + 1) * 128],
                                                 op0=ALU.mult, op1=ALU.mult)
                return ins
            P.dve(f_ro, reads=[ps_b[5], sm_b, sg_b[li]], writes=[rob_b])
            tb = tp_rot.next()
            P.pe(lambda e, tb=tb, ntok=ntok: transposes(e, tb, ROB, ntok, 4), reads=[rob_b, c_b], writes=[ps_b[tb]])
            dst = MIXT[:, 0:4, col0:col0 + ntok]
            srcp = PSB[:, tb, 0:512].rearrange("p (a b) -> p a b", a=4)[:, :, 0:ntok]
            P.act(lambda e, dst=dst, srcp=srcp: e.activation(out=dst, in_=srcp, func=AF.Identity),
                  reads=[ps_b[tb]], writes=[mixt_b[li]])
            ret_state_update(KBF[:, li, :], VBF[:, li, :], ntok, [kbf_b[li], vbf_b[li]], GL)
            if is_s:
                s = t["seq"]
                P.dma("sp", lambda e, s=s: e.dma_start(out=sret_s[s].rearrange("h p d -> p h d"),
                                                       in_=S[:, :].rearrange("p (h d) -> p h d", h=4)), reads=[s_b])
            elif t.get("last"):
                P.dma("sp", lambda e: e.dma_start(out=sret_o.rearrange("h p d -> p h d"),
                                                  in_=S[:, :].rearrange("p (h d) -> p h d", h=4)), reads=[s_b])
        for li, t in enumerate(tiles):
            _ret_tile(li, t)
        if stop_after == "ret":
            return
        fence()
        AR.off = mark
        SQTE = AR.take([128, 4, 640], BF16)
        SQTO = AR.take([128, 4, 640], BF16)
        SQT = (SQTE, SQTO)
        SQB = AR.take([128, 512], BF16)
        SKB = AR.take([128, 512], BF16)
        STK = AR.take([128, 512], F32)
        STV = AR.take([128, 512], F32)
        zw = 512 if is_s else 1024
        if is_s:
            KTOWN = AR.take([128, 4, 256], BF16)
            VOWN = AR.take([128, 2, 512], BF16)
            KSTG = [AR.take([128, 2, 512], BF16) for _ in range(2)]
        EXPZ = [AR.take([128, zw], F32)] * 2
        SPt = [AR.take([128, zw], BF16) for _ in range(2)]
        ATt = [AR.take([128, zw], BF16) for _ in range(2)]
        Rt = AR.take([128, 512], BF16)
        tmp = {"EXPZ": EXPZ, "SP": SPt, "AT": ATt, "R": Rt,
               "ez_b": Buf("expz"), "sp_b": [Buf("sp0"), Buf("sp1")], "at_b": [Buf("at0"), Buf("at1")], "r_b": Buf("R")}
        sqt_b = [Buf("sqt%d" % i) for i in range(nt)]
        P.dve(lambda e: e.memset(SQTE[:, :, :], 0.0), writes=sqt_b)
        P.dve(lambda e: e.memset(SQTO[:, :, :], 0.0), writes=sqt_b)
        sqb_b, skb_b, stk_b, stv_b = Buf("sqb"), Buf("skb"), Buf("stk"), Buf("stv")
        ktown_b = [Buf("ktown0"), Buf("ktown1")]
        vown_b = [Buf("vown0"), Buf("vown1")]
        kstg_b = [Buf("kstg0"), Buf("kstg1")]
        for cbk in (6, 5, 4):
            si = wload(wview(w_in, cbk * 512))
            bks = {0: proj_tok(si, 0, tiles[0]["ntok"], tiles[0]["col0"])}
            for li, t in enumerate(tiles):
                ntok, col0 = t["ntok"], t["col0"]
                if li + 1 < nt:
                    bks[li + 1] = proj_tok(si, li + 1, tiles[li + 1]["ntok"], tiles[li + 1]["col0"])
                bk = bks[li]
                if cbk == 6:
                    if is_s:
                        s = t["seq"]
                        P.dve(lambda e, s=s, bk=bk, ntok=ntok: e.tensor_copy(out=VOWN[0:ntok, s, :], in_=PS[0:ntok, bk, :]),
                              reads=[ps_b[bk]], writes=[vown_b[s]])
                    else:
                        blk = t["blk"]
                        P.dve(lambda e, blk=blk, bk=bk: e.tensor_copy(out=VS[:, blk, :], in_=PS[:, bk, :]),
                              reads=[ps_b[bk]], writes=[vs_b[blk]])
                    if (t["otile"] is not None or is_s) and not DBG.get("nokv"):
                        P.dve(lambda e, bk=bk, ntok=ntok: e.tensor_copy(out=STV[0:ntok, :], in_=PS[0:ntok, bk, :]),
                              reads=[ps_b[bk]], writes=[stv_b])
                        for hh in range(8):
                            if is_s:
                                dsto = v_s[t["seq"], hh]
                            else:
                                ot = t["otile"]
                                dsto = v_o[hh, ot * 128:(ot + 1) * 128, :]
                            if DBG.get("nodma"):
                                continue
                            P.dma("sp", lambda e, dsto=dsto, nr=t["nreal"], hh=hh: e.dma_start(
                                out=dsto, in_=STV[0:nr, hh * 64:(hh + 1) * 64]), reads=[stv_b])
                elif cbk == 5:
                    P.dve(lambda e, bk=bk, ntok=ntok: e.tensor_copy(out=SKB[0:ntok, :], in_=PS[0:ntok, bk, :]),
                          reads=[ps_b[bk]], writes=[skb_b])
                    tb = tp_rot.next()
                    P.pe(lambda e, tb=tb, ntok=ntok: transposes(e, tb, SKB, ntok, 4), reads=[skb_b, c_b], writes=[ps_b[tb]])
                    srcp = PSB[:, tb, 0:512].rearrange("p (a b) -> p a b", a=4)[:, :, 0:ntok]
                    if is_s:
                        s = t["seq"]
                        dst = KTOWN[:, :, s * 128:(s + 1) * 128]
                        P.dve(lambda e, dst=dst, srcp=srcp: e.tensor_copy(out=dst, in_=srcp),
                              reads=[ps_b[tb]], writes=[ktown_b[s]])
                    else:
                        blk = t["blk"]
                        dst = KT[:, :, blk * 128:(blk + 1) * 128]
                        P.dve(lambda e, dst=dst, srcp=srcp: e.tensor_copy(out=dst, in_=srcp),
                              reads=[ps_b[tb]], writes=[kt_b[blk]])
                    if (t["otile"] is not None or is_s) and not DBG.get("nokv"):
                        P.dve(lambda e, bk=bk, ntok=ntok: e.tensor_copy(out=STK[0:ntok, :], in_=PS[0:ntok, bk, :]),
                              reads=[ps_b[bk]], writes=[stk_b])
                        for hh in range(8):
                            if is_s:
                                dsto = k_s[t["seq"], hh]
                            else:
                                ot = t["otile"]
                                dsto = k_o[hh, ot * 128:(ot + 1) * 128, :]
                            if DBG.get("nodma"):
                                continue
                            P.dma("sp", lambda e, dsto=dsto, nr=t["nreal"], hh=hh: e.dma_start(
                                out=dsto, in_=STK[0:nr, hh * 64:(hh + 1) * 64]), reads=[stk_b])
                else:
                    P.dve(lambda e, bk=bk, ntok=ntok: e.tensor_copy(out=SQB[0:ntok, :], in_=PS[0:ntok, bk, :]),
                          reads=[ps_b[bk]], writes=[sqb_b])
                    tb = tp_rot.next()
                    P.pe(lambda e, tb=tb, ntok=ntok: transposes(e, tb, SQB, ntok, 4), reads=[sqb_b, c_b], writes=[ps_b[tb]])
                    srcp = PSB[:, tb, 0:512].rearrange("p (a b) -> p a b", a=4)[:, :, 0:ntok]
                    def f_sq(e, srcp=srcp, col0=col0, ntok=ntok):
                        e.tensor_copy(out=SQTE[0:64, :, col0:col0 + ntok], in_=srcp[0:64])
                        return e.tensor_copy(out=SQTO[64:128, :, col0:col0 + ntok], in_=srcp[64:128])
                    P.dve(f_sq, reads=[ps_b[tb]], writes=[sqt_b[li]])
        SOT = MIXT[:, 4:8, :]
        if stop_after == "sbproj":
            return
        if not is_s:
            b0 = tiles[0]["blk"]
            if gi in conv_sched:
                conv_pieces(conv_sched[gi])

            def mk_kb(blk, mask):
                return {"nk": 128,
                        "kt": (lambda c, hp, blk=blk: KT[:, c, blk * 128:(blk + 1) * 128]),
                        "v": (lambda c, blk=blk: VS[:, blk, c * 128:(c + 1) * 128]),
                        "bufs": [kt_b[blk], vs_b[blk]], "mask": mask}
            for (qa, qb) in ([(0, 4), (4, 5)] if nt == 5 else [(0, nt)]):
                kbs = []
                for blk in range(0, b0 + qb):
                    j = blk - b0
                    if j >= qa:
                        jj = j - qa
                        mask = ((jj + 1) * 128, (4 - jj) * 128)
                    else:
                        mask = None
                    kbs.append(mk_kb(blk, mask))
                sb_attention(SQT, [sqt_b[i] for i in range(qa, qb)], qa * 128, (qb - qa) * 128, kbs, SOT, sot_b, tmp,
                             hook=(conv_hook if gi in conv_sched else None))
            if gi in conv_sched:
                conv_step(1000)
                conv_done.update(conv_sched[gi])
            if gi == 3:
                prefetch_sample_v(0)
        else:
            kt_rot = Rot([3, 5])

            def k_steps(s):
                steps = []
                for q8 in range(8):
                    ki = q8 % 2

                    def dma_step(s=s, q8=q8, ki=ki):
                        for bb in range(2):
                            P.dma("pool", lambda e, bb=bb: e.dma_start(
                                out=KSTG[ki][:, bb, :].rearrange("p (h d) -> p h d", h=8),
                                in_=ck[s, :, (q8 * 2 + bb) * 128:(q8 * 2 + bb + 1) * 128, :].rearrange("h p d -> p h d")),
                                writes=[kstg_b[ki]])

                    def tr_step(s=s, q8=q8, ki=ki):
                        for bb in range(2):
                            blk = s * 16 + q8 * 2 + bb
                            tb = kt_rot.next()
                            P.pe(lambda e, tb=tb, bb=bb: transposes(e, tb, KSTG[ki][:, bb, :], 128, 4),
                                 reads=[kstg_b[ki], c_b], writes=[ps_b[tb]])
                            dst = KT[:, :, blk * 128:(blk + 1) * 128]
                            srcp = PSB[:, tb, 0:512].rearrange("p (a b) -> p a b", a=4)
                            P.dve(lambda e, dst=dst, srcp=srcp: e.tensor_copy(out=dst, in_=srcp),
                                  reads=[ps_b[tb]], writes=[kt_b[blk]])
                    steps.append(dma_step)
                    if q8 >= 1:
                        steps.append(prev_tr)
                    prev_tr = tr_step
                steps.append(prev_tr)
                return steps

            def kbs_for(s):
                kbs = []
                for blk in range(16):
                    gb = s * 16 + blk
                    kbs.append({"nk": 128,
                                "kt": (lambda c, hp, gb=gb: KT[:, c, gb * 128:(gb + 1) * 128]),
                                "v": (lambda c, gb=gb: VS[:, gb, c * 128:(c + 1) * 128]),
                                "bufs": [kt_b[gb], vs_b[gb]], "mask": None})
                kbs.append({"nk": 128,
                            "kt": (lambda c, hp, s=s: KTOWN[:, c, s * 128:(s + 1) * 128]),
                            "v": (lambda c, s=s: VOWN[:, s, c * 128:(c + 1) * 128]),
                            "bufs": [ktown_b[s], vown_b[s]], "mask": (128, 4 * 128)})
                return kbs
            for st_ in k_steps(tiles[0]["seq"]):
                st_()
            pend = k_steps(tiles[1]["seq"])
            cnt_ = [0]

            def k_hook():
                cnt_[0] += 1
                if cnt_[0] % 2 == 0 and pend:
                    pend.pop(0)()
            sb_attention(SQT, [sqt_b[0]], tiles[0]["col0"], 128, kbs_for(tiles[0]["seq"]), SOT, sot_b, tmp, hook=k_hook)
            while pend:
                pend.pop(0)()
            sb_attention(SQT, [sqt_b[1]], tiles[1]["col0"], 128, kbs_for(tiles[1]["seq"]), SOT, sot_b, tmp)
        if stop_after == "sb":
            return
        fence()
        AR.off = mark
        load_ln(0)
        sis = [wload(wview(w_o, cbk * 512)) for cbk in range(2)]
        xbi = {}

        def _wo_mm(li, t):
            ntok, col0 = t["ntok"], t["col0"]
            for cbk in range(2):
                bk = pj_rot.next()

                def f(e, bk=bk, cbk=cbk, ntok=ntok, col0=col0):
                    ins = None
                    for kc in range(8):
                        ins = e.matmul(PS[0:ntok, bk, :], MIXT[:, kc, col0:col0 + ntok], WS[sis[cbk]][:, kc, :],
                                       start=(kc == 0), stop=(kc == 7))
                    return ins
                P.pe(f, reads=[mixt_b[li], sot_b, ws_b[sis[cbk]]], writes=[ps_b[bk]])
                resid_add(li, ntok, cbk, bk)
        _wo_mm(0, tiles[0])
        for li, t in enumerate(tiles):
            if li + 1 < nt:
                _wo_mm(li + 1, tiles[li + 1])
            xbi[li] = layer_norm(li, t["ntok"])
            to_xt(xbi[li], li, t["ntok"], t["col0"])
        if stop_after == "ln1":
            return
        fence()
        AR.reset()
        QMT = AR.take([128, 8, 640], BF16)
        OTs = [AR.take([128, 8, 128], BF16)] * 2
        PF = AR.take([128, D], F32)
        MKTM_S = PF.bitcast(BF16).rearrange("p (a b) -> p a b", a=2)
        PBFs = [AR.take([128, D], BF16) for _ in range(2)]
        PTs = [AR.take([128, 8, 128], BF16) for _ in range(2)]
        MM["MKT"] = AR.take([128, 8, 256], BF16)
        MM["MV"] = AR.take([128, 2, D], BF16)
        MKT, MV = MM["MKT"], MM["MV"]
        qmt_b = Buf("qmt")
        ots_b = [Buf("ot0")] * 2
        pf_b = Buf("pf")
        pbfs_b = [Buf("pbf0"), Buf("pbf1")]
        pts_b = [Buf("pt0"), Buf("pt1")]
        mktm_s_b = pf_b
        load_ln(2)
        if not is_s:
            mem_load(mk_o, mv_o, MKTM_S, mktm_s_b, [mko_b])
        if gi == 3:
            prefetch_sample_v(1)
        pieces = [(c0, min(512, ntot - c0)) for c0 in range(0, ntot, 512)]
        for cbk in range(2):
            si = wload(wview(w_q, cbk * 512))
            for fc in range(4):
                fch = cbk * 4 + fc
                pb = [0, 1] if fc % 2 == 0 else [2, 3]

                def f(e, si=si, fc=fc, pb=pb):
                    ins = None
                    for pi, (c0, n) in enumerate(pieces):
                        for kc in range(8):
                            ins = e.matmul(PS[:, pb[pi], 0:n], WS[si][:, kc, fc * 128:(fc + 1) * 128],
                                           XT[:, kc, c0:c0 + n], start=(kc == 0), stop=(kc == 7))
                    return ins
                P.pe(f, reads=[xt_b[i] for i in range(nt)] + [ws_b[si]], writes=[ps_b[pb[i]] for i in range(len(pieces))])

                def g(e, fch=fch, pb=pb):
                    ins = None
                    for pi, (c0, n) in enumerate(pieces):
                        ins = e.activation(out=QMT[:, fch, c0:c0 + n], in_=PS[:, pb[pi], 0:n], func=AF.Identity)
                    return ins
                P.act(g, reads=[ps_b[pb[i]] for i in range(len(pieces))], writes=[qmt_b])
        sis = [wload(wview(w_om, cbk * 512)) for cbk in range(2)]

        def _b_tile(li, t):
            ntok, col0 = t["ntok"], t["col0"]
            so, sm_b = small_next()
            if is_s:
                mem_load(cmk[t["seq"]], cmv[t["seq"]], MKTM_S, mktm_s_b, [])
            oi = li % 2
            OT, ot_bb = OTs[oi], ots_b[oi]
            PBF, pbf_b, PT, pt_b = PBFs[oi], pbfs_b[oi], PTs[oi], pts_b[oi]

            def f_s(e, ntok=ntok, col0=col0):
                ins = None
                for h in range(4):
                    for hf in range(2):
                        ins = e.matmul(PS[0:ntok, 4 + h // 2, (h % 2) * 256:(h % 2) * 256 + 256],
                                       QMT[:, 2 * h + hf, col0:col0 + ntok], MKT[:, 2 * h + hf, :],
                                       start=(hf == 0), stop=(hf == 1))
                return ins
            P.pe(f_s, reads=[qmt_b, mkt_b], writes=[ps_b[4], ps_b[5]])
            sc3 = PS[0:ntok, 4:6, :].rearrange("p a (b m) -> p (a b) m", b=2)
            mx = SMALL[0:ntok, so + 24:so + 28]
            nmx = SMALL[0:ntok, so + 28:so + 32]

            P.chain("dve", [lambda e, sc3=sc3, mx=mx: e.tensor_reduce(out=mx, in_=sc3, axis=AX.X, op=ALU.max),
                            lambda e, mx=mx, nmx=nmx: e.tensor_scalar(out=nmx, in0=mx, scalar1=-1.0 / 16.0,
                                                                      scalar2=None, op0=ALU.mult)],
                    reads=[ps_b[4], ps_b[5]], writes=[sm_b])

            def f_e(e, ntok=ntok):
                ins = None
                for h in range(4):
                    ins = e.activation(out=PF[0:ntok, h * 256:(h + 1) * 256],
                                       in_=PS[0:ntok, 4 + h // 2, (h % 2) * 256:(h % 2) * 256 + 256],
                                       func=AF.Exp, bias=SMALL[0:ntok, so + 28 + h:so + 29 + h], scale=1.0 / 16.0,
                                       accum_out=SMALL[0:ntok, so + 32 + h:so + 33 + h])
                return ins
            P.act(f_e, reads=[ps_b[4], ps_b[5], sm_b], writes=[pf_b, sm_b])

            def f_n(e, ntok=ntok):
                ins = None
                for h in range(4):
                    ins = e.tensor_scalar(out=PBF[0:ntok, h * 256:(h + 1) * 256], in0=PF[0:ntok, h * 256:(h + 1) * 256],
                                          scalar1=SMALL[0:ntok, so + 36 + h:so + 37 + h], scalar2=None, op0=ALU.mult)
                return ins
            P.chain("dve", [lambda e, ntok=ntok: e.reciprocal(out=SMALL[0:ntok, so + 36:so + 40], in_=SMALL[0:ntok, so + 32:so + 36]),
                            f_n], reads=[pf_b, sm_b], writes=[pbf_b, sm_b])
            yield
            tb = tp_rot.next()
            P.pe(lambda e, tb=tb, ntok=ntok: transposes(e, tb, PBF, ntok, 8), reads=[pbf_b, c_b], writes=[ps_b[tb]])
            srcp = PSB[:, tb, :].rearrange("p (a b) -> p a b", a=8)[:, :, 0:ntok]
            P.act(lambda e, srcp=srcp, ntok=ntok, PT=PT: e.activation(out=PT[:, :, 0:ntok], in_=srcp, func=AF.Identity),
                  reads=[ps_b[tb]], writes=[pt_b])

            def f_pv(e, ntok=ntok):
                ins = None
                for h in range(4):
                    for hf in range(2):
                        och = 2 * h + hf
                        for mc in range(2):
                            ins = e.matmul(PS[:, 6 + och // 4, (och % 4) * 128:(och % 4) * 128 + ntok],
                                           MV[:, mc, h * 256 + hf * 128:h * 256 + hf * 128 + 128],
                                           PT[:, 2 * h + mc, 0:ntok], start=(mc == 0), stop=(mc == 1))
                return ins
            P.pe(f_pv, reads=[pt_b, mv_b], writes=[ps_b[6], ps_b[7]])
            dst = OT[:, :, 0:ntok]
            srcp2 = PS[:, 6:8, :].rearrange("p a (b m) -> p (a b) m", b=4)[:, :, 0:ntok]
            P.act(lambda e, dst=dst, srcp2=srcp2: e.activation(out=dst, in_=srcp2, func=AF.Identity),
                  reads=[ps_b[6], ps_b[7]], writes=[ot_bb])
            for cbk in range(2):
                bk = pj_rot.next()

                def f(e, bk=bk, cbk=cbk, ntok=ntok, OT=OT):
                    ins = None
                    for kc in range(8):
                        ins = e.matmul(PS[0:ntok, bk, :], OT[:, kc, 0:ntok], WS[sis[cbk]][:, kc, :],
                                       start=(kc == 0), stop=(kc == 7))
                    return ins
                P.pe(f, reads=[ot_bb, ws_b[sis[cbk]]], writes=[ps_b[bk]])
                resid_add(li, ntok, cbk, bk)
            xbi[li] = layer_norm(li, ntok)
            to_xt(xbi[li], li, ntok, col0)
        gens = [_b_tile(li, t) for li, t in enumerate(tiles)]
        if is_s:
            for g_ in gens:
                for _ in g_:
                    pass
        else:
            next(gens[0])
            for li in range(nt):
                if li + 1 < nt:
                    next(gens[li + 1])
                for _ in gens[li]:
                    pass
        if stop_after == "ln2":
            return
        fence()
        AR.reset()
        ncw_c = 640 if ntot > 512 else 512
        ndb = 1
        ln3_pre = ntot <= 512
        HT = AR.take([128, NCH, ncw_c], BF16)
        UB = [AR.take([128, ncw_c + 8], F32) for _ in range(2 * ndb)]
        CB = [AR.take([128, ncw_c], F32) for _ in range(2 * ndb)]
        SA = AR.take([128, ncw_c], F32)
        ht_b = [Buf("ht%d" % i) for i in range(NCH)]
        ub_b = [Buf("ub%d" % i) for i in range(2 * ndb)]
        cb_b = [Buf("cb%d" % i) for i in range(2 * ndb)]
        sa_b = Buf("sa")
        if ln3_pre:
            load_ln(4)
        if is_s:
            segs = [(t["col0"], t["nreal"], 2 + li * 36) for li, t in enumerate(tiles)]
        else:
            segs = [(0, ntot, 2)]
        pair_rot = Rot([(0, 1), (2, 3), (4, 5), (6, 7)])
        if gi == 3:
            prefetch_sample_v(2)
        for pg in range(6):
            nch = 4 if pg < 5 else 2
            sa_i = wload(wview(w_up, pg * 512, ncols=nch * 128), ncols=nch * 128)
            sg_i = wload(wview(w_up, DFF + pg * 512, ncols=nch * 128), ncols=nch * 128)
            for ci in range(nch):
                chunk_a = pg * 4 + ci
                for which, (si, chunk) in enumerate(((sa_i, chunk_a), (sg_i, NCH + chunk_a))):
                    pb = pair_rot.next()

                    def f(e, si=si, ci=ci, pb=pb):
                        ins = None
                        for pi, (c0, n) in enumerate(pieces):
                            for kc in range(8):
                                ins = e.matmul(PS[:, pb[pi], 0:n], WS[si][:, kc, ci * 128:(ci + 1) * 128],
                                               XT[:, kc, c0:c0 + n], start=(kc == 0), stop=(kc == 7))
                        return ins
                    pbufs = [ps_b[pb[i]] for i in range(len(pieces))]
                    P.pe(f, reads=[xt_b[i] for i in range(nt)] + [ws_b[si]], writes=pbufs)
                    bi_ = which + 2 * (chunk_a % ndb)
                    U, Cb = UB[bi_], CB[bi_]

                    def f_cp(e, pb=pb, U=U, Cb=Cb, chunk=chunk):
                        ins = None
                        for (c0, n, uo) in segs:
                            for pi, (p0, pn) in enumerate(pieces):
                                lo, hi = max(c0, p0), min(c0 + n, p0 + pn)
                                if lo >= hi:
                                    continue
                                e.activation(out=U[:, uo + lo - c0:uo + hi - c0], in_=PS[:, pb[pi], lo - p0:hi - p0],
                                             func=AF.Identity)
                                ins = e.activation(out=Cb[:, lo:hi], in_=PS[:, pb[pi], lo - p0:hi - p0],
                                                   func=AF.Identity, scale=CONVP[:, 2, chunk:chunk + 1],
                                                   bias=CONVP[:, 3, chunk:chunk + 1])
                        return ins
                    P.act(f_cp, reads=pbufs + [c_b], writes=[ub_b[bi_], cb_b[bi_]])

                    def f_cv0(e, U=U, chunk=chunk):
                        ins = e.tensor_copy(out=U[:, 0:2], in_=USTATE[:, chunk, :])
                        if gi == 0:
                            ins = e.tensor_scalar(out=U[:, 2 + 126:2 + 128], in0=U[:, 2 + 126:2 + 128],
                                                  scalar1=FLAG[:, 0:1], scalar2=None, op0=ALU.mult)
                        return ins

                    def f_cv1(e, U=U, Cb=Cb, chunk=chunk):
                        ins = None
                        for sgi, (c0, n, uo) in enumerate(segs):
                            ins = e.scalar_tensor_tensor(out=Cb[:, c0:c0 + n], in0=U[:, uo - 1:uo - 1 + n],
                                                         scalar=CONVP[:, 1, chunk:chunk + 1], in1=Cb[:, c0:c0 + n],
                                                         op0=ALU.mult, op1=ALU.add)
                        return ins

                    def f_cv2(e, U=U, Cb=Cb, chunk=chunk):
                        ins = None
                        for sgi, (c0, n, uo) in enumerate(segs):
                            e.scalar_tensor_tensor(out=Cb[:, c0:c0 + n], in0=U[:, uo - 2:uo - 2 + n],
                                                   scalar=CONVP[:, 0, chunk:chunk + 1], in1=Cb[:, c0:c0 + n],
                                                   op0=ALU.mult, op1=ALU.add)
                            if is_s:
                                ins = e.tensor_copy(out=USTS[:, sgi, chunk, :], in_=U[:, uo + n - 2:uo + n])
                            else:
                                ins = e.tensor_copy(out=USTATE[:, chunk, :], in_=U[:, uo + n - 2:uo + n])
                        return ins
                    if is_s:
                        def f_st(e, U=U, chunk=chunk):
                            ins = None
                            for sgi, (c0, n, uo) in enumerate(segs):
                                ins = e.tensor_copy(out=U[:, uo - 2:uo], in_=USTS_IN[:, sgi, chunk, :])
                            return ins
                        P.dve(f_st, reads=[usin_b], writes=[ub_b[bi_]])
                    P.chain("dve", ([] if is_s else [f_cv0]) + [f_cv1, f_cv2],
                            reads=[ub_b[bi_], cb_b[bi_], us_b, c_b], writes=[cb_b[bi_], us_b, ub_b[bi_]])
                ia_, ig_ = 2 * (chunk_a % ndb), 1 + 2 * (chunk_a % ndb)
                P.act(lambda e, ia_=ia_: e.activation(out=SA[:, 0:ntot], in_=CB[ia_][:, 0:ntot], func=AF.Silu),
                      reads=[cb_b[ia_]], writes=[sa_b])
                P.dve(lambda e, chunk_a=chunk_a, ig_=ig_: e.tensor_tensor(out=HT[:, chunk_a, 0:ntot], in0=SA[:, 0:ntot],
                                                                         in1=CB[ig_][:, 0:ntot], op=ALU.mult),
                      reads=[sa_b, cb_b[ig_]], writes=[ht_b[chunk_a]])
        with nc.allow_non_contiguous_dma(reason="feature-major conv state scatter"):
            if is_s:
                for sgi, t in enumerate(tiles):
                    for r in range(2):
                        P.dma("sp", lambda e, sgi=sgi, t=t, r=r: e.dma_start(
                            out=conv_s[t["seq"], r, :].rearrange("(c p) -> p c", p=128), in_=USTS[:, sgi, :, r],
                            allow_slow_non_contiguous=True),
                            reads=[us_b])
            elif tiles[-1].get("last"):
                for r in range(2):
                    P.dma("sp", lambda e, r=r: e.dma_start(out=conv_o[r, :].rearrange("(c p) -> p c", p=128),
                                                           in_=USTATE[:, :, r], allow_slow_non_contiguous=True),
                          reads=[us_b])
        if stop_after == "conv":
            return
        if gi == 3:
            prefetch_sample_v(3)
        for cbk in range(2):
            sl = [wload(wview(w_dn, cbk * 512, nk=8, k0=0)), wload(wview(w_dn, cbk * 512, nk=8, k0=8)),
                  wload(wview(w_dn, cbk * 512, nk=6, k0=16), nk=6)]
            for li, t in enumerate(tiles):
                ntok, col0 = t["ntok"], t["col0"]
                bk = pj_rot.next()

                def f(e, bk=bk, ntok=ntok, col0=col0, sl=sl):
                    ins = None
                    for ch in range(NCH):
                        ins = e.matmul(PS[0:ntok, bk, :], HT[:, ch, col0:col0 + ntok], WS[sl[ch // 8]][:, ch % 8, :],
                                       start=(ch == 0), stop=(ch == NCH - 1))
                    return ins
                P.pe(f, reads=ht_b + [ws_b[i] for i in sl], writes=[ps_b[bk]])
                resid_add(li, ntok, cbk, bk)
        if not ln3_pre:
            fence()
            AR.reset()
            load_ln(4)
        for li, t in enumerate(tiles):
            ntok = t["ntok"]
            layer_norm(li, ntok, want_xb=False)
            if is_s:
                dsto = y_s[t["seq"]]
            elif t["otile"] is not None:
                dsto = y_o[t["otile"] * 128:(t["otile"] + 1) * 128, :]
            else:
                continue
            P.dma("sp", lambda e, dsto=dsto, li=li, nr=t["nreal"]: e.dma_start(out=dsto, in_=X[0:nr, li, :]),
                  reads=[x_b[li]])

    USTS_IN = T("USTS_IN", [128, 2, 44, 2], F32)
    USTS = T("USTS", [128, 2, 44, 2], F32)
    usin_b = Buf("usts_in")
    with nc.allow_non_contiguous_dma(reason="feature-major conv state gather"):
        for s in range(2):
            for r in range(2):
                P.dma("sp", lambda e, s=s, r=r: e.dma_start(out=USTS_IN[:, s, :, r],
                                                            in_=scv[s, r, :].rearrange("(c p) -> p c", p=128),
                                                            allow_slow_non_contiguous=True),
                      writes=[usin_b], extra=False)

    mem_phase_prompt()
    if stop_after != "mem":
        history_phase()
        groups = [[0, 1, 2, 3, 4], [5, 6, 7, 8], [9, 10, 11, 12], [13, 14, 15, 16]]
        for gi, g in enumerate(groups):
            tiles = []
            for li, ti in enumerate(g):
                tiles.append({"ntok": 128, "nreal": 128, "col0": li * 128, "kind": "p", "blk": NH_TILES + ti,
                              "otile": (ti - 1) if ti >= 1 else None,
                              "xsrc": xo[ti * 128:(ti + 1) * 128, :], "rope": ropeo[ti],
                              "last": ti == 16})
            main_group(gi, tiles)
            if stop_after in ("g0",) or (stop_after is not None and stop_after not in ("prompt",)):
                break
        if stop_after is None:
            tiles = [{"ntok": 128, "nreal": 32, "col0": s * 128, "kind": "s", "seq": s, "otile": None,
                      "xsrc": xs[s], "rope": ropes} for s in range(2)]
            main_group(4, tiles)
    P.emit(nc, st)
    st.close()
    return nc


def _rope_table(pos):
    half = 64
    inv = (1.0 / (np.float32(10000.0) ** (np.arange(half, dtype=np.float32) / np.float32(half)))).astype(np.float32)
    ang = pos.astype(np.float32)[:, None] * inv[None, :]
    c = np.cos(ang).astype(np.float32)
    s = np.sin(ang).astype(np.float32)
    return np.concatenate([c, c, -s, s], axis=1).astype(np.float32)


def _consts():
    ident = np.eye(128, dtype=np.float32)
    j = np.arange(128)[:, None]
    s_ = np.arange(128)[None, :]
    negu = -(j >= s_).astype(np.float32)
    nego = -np.ones((128, 128), np.float32)
    diag = np.where(j < s_, 0.0, NEG).astype(np.float32)
    negm = np.concatenate([np.full((128, 512), NEG, np.float32), diag], axis=1)
    m_ = np.arange(128)
    m01h = (m_[None, :] >= m_[:, None]).astype(np.float32)
    cb16 = np.concatenate([ident, negu, nego, negm, m01h, m01h, m01h, m01h], axis=1)
    g = (1.0 - 2.0 ** (-5.0 - np.arange(4, dtype=np.float64)))
    m = np.arange(128)
    kd = np.zeros((128, 4), np.float32)
    epsc = np.zeros((128, 4), np.float32)
    for h in range(4):
        kd[:, h] = (128.0 ** -0.5) * g[h] ** (-(m + 1.0))
        epsc[:, h] = RMS_EPS / (g[h] ** (2.0 * (m + 1.0)))
    cf32 = np.concatenate([kd, epsc], axis=1).astype(np.float32)
    return cb16, cf32


_NC_CACHE = {}


def kernel(x_prompt, x_sample, cache_sb_k, cache_sb_v, state_ret, state_ffn_conv,
           cache_mem_k, cache_mem_v, mem_prompt,
           w_in, w_o, ln1_g, ln1_b, w_q_mem, w_k_mem, w_v_mem, w_o_mem, ln2_g, ln2_b,
           w_up, conv_w, conv_b, w_down, ln3_g, ln3_b, _stop_after=None):
    f = lambda a: np.ascontiguousarray(np.asarray(a, dtype=np.float32))
    x_prompt, x_sample = f(x_prompt), f(x_sample)
    cache_sb_k, cache_sb_v = f(cache_sb_k), f(cache_sb_v)
    state_ret, state_ffn_conv = f(state_ret), f(state_ffn_conv)
    cache_mem_k, cache_mem_v, mem_prompt = f(cache_mem_k), f(cache_mem_v), f(mem_prompt)
    cb16, cf32 = _consts()
    lnp = np.stack([f(ln1_g)[0], f(ln1_b)[0], f(ln2_g)[0], f(ln2_b)[0], f(ln3_g)[0], f(ln3_b)[0]])
    convp = np.concatenate([f(conv_w)[0], f(conv_b)], axis=0)
    shared = {"w_in": f(w_in)[0], "w_o": f(w_o)[0], "w_q": f(w_q_mem)[0], "w_k": f(w_k_mem)[0],
              "w_v": f(w_v_mem)[0], "w_om": f(w_o_mem)[0], "w_up": f(w_up)[0], "w_dn": f(w_down)[0],
              "lnp": np.ascontiguousarray(lnp), "convp": np.ascontiguousarray(convp),
              "cb16": cb16, "cf32": cf32, "ropes": _rope_table(2048 + np.arange(32))}
    in_maps = []
    for c in range(8):
        b, half = c // 2, c % 2
        m = dict(shared)
        if half == 0:
            m["xh"] = np.zeros((NH_TILES * 128, D), np.float32)
            m["xo"] = np.concatenate([np.zeros((128, D), np.float32), x_prompt[b, 0:2048]], axis=0)
            pos_o = np.concatenate([np.zeros(128), np.arange(2048)])
            pos_h = np.zeros(NH_TILES * 128)
            m["flag"] = np.zeros((128, 1), np.float32)
        else:
            m["xh"] = x_prompt[b, 0:NH_TILES * 128]
            m["xo"] = x_prompt[b, NH_TILES * 128:4096]
            pos_o = NH_TILES * 128 + np.arange(NO_TILES * 128)
            pos_h = np.arange(NH_TILES * 128)
            m["flag"] = np.ones((128, 1), np.float32)
        m["ropeo"] = _rope_table(pos_o).reshape(NO_TILES, 128, 256)
        m["ropeh"] = _rope_table(pos_h).reshape(NH_TILES, 128, 256)
        m["xs"] = x_sample[2 * c:2 * c + 2]
        m["memp"] = mem_prompt[b]
        m["ck"] = cache_sb_k[0, 2 * c:2 * c + 2]
        m["cv"] = cache_sb_v[0, 2 * c:2 * c + 2]
        m["sr"] = state_ret[0, 2 * c:2 * c + 2]
        m["scv"] = state_ffn_conv[0, 2 * c:2 * c + 2]
        m["cmk"] = cache_mem_k[0, 2 * c:2 * c + 2]
        m["cmv"] = cache_mem_v[0, 2 * c:2 * c + 2]
        in_maps.append({k: np.ascontiguousarray(v, dtype=np.float32) for k, v in m.items()})
    key = _stop_after
    if key not in _NC_CACHE:
        _NC_CACHE[key] = build_program(_stop_after)
    nc = _NC_CACHE[key]
    res = run_bass_kernel_spmd(nc, in_maps, core_ids=list(range(8)))
    R = res.results
    y_prompt = np.zeros((4, 4096, D), np.float32)
    nk = np.zeros((1, 4, 8, 4096, 64), np.float32)
    nv = np.zeros((1, 4, 8, 4096, 64), np.float32)
    for c in range(8):
        b, half = c // 2, c % 2
        y_prompt[b, half * 2048:(half + 1) * 2048] = R[c]["y_o"]
        nk[0, b, :, half * 2048:(half + 1) * 2048] = R[c]["k_o"]
        nv[0, b, :, half * 2048:(half + 1) * 2048] = R[c]["v_o"]
    y_sample = np.concatenate([R[c]["y_s"] for c in range(8)], axis=0)
    sret_p = np.stack([R[2 * b + 1]["sret_o"] for b in range(4)])[None]
    conv_p = np.stack([R[2 * b + 1]["conv_o"] for b in range(4)])[None]
    mk_p = np.stack([R[2 * b]["mk_o"] for b in range(4)])[None]
    mv_p = np.stack([R[2 * b]["mv_o"] for b in range(4)])[None]
    ks = np.concatenate([R[c]["k_s"] for c in range(8)], axis=0)[None]
    vs = np.concatenate([R[c]["v_s"] for c in range(8)], axis=0)[None]
    sret_s = np.concatenate([R[c]["sret_s"] for c in range(8)], axis=0)[None]
    conv_s = np.concatenate([R[c]["conv_s"] for c in range(8)], axis=0)[None]
    return (y_prompt, y_sample, nk, nv, sret_p, conv_p, mk_p, mv_p, ks, vs, sret_s, conv_s)
```

```python
import contextlib
import math
import numpy as np
import concourse.bass as bass
import concourse.mybir as mybir
from concourse.bass_utils import run_bass_kernel_spmd

F32 = mybir.dt.float32
BF16 = mybir.dt.bfloat16
AF = mybir.ActivationFunctionType
ALU = mybir.AluOpType
AX = mybir.AxisListType

D = 1024
NH_TILES = 15
NO_TILES = 17
DFF = 2816
NCH = 22
ALPHA = (2.0) ** 0.25
LN_EPS = 1e-5
RMS_EPS = 1e-6
NEG = -30000.0


class Buf:
    __slots__ = ("name", "writers", "readers")

    def __init__(self, name):
        self.name = name
        self.writers = {}
        self.readers = {}


class Op:
    __slots__ = ("idx", "eng", "fn", "deps", "signal", "seq", "is_dma", "slot", "val", "prev_val")

    def __init__(self, idx, eng, fn, is_dma):
        self.idx = idx
        self.eng = eng
        self.fn = fn
        self.deps = set()
        self.signal = False
        self.seq = 0
        self.is_dma = is_dma
        self.slot = None
        self.val = 0
        self.prev_val = 0


COMPUTE = ("pe", "act", "dve", "pool")
NDMA_SLOTS = {"sp": 24, "pool": 16}


class Prog:
    def __init__(self):
        self.ops = []
        self.dma_count = {q: 0 for q in NDMA_SLOTS}
        self.dma_slot_val = {q: [0] * n for q, n in NDMA_SLOTS.items()}
        self.extra_reads = []

    def add(self, eng, fn, reads=(), writes=(), dma=False, extra=True):
        op = Op(len(self.ops), eng, fn, dma)
        key = ("dma", op.idx) if dma else eng
        reads = list(reads)
        if extra:
            reads += self.extra_reads
        writes = list(writes)
        for b in reads:
            for k, j in b.writers.items():
                op.deps.add(j)
        for b in writes:
            for k, j in b.readers.items():
                if not (k == eng and eng == "pe"):
                    op.deps.add(j)
            for k, j in b.writers.items():
                if not (k == eng and eng == "pe"):
                    op.deps.add(j)
        for b in writes:
            b.writers = {key: op.idx}
            b.readers = {}
        for b in reads:
            if b not in writes:
                b.readers[key] = op.idx
        op.deps.discard(op.idx)
        if dma:
            n = NDMA_SLOTS[eng]
            s = self.dma_count[eng] % n
            self.dma_count[eng] += 1
            op.slot = s
            op.prev_val = self.dma_slot_val[eng][s]
            op.val = op.prev_val + 16
            self.dma_slot_val[eng][s] = op.val
        self.ops.append(op)
        return op

    def pe(self, fn, reads=(), writes=()):
        return self.add("pe", fn, reads, writes)

    def act(self, fn, reads=(), writes=()):
        return self.add("act", fn, reads, writes)

    def dve(self, fn, reads=(), writes=()):
        return self.add("dve", fn, reads, writes)

    def chain(self, eng, fns, reads=(), writes=()):
        op = None
        for i, fn in enumerate(fns):
            op = self.add(eng, fn, list(reads) + (list(writes) if i > 0 else []), writes)
        return op

    def dma(self, q, fn, reads=(), writes=(), extra=True):
        return self.add(q, fn, reads, writes, dma=True, extra=extra)

    def emit(self, nc, stack):
        ops = self.ops
        for op in ops:
            for j in op.deps:
                if not ops[j].is_dma:
                    ops[j].signal = True
        cnt = {e: 0 for e in COMPUTE}
        for op in ops:
            if not op.is_dma and op.signal:
                cnt[op.eng] += 1
                op.seq = cnt[op.eng]
        sem = {e: stack.enter_context(nc.semaphore("s_" + e)) for e in COMPUTE}
        dsem = {q: [stack.enter_context(nc.semaphore("d_%s%d" % (q, i))) for i in range(n)]
                for q, n in NDMA_SLOTS.items()}
        block = stack.enter_context(nc.Block())
        engs = {"pe": block.tensor, "act": block.scalar, "dve": block.vector,
                "pool": block.gpsimd, "sp": block.sync}

        def make(ename):
            mine = [op for op in ops if op.eng == ename]

            def body(e):
                waited = {}

                def wait(s, v):
                    k = id(s)
                    if waited.get(k, 0) >= v:
                        return
                    waited[k] = v
                    e.wait_ge(s, v)

                for op in mine:
                    for j in sorted(op.deps):
                        d = ops[j]
                        if d.is_dma:
                            wait(dsem[d.eng][d.slot], d.val)
                        else:
                            wait(sem[d.eng], d.seq)
                    if op.is_dma:
                        if op.prev_val > 0:
                            wait(dsem[op.eng][op.slot], op.prev_val)
                        ins = op.fn(e)
                        ins.then_inc(dsem[op.eng][op.slot], 16)
                    else:
                        ins = op.fn(e)
                        if op.signal:
                            ins.then_inc(sem[op.eng], 1)
                if ename in NDMA_SLOTS:
                    for s, v in enumerate(self.dma_slot_val[ename]):
                        if v > 0:
                            wait(dsem[ename][s], v)
            return body

        for ename, deco in engs.items():
            deco(make(ename))


class Rot:
    def __init__(self, items):
        self.items = list(items)
        self.i = 0

    def next(self):
        x = self.items[self.i % len(self.items)]
        self.i += 1
        return x


DBG = {}


def build_program(stop_after=None):
    if stop_after and stop_after.startswith("nokv:"):
        DBG["nokv"] = True
        stop_after = stop_after[5:]
    if stop_after and stop_after.startswith("nodma:"):
        DBG["nodma"] = True
        stop_after = stop_after[6:]
    if stop_after and stop_after.startswith("sb:"):
        _, hh, nn = stop_after.split(":")
        DBG["heads"] = int(hh)
        DBG["nkb"] = int(nn)
        stop_after = "sb"
    nc = bass.Bass("TRN2", target_bir_lowering=False, dynamic_dma_scratch_size=8192)
    P = Prog()
    sb_used = [8192]
    st = contextlib.ExitStack()

    def din(name, shape):
        return nc.dram_tensor(name, list(shape), F32, kind="ExternalInput").ap()

    def dout(name, shape):
        return nc.dram_tensor(name, list(shape), F32, kind="ExternalOutput").ap()

    xh = din("xh", [NH_TILES * 128, D])
    xo = din("xo", [NO_TILES * 128, D])
    xs = din("xs", [2, 32, D])
    memp = din("memp", [256, D])
    ck = din("ck", [2, 8, 2048, 64])
    cv = din("cv", [2, 8, 2048, 64])
    sr = din("sr", [2, 4, 128, 128])
    scv = din("scv", [2, 2, 2 * DFF])
    cmk = din("cmk", [2, 4, 256, 256])
    cmv = din("cmv", [2, 4, 256, 256])
    w_in = din("w_in", [D, 3584])
    w_o = din("w_o", [D, D])
    w_q = din("w_q", [D, D])
    w_k = din("w_k", [D, D])
    w_v = din("w_v", [D, D])
    w_om = din("w_om", [D, D])
    w_up = din("w_up", [D, 2 * DFF])
    w_dn = din("w_dn", [DFF, D])
    lnp = din("lnp", [6, D])
    convp = din("convp", [4, 2 * DFF])
    flag = din("flag", [128, 1])
    ropeh = din("ropeh", [NH_TILES, 128, 256])
    ropeo = din("ropeo", [NO_TILES, 128, 256])
    ropes = din("ropes", [32, 256])
    cb16 = din("cb16", [128, 1536])
    cf32 = din("cf32", [128, 8])

    y_o = dout("y_o", [2048, D])
    y_s = dout("y_s", [2, 32, D])
    k_o = dout("k_o", [8, 2048, 64])
    v_o = dout("v_o", [8, 2048, 64])
    k_s = dout("k_s", [2, 8, 32, 64])
    v_s = dout("v_s", [2, 8, 32, 64])
    sret_o = dout("sret_o", [4, 128, 128])
    sret_s = dout("sret_s", [2, 4, 128, 128])
    conv_o = dout("conv_o", [2, 2 * DFF])
    conv_s = dout("conv_s", [2, 2, 2 * DFF])
    mk_o = dout("mk_o", [4, 256, 256])
    mv_o = dout("mv_o", [4, 256, 256])

    def T(name, shape, dt):
        n = 1
        for d_ in shape[1:]:
            n *= d_
        sb_used[0] += (n * (4 if dt == F32 else 2) + 31) // 32 * 32
        assert sb_used[0] <= 196608, ("SBUF over 192 KiB/partition", name, sb_used[0])
        return st.enter_context(nc.sbuf_tensor(name, list(shape), dt))

    KT = T("KT", [128, 4, 4096], BF16)
    VS = T("VS", [128, 32, 512], BF16)
    kt_b = [Buf("kt%d" % i) for i in range(32)]
    vs_b = [Buf("vs%d" % i) for i in range(32)]
    WS = [T("WS%d" % i, [128, 8, 512], BF16) for i in range(4)]
    ws_b = [Buf("ws%d" % i) for i in range(4)]
    ws_rot = Rot(range(4))
    X = T("X", [128, 5, D], F32)
    x_b = [Buf("x%d" % i) for i in range(5)]
    XT = T("XT", [128, 8, 640], BF16)
    xt_b = [Buf("xt%d" % i) for i in range(5)]
    XB = [T("XB%d" % i, [128, D], BF16) for i in range(2)]
    xb_b = [Buf("xb%d" % i) for i in range(2)]
    xb_rot = Rot(range(2))
    mkt_b = Buf("mkt")
    mv_b = Buf("mv")
    lng_b = Buf("lng")
    lnb_b = Buf("lnb")
    LN = {}
    MM = {}
    mko_b = Buf("mk_o_dram")
    ROPE = [T("ROPE%d" % i, [128, 256], F32) for i in range(2)]
    rope_b = [Buf("rope%d" % i) for i in range(2)]
    rope_rot = Rot(range(2))
    C16 = T("C16", [128, 1536], BF16)
    CF = T("CF", [128, 8], F32)
    c_b = Buf("consts")
    IDENT = C16[:, 0:128]
    NEGU = C16[:, 128:256]
    NEGO = C16[:, 256:384]
    NEGM = C16[:, 384:1024]
    M01 = C16[:, 1024:1536]
    GAM = [1.0 - 2.0 ** (-5.0 - h) for h in range(4)]
    GL128 = [g ** 128 for g in GAM]
    GL32 = [g ** 32 for g in GAM]
    KD = CF[:, 0:4]
    EPSC = CF[:, 4:8]
    CONVP = T("CONVP", [128, 4, 44], F32)
    FLAG = T("FLAG", [128, 1], F32)
    S = T("S", [128, 512], F32)
    S16 = T("S16", [128, 512], BF16)
    s_b = Buf("S")
    s16_b = Buf("S16")
    USTATE = T("USTATE", [128, 44, 2], F32)
    us_b = Buf("ustate")
    SMALL = T("SMALL", [128, 4 * 64 + 8], F32)
    ARENA_BYTES = 41 * 1024
    ARENA = T("ARENA", [128, ARENA_BYTES // 2], BF16)
    arena_b = Buf("arena")
    P.extra_reads = [arena_b]

    PS = st.enter_context(nc.psum_tensor("PS", [128, 8, 512], F32))
    PSB = PS.bitcast(BF16)
    ps_b = [Buf("ps%d" % i) for i in range(8)]

    class Arena:
        def __init__(self):
            self.off = 0

        def reset(self):
            self.off = 0

        def take(self, shape, dt):
            n = 1
            for s in shape[1:]:
                n *= s
            nb = n * (4 if dt == F32 else 2)
            nb = (nb + 63) // 64 * 64
            a = ARENA[:, self.off // 2:(self.off + nb) // 2]
            self.off += nb
            assert self.off <= ARENA_BYTES, ("arena overflow", self.off)
            if dt == F32:
                a = a.bitcast(F32)
                a = a[:, 0:n]
            else:
                a = a[:, 0:n]
            if len(shape) == 3:
                a = a.rearrange("p (a b) -> p a b", a=shape[1])
            return a

    AR = Arena()

    def fence():
        j = SMALL[:, 256:257]
        P.add("dve", lambda e: e.memset(j, 0.0), reads=[], writes=[arena_b], extra=False)

    WBF = {}
    wconv_b = {}
    WSPEC = {"w_k": (w_k, D, D), "w_v": (w_v, D, D), "w_in": (w_in, D, 3584), "w_o": (w_o, D, D),
             "w_q": (w_q, D, D), "w_om": (w_om, D, D), "w_up": (w_up, D, 2 * DFF), "w_dn": (w_dn, DFF, D)}
    for wname, (wap, rows, cols) in WSPEC.items():
        sc = nc.dram_tensor(wname + "_bf", [rows, cols], BF16, kind="Internal").ap()
        WBF[id(wap)] = (wname, sc)
        wconv_b[wname] = []

    conv_done = set()
    conv_pending = []
    conv_sched = {0: ["w_in", "w_o", "w_q", "w_om"], 1: ["w_up", "w_dn"]}
    hook_cnt = [0]

    def conv_hook():
        hook_cnt[0] += 1
        if hook_cnt[0] % 3 == 0:
            conv_step(1)

    def conv_pieces(names):
        for wname in names:
            wap, rows, cols = WSPEC[wname]
            sc = WBF[id(wap)][1]
            step = 128
            for r0 in range(0, rows, step):
                r1 = min(rows, r0 + step)
                b = Buf("conv_%s_%d" % (wname, r0))
                wconv_b[wname].append(b)

                def thunk(sc=sc, wap=wap, r0=r0, r1=r1, b=b):
                    P.dma("pool", lambda e: e.dma_start(out=sc[r0:r1, :], in_=wap[r0:r1, :]), writes=[b], extra=False)
                conv_pending.append(thunk)

    def conv_step(n=1):
        for _ in range(n):
            if conv_pending:
                conv_pending.pop(0)()

    def wload(src_view, nk=8, ncols=512):
        view, wname, fview = src_view
        i = ws_rot.next()
        dst = WS[i][:, 0:nk, 0:ncols]
        if wname not in conv_done:
            P.dma("pool", lambda e: e.dma_start(out=dst, in_=fview), writes=[ws_b[i]], extra=False)
        else:
            P.dma("pool", lambda e: e.dma_start(out=dst, in_=view), reads=wconv_b[wname], writes=[ws_b[i]],
                  extra=False)
        return i

    def wview(w, cb, nk=8, k0=0, ncols=512):
        wname, sc = WBF[id(w)]
        return (sc.rearrange("(kc p) n -> p kc n", p=128)[:, k0:k0 + nk, cb:cb + ncols], wname,
                w.rearrange("(kc p) n -> p kc n", p=128)[:, k0:k0 + nk, cb:cb + ncols])

    def transposes(e, bank, src, ntok, nblk, src_step=128):
        ins = None
        for i in range(nblk):
            ins = e.transpose(PSB[:, bank, i * 128:i * 128 + ntok],
                              src[0:ntok, i * src_step:i * src_step + 128], IDENT[0:ntok, 0:ntok])
        return ins

    tp_rot = Rot([2, 3])
    pj_rot = Rot([0, 1])

    def to_xt(xb_i, t, ntok, col0):
        bk = tp_rot.next()
        src = XB[xb_i]
        P.pe(lambda e: transposes(e, bk, src, ntok, 8), reads=[xb_b[xb_i], c_b], writes=[ps_b[bk]])
        dst = XT[:, :, col0:col0 + ntok]
        srcp = PSB[:, bk, :].rearrange("p (a b) -> p a b", a=8)[:, :, 0:ntok]
        P.act(lambda e: e.activation(out=dst, in_=srcp, func=AF.Identity), reads=[ps_b[bk]], writes=[xt_b[t]])

    def load_ln(idx):
        LNG = AR.take([128, D], F32)
        LNB = AR.take([128, D], F32)
        LN["g"], LN["b"] = LNG, LNB
        g = lnp[idx:idx + 1, :].to_broadcast([128, D])
        b = lnp[idx + 1:idx + 2, :].to_broadcast([128, D])
        P.dma("sp", lambda e: e.dma_start(out=LNG, in_=g), writes=[lng_b])
        P.dma("sp", lambda e: e.dma_start(out=LNB, in_=b), writes=[lnb_b])

    def resid_add(t, ntok, cbk, bank):
        xs_ = X[0:ntok, t, cbk * 512:(cbk + 1) * 512]
        P.dve(lambda e: e.scalar_tensor_tensor(out=xs_, in0=xs_, scalar=ALPHA, in1=PS[0:ntok, bank, :],
                                               op0=ALU.mult, op1=ALU.add),
              reads=[ps_b[bank], x_b[t]], writes=[x_b[t]])

    sm_bufs = [Buf("small%d" % i) for i in range(4)]
    sm_rot = Rot(range(4))

    def small_next():
        r = sm_rot.next()
        return r * 64, sm_bufs[r]

    def layer_norm(t, ntok, want_xb=True):
        LNG, LNB = LN["g"], LN["b"]
        so, sm_b = small_next()
        xt_ = X[0:ntok, t, :]
        stats = SMALL[0:ntok, so + 0:so + 12]
        mv = SMALL[0:ntok, so + 12:so + 14]
        rstd = SMALL[0:ntok, so + 14:so + 15]

        def f_stats(e):
            e.bn_stats(out=SMALL[0:ntok, so + 0:so + 6], in_=X[0:ntok, t, 0:512])
            return e.bn_stats(out=SMALL[0:ntok, so + 6:so + 12], in_=X[0:ntok, t, 512:1024])
        P.chain("dve", [f_stats, lambda e: e.bn_aggr(out=mv, in_=stats)], reads=[x_b[t]], writes=[sm_b])
        P.chain("act", [lambda e: e.activation(out=rstd, in_=SMALL[0:ntok, so + 13:so + 14], func=AF.Ln,
                                               bias=EPSLN[0:ntok, :], scale=1.0),
                        lambda e: e.activation(out=rstd, in_=rstd, func=AF.Exp, scale=-0.5)],
                reads=[sm_b, c_b], writes=[sm_b])
        P.chain("dve", [lambda e: e.scalar_tensor_tensor(out=xt_, in0=xt_, scalar=SMALL[0:ntok, so + 12:so + 13],
                                                         in1=LNG[0:ntok, :], op0=ALU.subtract, op1=ALU.mult),
                        lambda e: e.scalar_tensor_tensor(out=xt_, in0=xt_, scalar=rstd, in1=LNB[0:ntok, :],
                                                         op0=ALU.mult, op1=ALU.add)],
                reads=[sm_b, x_b[t], lng_b, lnb_b], writes=[x_b[t]])
        if want_xb:
            i = xb_rot.next()
            P.dve(lambda e: e.tensor_copy(out=XB[i][0:ntok, :], in_=xt_), reads=[x_b[t]], writes=[xb_b[i]])
            return i
        return None

    EPSLN = T("EPSLN", [128, 1], F32)

    P.dma("pool", lambda e: e.dma_start(out=C16[:, :], in_=cb16), writes=[c_b], extra=False)
    P.dma("sp", lambda e: e.dma_start(out=CF[:, :], in_=cf32), writes=[c_b], extra=False)
    P.dma("sp", lambda e: e.dma_start(out=FLAG[:, :], in_=flag), writes=[c_b], extra=False)
    with nc.allow_non_contiguous_dma(reason="tiny per-partition conv parameter gather"):
        for r in range(4):
            P.dma("sp", lambda e, r=r: e.dma_start(out=CONVP[:, r, :],
                                                   in_=convp[r, :].rearrange("(c p) -> p c", p=128),
                                                   allow_slow_non_contiguous=True),
                  writes=[c_b], extra=False)
    P.dve(lambda e: e.memset(EPSLN[:, :], LN_EPS), writes=[c_b])
    P.dve(lambda e: e.memset(USTATE[:, :, :], 0.0), writes=[us_b])
    P.dve(lambda e: e.memset(S[:, :], 0.0), writes=[s_b])
    P.dve(lambda e: e.memset(S16[:, :], 0.0), writes=[s16_b])

    def build_mkt(mk_tm, mk_buf):
        for mc in range(2):
            bk = tp_rot.next()
            src = mk_tm[:, mc, :]
            P.pe(lambda e, src=src, bk=bk: transposes(e, bk, src, 128, 8), reads=[mk_buf, c_b], writes=[ps_b[bk]])
            dst = MM["MKT"][:, :, mc * 128:(mc + 1) * 128]
            srcp = PSB[:, bk, :].rearrange("p (a b) -> p a b", a=8)
            P.dve(lambda e, dst=dst, srcp=srcp: e.tensor_copy(out=dst, in_=srcp), reads=[ps_b[bk]], writes=[mkt_b])

    def mem_phase_prompt():
        fence()
        AR.reset()
        MEMB = AR.take([128, 2, D], BF16)
        MEMT = AR.take([128, 8, 256], BF16)
        STG = AR.take([128, D], F32)
        memb_b, memt_b, stg_b = Buf("memb"), Buf("memt"), Buf("stg")
        P.dma("pool", lambda e: e.dma_start(out=MEMB, in_=memp.rearrange("(c p) n -> p c n", p=128)),
              writes=[memb_b])
        for mc in range(2):
            bk = tp_rot.next()
            src = MEMB[:, mc, :]
            P.pe(lambda e, src=src, bk=bk: transposes(e, bk, src, 128, 8), reads=[memb_b, c_b], writes=[ps_b[bk]])
            dst = MEMT[:, :, mc * 128:(mc + 1) * 128]
            srcp = PSB[:, bk, :].rearrange("p (a b) -> p a b", a=8)
            P.dve(lambda e, dst=dst, srcp=srcp: e.tensor_copy(out=dst, in_=srcp), reads=[ps_b[bk]], writes=[memt_b])
        for wi, (w, outd) in enumerate(((w_k, mk_o), (w_v, mv_o))):
            for cbk in range(2):
                si = wload(wview(w, cbk * 512))
                for mc in range(2):
                    bk = pj_rot.next()

                    def f(e, si=si, mc=mc, bk=bk):
                        ins = None
                        for kc in range(8):
                            ins = e.matmul(PS[:, bk, :], MEMT[:, kc, mc * 128:(mc + 1) * 128], WS[si][:, kc, :],
                                           start=(kc == 0), stop=(kc == 7))
                        return ins
                    P.pe(f, reads=[memt_b, ws_b[si]], writes=[ps_b[bk]])
                    sg_ = STG[:, cbk * 512:(cbk + 1) * 512]
                    P.act(lambda e, sg_=sg_, bk=bk: e.activation(out=sg_, in_=PS[:, bk, :], func=AF.Identity),
                          reads=[ps_b[bk]], writes=[stg_b])
                    dsto = outd[cbk * 2:cbk * 2 + 2, mc * 128:(mc + 1) * 128, :].rearrange("h p d -> p h d")
                    srco = sg_.rearrange("p (h d) -> p h d", h=2)
                    P.dma("sp", lambda e, dsto=dsto, srco=srco: e.dma_start(out=dsto, in_=srco), reads=[stg_b],
                          writes=[mko_b])

    def mem_load(srck, srcv, MKTM, mktm_b, rd):
        MV = MM["MV"]
        for mc in range(2):
            P.dma("pool", lambda e, mc=mc: e.dma_start(
                out=MKTM[:, mc, :].rearrange("p (h d) -> p h d", h=4),
                in_=srck[:, mc * 128:(mc + 1) * 128, :].rearrange("h p d -> p h d")), reads=rd, writes=[mktm_b])
            P.dma("pool", lambda e, mc=mc: e.dma_start(
                out=MV[:, mc, :].rearrange("p (h d) -> p h d", h=4),
                in_=srcv[:, mc * 128:(mc + 1) * 128, :].rearrange("h p d -> p h d")), reads=rd, writes=[mv_b])
        build_mkt(MKTM, mktm_b)

    def proj_tok(si, t, ntok, col0):
        bk = pj_rot.next()

        def f(e):
            ins = None
            for kc in range(8):
                ins = e.matmul(PS[0:ntok, bk, :], XT[:, kc, col0:col0 + ntok], WS[si][:, kc, :],
                               start=(kc == 0), stop=(kc == 7))
            return ins
        P.pe(f, reads=[xt_b[t], ws_b[si]], writes=[ps_b[bk]])
        return bk

    def rope(bk, ntok, ri, dst_bf, dst_buf, T1, T2, t12_b, kscale):
        ps4 = PS[0:ntok, bk, :].rearrange("p (h c d) -> p h c d", h=4, c=2)
        c2 = ROPE[ri][0:ntok, 0:128].unsqueeze(1).to_broadcast([ntok, 4, 128])
        s_lo = ROPE[ri][0:ntok, 128:192].unsqueeze(1).to_broadcast([ntok, 4, 64])
        s_hi = ROPE[ri][0:ntok, 192:256].unsqueeze(1).to_broadcast([ntok, 4, 64])
        t1 = T1[0:ntok, :].rearrange("p (h d) -> p h d", h=4)
        t2 = T2[0:ntok, :].rearrange("p (h c d) -> p h c d", h=4, c=2)

        def f(e):
            e.tensor_tensor(out=t1, in0=PS[0:ntok, bk, :].rearrange("p (h d) -> p h d", h=4), in1=c2, op=ALU.mult)
            e.tensor_tensor(out=t2[:, :, 0, :], in0=ps4[:, :, 1, :], in1=s_lo, op=ALU.mult)
            return e.tensor_tensor(out=t2[:, :, 1, :], in0=ps4[:, :, 0, :], in1=s_hi, op=ALU.mult)
        P.dve(f, reads=[ps_b[bk], rope_b[ri]], writes=[t12_b])
        if not kscale:
            P.dve(lambda e: e.tensor_tensor(out=dst_bf, in0=T1[0:ntok, :], in1=T2[0:ntok, :], op=ALU.add),
                  reads=[t12_b], writes=[dst_buf])
        else:
            def g(e):
                ins = None
                for h in range(4):
                    ins = e.tensor_scalar(out=dst_bf[:, h * 128:(h + 1) * 128], in0=T1[0:ntok, h * 128:(h + 1) * 128],
                                          scalar1=KD[0:ntok, h:h + 1], scalar2=None, op0=ALU.mult)
                return ins
            P.chain("dve", [lambda e: e.tensor_tensor(out=T1[0:ntok, :], in0=T1[0:ntok, :], in1=T2[0:ntok, :],
                                                      op=ALU.add), g],
                    reads=[t12_b, c_b], writes=[dst_buf, t12_b])

    def ret_state_update(kbf, vbf, ntok, kv_bufs, GL):
        bk = 6

        def f(e):
            ins = None
            for h in range(4):
                ins = e.matmul(PS[:, bk, h * 128:(h + 1) * 128], kbf[0:ntok, h * 128:(h + 1) * 128],
                               vbf[0:ntok, h * 128:(h + 1) * 128], start=True, stop=True)
            return ins
        P.pe(f, reads=kv_bufs, writes=[ps_b[bk]])

        def g(e):
            ins = None
            for h in range(4):
                ins = e.tensor_scalar(out=S[:, h * 128:(h + 1) * 128], in0=S[:, h * 128:(h + 1) * 128],
                                      scalar1=float(GL[h]), scalar2=None, op0=ALU.mult)
            return ins
        P.chain("dve", [lambda e: e.tensor_tensor(out=S[:, :], in0=S[:, :], in1=PS[:, bk, :], op=ALU.add), g],
                reads=[ps_b[bk], s_b, c_b], writes=[s_b])
        P.dve(lambda e: e.tensor_copy(out=S16[:, :], in_=S[:, :]), reads=[s_b], writes=[s16_b])

    def history_phase():
        for g0 in range(0, NH_TILES, 5):
            tiles = list(range(g0, min(g0 + 5, NH_TILES)))
            fence()
            AR.reset()
            KBF = AR.take([128, 5, 512], BF16)
            VBF = AR.take([128, 5, 512], BF16)
            T1s = [AR.take([128, 512], F32) for _ in range(2)]
            T2s = [AR.take([128, 512], F32) for _ in range(2)]
            SKBs = [AR.take([128, 512], BF16) for _ in range(2)]
            kbf_b = [Buf("kbf%d" % i) for i in range(5)]
            vbf_b = [Buf("vbf%d" % i) for i in range(5)]
            t12s_b, skbs_b = [Buf("t12a"), Buf("t12b")], [Buf("skba"), Buf("skbb")]
            ropei = {}
            for li, t in enumerate(tiles):
                i = xb_rot.next()
                P.dma("pool", lambda e, i=i, t=t: e.dma_start(out=XB[i][:, :], in_=xh[t * 128:(t + 1) * 128, :]),
                      writes=[xb_b[i]], extra=False)
                to_xt(i, li, 128, li * 128)
            for cbk in (1, 2, 5, 6):
                si = wload(wview(w_in, cbk * 512))
                bks = {0: proj_tok(si, 0, 128, 0)}
                for li, t in enumerate(tiles):
                    if li + 1 < len(tiles):
                        bks[li + 1] = proj_tok(si, li + 1, 128, (li + 1) * 128)
                    bk = bks[li]
                    T1, T2, t12_b = T1s[li % 2], T2s[li % 2], t12s_b[li % 2]
                    SKB, skb_b = SKBs[li % 2], skbs_b[li % 2]
                    if cbk == 1:
                        ri = rope_rot.next()
                        P.dma("sp", lambda e, ri=ri, t=t: e.dma_start(out=ROPE[ri][:, :], in_=ropeh[t]),
                              writes=[rope_b[ri]], extra=False)
                        rope(bk, 128, ri, KBF[:, li, :], kbf_b[li], T1, T2, t12_b, True)
                    elif cbk == 2:
                        P.act(lambda e, li=li, bk=bk: e.activation(out=VBF[:, li, :], in_=PS[:, bk, :], func=AF.Identity),
                              reads=[ps_b[bk]], writes=[vbf_b[li]])
                        ret_state_update(KBF[:, li, :], VBF[:, li, :], 128, [kbf_b[li], vbf_b[li]], GL128)
                    elif cbk == 5:
                        P.dve(lambda e, bk=bk, SKB=SKB: e.tensor_copy(out=SKB[:, :], in_=PS[:, bk, :]),
                              reads=[ps_b[bk]], writes=[skb_b])
                        tb = tp_rot.next()
                        P.pe(lambda e, tb=tb, SKB=SKB: transposes(e, tb, SKB, 128, 4), reads=[skb_b, c_b],
                             writes=[ps_b[tb]])
                        dst = KT[:, :, t * 128:(t + 1) * 128]
                        srcp = PSB[:, tb, 0:512].rearrange("p (a b) -> p a b", a=4)
                        P.act(lambda e, dst=dst, srcp=srcp: e.activation(out=dst, in_=srcp, func=AF.Identity),
                              reads=[ps_b[tb]], writes=[kt_b[t]])
                    else:
                        P.act(lambda e, t=t, bk=bk: e.activation(out=VS[:, t, :], in_=PS[:, bk, :], func=AF.Identity),
                              reads=[ps_b[bk]], writes=[vs_b[t]])

    def sb_attention(SQT, sqt_bufs, qcol0, nq, keyblocks, SOT, sot_buf, tmp, hook=None):
        EXPZ, SP, AT, R = tmp["EXPZ"], tmp["SP"], tmp["AT"], tmp["R"]
        W = nq
        wide = (W == 512)
        gsz = 2 if wide else max(1, 512 // W)
        z_rot = Rot([2, 4, 6])
        o_rot = Rot([0, 1])
        kbl = list(reversed(keyblocks))
        if "nkb" in DBG:
            kbl = kbl[:DBG["nkb"]]
        groups_ = [kbl[i:i + gsz] for i in range(0, len(kbl), gsz)]
        items = []
        for h in range(DBG.get("heads", 8)):
            ob = o_rot.next()
            for ii, blks in enumerate(groups_):
                items.append({"h": h, "c": h // 2, "hp": h % 2, "ob": ob, "blks": blks, "first": ii == 0,
                              "last": ii == len(groups_) - 1, "xi": len(items) % 2})

        def zsl(zb, b, lo=0, hi=None):
            hi = W if hi is None else hi
            return PS[:, zb + b, lo:hi] if wide else PS[:, zb, b * W + lo:b * W + hi]

        def zall(zb, g):
            if wide:
                return PS[:, zb, :] if g == 1 else PS[:, zb:zb + g, :]
            return PS[:, zb, 0:g * W]

        def ball(buf, g):
            if wide and g > 1:
                return buf[:, 0:g * W].rearrange("p (g w) -> p g w", g=g)
            return buf[:, 0:g * W]

        def zbufs(zb, g):
            return [ps_b[zb + b] for b in range(g)] if wide else [ps_b[zb]]

        def stage1(it):
            blks, c, hp, xi = it["blks"], it["c"], it["hp"], it["xi"]
            g = len(blks)
            zb = z_rot.next()
            it["zb"] = zb
            qap = SQT[hp][:, c, qcol0:qcol0 + nq]

            def f_qk(e):
                ins = None
                for b, kb in enumerate(blks):
                    msk = kb["mask"]
                    st_ = True if wide else (b == 0)
                    ins = e.matmul(zsl(zb, b), kb["kt"](c, hp), qap, start=st_,
                                   stop=((wide or b == g - 1) and msk is None), skip_group_check=True)
                    if msk is not None:
                        ncol, moff = msk
                        ins = e.matmul(zsl(zb, b, 0, ncol), IDENT, NEGM[:, moff:moff + ncol],
                                       start=False, stop=(wide or b == g - 1), skip_group_check=True)
                return ins
            bufs = [x for kb in blks for x in kb["bufs"]]
            P.pe(f_qk, reads=bufs + sqt_bufs + [c_b], writes=zbufs(zb, g))
            P.chain("act", [lambda e: e.activation(out=ball(EXPZ[xi], g), in_=zall(zb, g),
                                                   func=AF.Exp, scale=0.125),
                            lambda e: e.activation(out=SP[xi][:, 0:g * W], in_=EXPZ[xi][:, 0:g * W], func=AF.Ln,
                                                   bias=ONEC[:, :], scale=1.0)],
                    reads=zbufs(zb, g) + [c_b], writes=[tmp["sp_b"][xi], tmp["ez_b"]])

        def stage2(it):
            blks, c, hp, xi, zb, ob = it["blks"], it["c"], it["hp"], it["xi"], it["zb"], it["ob"]
            g, first, last = len(blks), it["first"], it["last"]

            def f_cum(e):
                ins = None
                if wide:
                    for b in range(g):
                        ins = e.matmul(zsl(zb, b), NEGU8[:, :], SP[xi][:, b * W:(b + 1) * W], start=False,
                                       stop=False, skip_group_check=True)
                        if not first:
                            ins = e.matmul(zsl(zb, b), NEGO8[:, :], R[:, 0:W], start=False, stop=False,
                                           skip_group_check=True)
                        for j in range(b):
                            ins = e.matmul(zsl(zb, b), NEGO8[:, :], SP[xi][:, j * W:(j + 1) * W], start=False,
                                           stop=False, skip_group_check=True)
                    return ins
                nmm = 1 + (0 if first else 1) + (g - 1)
                k = 0
                ins = e.matmul(PS[:, zb, 0:g * W], NEGU8[:, :], SP[xi][:, 0:g * W], start=False, stop=(k == nmm - 1),
                               skip_group_check=True)
                k += 1
                if not first:
                    rb = R[:, 0:W] if g == 1 else R[:, 0:W].unsqueeze(1).to_broadcast([128, g, W])
                    ob_ = PS[:, zb, 0:g * W] if g == 1 else PS[:, zb, 0:g * W].rearrange("p (g w) -> p g w", g=g)
                    ins = e.matmul(ob_, NEGO8[:, :], rb, start=False, stop=(k == nmm - 1), skip_group_check=True)
                    k += 1
                for j in range(g - 1):
                    r = g - 1 - j
                    sb_ = SP[xi][:, j * W:(j + 1) * W]
                    o2 = PS[:, zb, (j + 1) * W:g * W]
                    if r > 1:
                        sb_ = sb_.unsqueeze(1).to_broadcast([128, r, W])
                        o2 = o2.rearrange("p (g w) -> p g w", g=r)
                    ins = e.matmul(o2, NEGO8[:, :], sb_, start=False, stop=(k == nmm - 1), skip_group_check=True)
                    k += 1
                return ins
            P.pe(f_cum, reads=[tmp["sp_b"][xi], tmp["r_b"], c_b], writes=zbufs(zb, g))
            P.act(lambda e: e.activation(out=ball(AT[xi], g), in_=zall(zb, g), func=AF.Exp, scale=0.125),
                  reads=zbufs(zb, g), writes=[tmp["at_b"][xi]])
            if not last:
                fr = []
                if wide:
                    if first:
                        if g == 1:
                            fr.append(lambda e: e.tensor_copy(out=R[:, 0:W], in_=SP[xi][:, 0:W]))
                        else:
                            fr.append(lambda e: e.tensor_tensor(out=R[:, 0:W], in0=SP[xi][:, 0:W],
                                                                in1=SP[xi][:, W:2 * W], op=ALU.add))
                    else:
                        for b in range(g):
                            fr.append(lambda e, b=b: e.tensor_tensor(out=R[:, 0:W], in0=R[:, 0:W],
                                                                     in1=SP[xi][:, b * W:(b + 1) * W], op=ALU.add))
                elif g == 1:
                    if first:
                        fr.append(lambda e: e.tensor_copy(out=R[:, 0:W], in_=SP[xi][:, 0:W]))
                    else:
                        fr.append(lambda e: e.tensor_tensor(out=R[:, 0:W], in0=R[:, 0:W], in1=SP[xi][:, 0:W],
                                                            op=ALU.add))
                else:
                    spv = SP[xi][:, 0:g * W].rearrange("p (g w) -> p w g", g=g)
                    RT = R[:, 256:512].bitcast(F32)
                    fr.append(lambda e: e.tensor_reduce(out=RT[:, 0:W], in_=spv, axis=AX.X, op=ALU.add))
                    if first:
                        fr.append(lambda e: e.tensor_copy(out=R[:, 0:W], in_=RT[:, 0:W]))
                    else:
                        fr.append(lambda e: e.tensor_tensor(out=R[:, 0:W], in0=R[:, 0:W], in1=RT[:, 0:W],
                                                            op=ALU.add))
                P.chain("dve", fr, reads=[tmp["sp_b"][xi]], writes=[tmp["r_b"]])

        def stage3(it):
            blks, c, hp, xi, ob = it["blks"], it["c"], it["hp"], it["xi"], it["ob"]
            g, first, last = len(blks), it["first"], it["last"]

            def f_av(e):
                ins = None
                for b, kb in enumerate(blks):
                    ins = e.matmul(PS[:, ob, 0:W], kb["v"](c), AT[xi][:, b * W:(b + 1) * W],
                                   start=(first and b == 0), stop=(last and b == g - 1))
                return ins
            bufs = [x for kb in blks for x in kb["bufs"]]
            P.pe(f_av, reads=bufs + [tmp["at_b"][xi]], writes=[ps_b[ob]])
            if last:
                dst = SOT[hp * 64:(hp + 1) * 64, c, qcol0:qcol0 + nq]
                P.dve(lambda e: e.tensor_copy(out=dst, in_=PS[hp * 64:(hp + 1) * 64, ob, 0:nq]),
                      reads=[ps_b[ob]], writes=[sot_buf])

        n_it = len(items)
        for k in range(-2, n_it):
            if hook is not None:
                hook()
            if 0 <= k + 2 < n_it:
                stage1(items[k + 2])
            if 0 <= k + 1 < n_it:
                stage2(items[k + 1])
            if 0 <= k:
                stage3(items[k])

    ONEC = T("ONEC", [128, 1], F32)
    P.dve(lambda e: e.memset(ONEC[:, :], 1.0), writes=[c_b])
    NEGU8 = T("NEGU8", [128, 128], BF16)
    NEGO8 = T("NEGO8", [128, 128], BF16)
    P.dve(lambda e: e.tensor_scalar(out=NEGU8[:, :], in0=NEGU, scalar1=8.0, scalar2=None, op0=ALU.mult),
          reads=[c_b], writes=[c_b])
    P.dve(lambda e: e.tensor_scalar(out=NEGO8[:, :], in0=NEGO, scalar1=8.0, scalar2=None, op0=ALU.mult),
          reads=[c_b], writes=[c_b])

    def prefetch_sample_v(part):
        for n in range(part * 8, part * 8 + 8):
            sq_, blk = n // 16, n % 16
            gb = sq_ * 16 + blk
            P.dma("pool", lambda e, sq_=sq_, blk=blk, gb=gb: e.dma_start(
                out=VS[:, gb, :].rearrange("p (h d) -> p h d", h=8),
                in_=cv[sq_, :, blk * 128:(blk + 1) * 128, :].rearrange("h p d -> p h d")),
                writes=[vs_b[gb]], extra=False)

    def main_group(gi, tiles):
        nt = len(tiles)
        ntot = sum(t["ntok"] for t in tiles)
        is_s = tiles[0]["kind"] == "s"
        GL = GL32 if is_s else GL128
        fence()
        AR.reset()
        MIXT = AR.take([128, 8, 640], BF16)
        mixt_b = [Buf("mixt%d" % i) for i in range(nt)]
        sot_b = Buf("sot")
        mark = AR.off
        VBF = AR.take([128, 5, 512], BF16)
        KBF = AR.take([128, 5, 512], BF16)
        QKT = AR.take([128, 8, 640], BF16)
        SG = AR.take([128, 5, 512], BF16)
        T1 = AR.take([128, 512], F32)
        T2 = AR.take([128, 512], F32)
        SCB = AR.take([128, 512], BF16)
        ROB = AR.take([128, 512], BF16)
        QBF = ROB
        vbf_b = [Buf("vbf%d" % i) for i in range(nt)]
        kbf_b = [Buf("kbf%d" % i) for i in range(nt)]
        qkt_b = [Buf("qkt%d" % i) for i in range(nt)]
        sg_b = [Buf("sg%d" % i) for i in range(nt)]
        t12_b, scb_b, rob_b = Buf("t12"), Buf("scb"), Buf("rob")
        qbf_b = rob_b
        ropei = []
        for li, t in enumerate(tiles):
            ntok = t["ntok"]
            nr = t["nreal"]
            i = xb_rot.next()
            if nr < ntok:
                P.dve(lambda e, li=li: e.memset(X[:, li, :], 0.0), writes=[x_b[li]])
                P.dve(lambda e, i=i: e.memset(XB[i][:, :], 0.0), writes=[xb_b[i]])
            P.dma("sp", lambda e, li=li, t=t, nr=nr: e.dma_start(out=X[0:nr, li, :], in_=t["xsrc"]),
                  writes=[x_b[li]], extra=False)
            P.dma("pool", lambda e, i=i, t=t, nr=nr: e.dma_start(out=XB[i][0:nr, :], in_=t["xsrc"]),
                  writes=[xb_b[i]], extra=False)
            to_xt(i, li, ntok, t["col0"])
        for cbk in (1, 2, 0, 3):
            si = wload(wview(w_in, cbk * 512))
            bks = {0: proj_tok(si, 0, tiles[0]["ntok"], tiles[0]["col0"])}
            for li, t in enumerate(tiles):
                ntok, col0 = t["ntok"], t["col0"]
                if li + 1 < nt:
                    bks[li + 1] = proj_tok(si, li + 1, tiles[li + 1]["ntok"], tiles[li + 1]["col0"])
                bk = bks[li]
                if cbk == 1:
                    ri = rope_rot.next()
                    P.dma("sp", lambda e, ri=ri, t=t, ntok=ntok: e.dma_start(out=ROPE[ri][0:t["nreal"], :], in_=t["rope"]),
                          writes=[rope_b[ri]], extra=False)
                    t["ri"] = ri
                    rope(bk, ntok, ri, KBF[0:ntok, li, :], kbf_b[li], T1, T2, t12_b, True)
                    tb = tp_rot.next()
                    P.pe(lambda e, tb=tb, li=li, ntok=ntok: transposes(e, tb, KBF[:, li, :], ntok, 4),
                         reads=[kbf_b[li], c_b], writes=[ps_b[tb]])
                    dst = QKT[:, 4:8, col0:col0 + ntok]
                    srcp = PSB[:, tb, 0:512].rearrange("p (a b) -> p a b", a=4)[:, :, 0:ntok]
                    P.act(lambda e, dst=dst, srcp=srcp: e.activation(out=dst, in_=srcp, func=AF.Identity),
                          reads=[ps_b[tb]], writes=[qkt_b[li]])
                elif cbk == 2:
                    P.act(lambda e, li=li, bk=bk, ntok=ntok: e.activation(out=VBF[0:ntok, li, :], in_=PS[0:ntok, bk, :],
                                                                         func=AF.Identity),
                          reads=[ps_b[bk]], writes=[vbf_b[li]])
                elif cbk == 0:
                    ri = rope_rot.next()
                    P.dma("sp", lambda e, ri=ri, t=t, ntok=ntok: e.dma_start(out=ROPE[ri][0:t["nreal"], :], in_=t["rope"]),
                          writes=[rope_b[ri]], extra=False)
                    rope(bk, ntok, ri, QBF[0:ntok, :], qbf_b, T1, T2, t12_b, False)
                    tb = tp_rot.next()
                    P.pe(lambda e, tb=tb, ntok=ntok: transposes(e, tb, QBF, ntok, 4),
                         reads=[qbf_b, c_b], writes=[ps_b[tb]])
                    dst = QKT[:, 0:4, col0:col0 + ntok]
                    srcp = PSB[:, tb, 0:512].rearrange("p (a b) -> p a b", a=4)[:, :, 0:ntok]
                    P.act(lambda e, dst=dst, srcp=srcp: e.activation(out=dst, in_=srcp, func=AF.Identity),
                          reads=[ps_b[tb]], writes=[qkt_b[li]])
                else:
                    P.act(lambda e, li=li, bk=bk, ntok=ntok: e.activation(out=SG[0:ntok, li, :], in_=PS[0:ntok, bk, :],
                                                                         func=AF.Silu),
                          reads=[ps_b[bk]], writes=[sg_b[li]])
        def _ret_tile(li, t):
            ntok, col0 = t["ntok"], t["col0"]
            so, sm_b = small_next()
            if is_s:
                s = t["seq"]
                P.dma("sp", lambda e, s=s: e.dma_start(out=S[:, :].rearrange("p (h d) -> p h d", h=4),
                                                       in_=sr[s].rearrange("h p d -> p h d")),
                      writes=[s_b], extra=False)
                P.dve(lambda e: e.tensor_copy(out=S16[:, :], in_=S[:, :]), reads=[s_b], writes=[s16_b])

            def f_sc(e, ntok=ntok, col0=col0):
                ins = None
                for h in range(4):
                    ins = e.matmul(PS[0:ntok, 4, h * 128:h * 128 + ntok], QKT[:, 4 + h, col0:col0 + ntok],
                                   QKT[:, h, col0:col0 + ntok], start=True, stop=True)
                return ins
            P.pe(f_sc, reads=[qkt_b[li]], writes=[ps_b[4]])

            def f_msk(e, ntok=ntok):
                a = PS[0:ntok, 4, :].rearrange("p (h t) -> p h t", h=4)[:, :, 0:ntok]
                m = M01[0:ntok, :].rearrange("p (h t) -> p h t", h=4)[:, :, 0:ntok]
                o = SCB[0:ntok, :].rearrange("p (h t) -> p h t", h=4)[:, :, 0:ntok]
                return e.tensor_tensor(out=o, in0=a, in1=m, op=ALU.mult)
            P.dve(f_msk, reads=[ps_b[4], c_b], writes=[scb_b])

            def f_o(e, ntok=ntok, col0=col0, li=li):
                ins = None
                for h in range(4):
                    e.matmul(PS[0:ntok, 5, h * 128:(h + 1) * 128], SCB[0:ntok, h * 128:h * 128 + ntok],
                             VBF[0:ntok, li, h * 128:(h + 1) * 128], start=True, stop=False)
                    ins = e.matmul(PS[0:ntok, 5, h * 128:(h + 1) * 128], QKT[:, h, col0:col0 + ntok],
                                   S16[:, h * 128:(h + 1) * 128], start=False, stop=True)
                return ins
            P.pe(f_o, reads=[scb_b, vbf_b[li], qkt_b[li], s16_b], writes=[ps_b[5]])
            ss = SMALL[0:ntok, so + 16:so + 20]
            rs = SMALL[0:ntok, so + 20:so + 24]

            def f_ss(e, ntok=ntok):
                ins = None
                for h in range(4):
                    ins = e.activation(out=T1[0:ntok, h * 128:(h + 1) * 128], in_=PS[0:ntok, 5, h * 128:(h + 1) * 128],
                                       func=AF.Square, accum_out=SMALL[0:ntok, so + 16 + h:so + 17 + h])
                return ins
            P.act(f_ss, reads=[ps_b[5]], writes=[sm_b, t12_b])
            P.dve(lambda e, ntok=ntok, ss=ss: e.scalar_tensor_tensor(out=ss, in0=ss, scalar=1.0 / 128.0,
                                                                    in1=EPSC[0:ntok, :], op0=ALU.mult, op1=ALU.add),
                  reads=[sm_b, c_b], writes=[sm_b])

            P.chain("act", [lambda e, ss=ss, rs=rs: e.activation(out=rs, in_=ss, func=AF.Ln),
                            lambda e, rs=rs: e.activation(out=rs, in_=rs, func=AF.Exp, scale=-0.5)],
                    reads=[sm_b], writes=[sm_b])

            def f_ro(e, ntok=ntok, li=li):
                ins = None
                for h in range(4):
                    ins = e.scalar_tensor_tensor(out=ROB[0:ntok, h * 128:(h + 1) * 128],
                                                 in0=PS[0:ntok, 5, h * 128:(h + 1) * 128],
                                                 scalar=SMALL[0:ntok, so + 20 + h:so + 21 + h],
                                                 in1=SG[0:ntok, li, h * 128:(h + 1) * 128],
                                                 op0=ALU.mult, op1=ALU.mult)
                return ins
            P.dve(f_ro, reads=[ps_b[5], sm_b, sg_b[li]], writes=[rob_b])
            tb = tp_rot.next()
            P.pe(lambda e, tb=tb, ntok=ntok: transposes(e, tb, ROB, ntok, 4), reads=[rob_b, c_b], writes=[ps_b[tb]])
            dst = MIXT[:, 0:4, col0:col0 + ntok]
            srcp = PSB[:, tb, 0:512].rearrange("p (a b) -> p a b", a=4)[:, :, 0:ntok]
            P.act(lambda e, dst=dst, srcp=srcp: e.activation(out=dst, in_=srcp, func=AF.Identity),
                  reads=[ps_b[tb]], writes=[mixt_b[li]])
            ret_state_update(KBF[:, li, :], VBF[:, li, :], ntok, [kbf_b[li], vbf_b[li]], GL)
            if is_s:
                s = t["seq"]
                P.dma("sp", lambda e, s=s: e.dma_start(out=sret_s[s].rearrange("h p d -> p h d"),
                                                       in_=S[:, :].rearrange("p (h d) -> p h d", h=4)), reads=[s_b])
            elif t.get("last"):
                P.dma("sp", lambda e: e.dma_start(out=sret_o.rearrange("h p d -> p h d"),
                                                  in_=S[:, :].rearrange("p (h d) -> p h d", h=4)), reads=[s_b])
        for li, t in enumerate(tiles):
            _ret_tile(li, t)
        if stop_after == "ret":
            return
        fence()
        AR.off = mark
        SQTE = AR.take([128, 4, 640], BF16)
        SQTO = AR.take([128, 4, 640], BF16)
        SQT = (SQTE, SQTO)
        SQB = AR.take([128, 512], BF16)
        SKB = AR.take([128, 512], BF16)
        STK = AR.take([128, 512], F32)
        STV = AR.take([128, 512], F32)
        zw = 512 if is_s else 1024
        if is_s:
            KTOWN = AR.take([128, 4, 256], BF16)
            VOWN = AR.take([128, 2, 512], BF16)
            KSTG = [AR.take([128, 2, 512], BF16) for _ in range(2)]
        EXPZ = [AR.take([128, zw], F32)] * 2
        SPt = [AR.take([128, zw], BF16) for _ in range(2)]
        ATt = [AR.take([128, zw], BF16) for _ in range(2)]
        Rt = AR.take([128, 512], BF16)
        tmp = {"EXPZ": EXPZ, "SP": SPt, "AT": ATt, "R": Rt,
               "ez_b": Buf("expz"), "sp_b": [Buf("sp0"), Buf("sp1")], "at_b": [Buf("at0"), Buf("at1")], "r_b": Buf("R")}
        sqt_b = [Buf("sqt%d" % i) for i in range(nt)]
        P.dve(lambda e: e.memset(SQTE[:, :, :], 0.0), writes=sqt_b)
        P.dve(lambda e: e.memset(SQTO[:, :, :], 0.0), writes=sqt_b)
        sqb_b, skb_b, stk_b, stv_b = Buf("sqb"), Buf("skb"), Buf("stk"), Buf("stv")
        ktown_b = [Buf("ktown0"), Buf("ktown1")]
        vown_b = [Buf("vown0"), Buf("vown1")]
        kstg_b = [Buf("kstg0"), Buf("kstg1")]
        for cbk in (6, 5, 4):
            si = wload(wview(w_in, cbk * 512))
            bks = {0: proj_tok(si, 0, tiles[0]["ntok"], tiles[0]["col0"])}
            for li, t in enumerate(tiles):
                ntok, col0 = t["ntok"], t["col0"]
                if li + 1 < nt:
                    bks[li + 1] = proj_tok(si, li + 1, tiles[li + 1]["ntok"], tiles[li + 1]["col0"])
                bk = bks[li]
                if cbk == 6:
                    if is_s:
                        s = t["seq"]
                        P.dve(lambda e, s=s, bk=bk, ntok=ntok: e.tensor_copy(out=VOWN[0:ntok, s, :], in_=PS[0:ntok, bk, :]),
                              reads=[ps_b[bk]], writes=[vown_b[s]])
                    else:
                        blk = t["blk"]
                        P.dve(lambda e, blk=blk, bk=bk: e.tensor_copy(out=VS[:, blk, :], in_=PS[:, bk, :]),
                              reads=[ps_b[bk]], writes=[vs_b[blk]])
                    if (t["otile"] is not None or is_s) and not DBG.get("nokv"):
                        P.dve(lambda e, bk=bk, ntok=ntok: e.tensor_copy(out=STV[0:ntok, :], in_=PS[0:ntok, bk, :]),
                              reads=[ps_b[bk]], writes=[stv_b])
                        for hh in range(8):
                            if is_s:
                                dsto = v_s[t["seq"], hh]
                            else:
                                ot = t["otile"]
                                dsto = v_o[hh, ot * 128:(ot + 1) * 128, :]
                            if DBG.get("nodma"):
                                continue
                            P.dma("sp", lambda e, dsto=dsto, nr=t["nreal"], hh=hh: e.dma_start(
                                out=dsto, in_=STV[0:nr, hh * 64:(hh + 1) * 64]), reads=[stv_b])
                elif cbk == 5:
                    P.dve(lambda e, bk=bk, ntok=ntok: e.tensor_copy(out=SKB[0:ntok, :], in_=PS[0:ntok, bk, :]),
                          reads=[ps_b[bk]], writes=[skb_b])
                    tb = tp_rot.next()
                    P.pe(lambda e, tb=tb, ntok=ntok: transposes(e, tb, SKB, ntok, 4), reads=[skb_b, c_b], writes=[ps_b[tb]])
                    srcp = PSB[:, tb, 0:512].rearrange("p (a b) -> p a b", a=4)[:, :, 0:ntok]
                    if is_s:
                        s = t["seq"]
                        dst = KTOWN[:, :, s * 128:(s + 1) * 128]
                        P.dve(lambda e, dst=dst, srcp=srcp: e.tensor_copy(out=dst, in_=srcp),
                              reads=[ps_b[tb]], writes=[ktown_b[s]])
                    else:
                        blk = t["blk"]
                        dst = KT[:, :, blk * 128:(blk + 1) * 128]
                        P.dve(lambda e, dst=dst, srcp=srcp: e.tensor_copy(out=dst, in_=srcp),
                              reads=[ps_b[tb]], writes=[kt_b[blk]])
                    if (t["otile"] is not None or is_s) and not DBG.get("nokv"):
                        P.dve(lambda e, bk=bk, ntok=ntok: e.tensor_copy(out=STK[0:ntok, :], in_=PS[0:ntok, bk, :]),
                              reads=[ps_b[bk]], writes=[stk_b])
                        for hh in range(8):
                            if is_s:
                                dsto = k_s[t["seq"], hh]
                            else:
                                ot = t["otile"]
                                dsto = k_o[hh, ot * 128:(ot + 1) * 128, :]
                            if DBG.get("nodma"):
                                continue
                            P.dma("sp", lambda e, dsto=dsto, nr=t["nreal"], hh=hh: e.dma_start(
                                out=dsto, in_=STK[0:nr, hh * 64:(hh + 1) * 64]), reads=[stk_b])
                else:
                    P.dve(lambda e, bk=bk, ntok=ntok: e.tensor_copy(out=SQB[0:ntok, :], in_=PS[0:ntok, bk, :]),
                          reads=[ps_b[bk]], writes=[sqb_b])
                    tb = tp_rot.next()
                    P.pe(lambda e, tb=tb, ntok=ntok: transposes(e, tb, SQB, ntok, 4), reads=[sqb_b, c_b], writes=[ps_b[tb]])
                    srcp = PSB[:, tb, 0:512].rearrange("p (a b) -> p a b", a=4)[:, :, 0:ntok]
                    def f_sq(e, srcp=srcp, col0=col0, ntok=ntok):
                        e.tensor_copy(out=SQTE[0:64, :, col0:col0 + ntok], in_=srcp[0:64])
                        return e.tensor_copy(out=SQTO[64:128, :, col0:col0 + ntok], in_=srcp[64:128])
                    P.dve(f_sq, reads=[ps_b[tb]], writes=[sqt_b[li]])
        SOT = MIXT[:, 4:8, :]
        if stop_after == "sbproj":
            return
        if not is_s:
            b0 = tiles[0]["blk"]
            if gi in conv_sched:
                conv_pieces(conv_sched[gi])

            def mk_kb(blk, mask):
                return {"nk": 128,
                        "kt": (lambda c, hp, blk=blk: KT[:, c, blk * 128:(blk + 1) * 128]),
                        "v": (lambda c, blk=blk: VS[:, blk, c * 128:(c + 1) * 128]),
                        "bufs": [kt_b[blk], vs_b[blk]], "mask": mask}
            for (qa, qb) in ([(0, 4), (4, 5)] if nt == 5 else [(0, nt)]):
                kbs = []
                for blk in range(0, b0 + qb):
                    j = blk - b0
                    if j >= qa:
                        jj = j - qa
                        mask = ((jj + 1) * 128, (4 - jj) * 128)
                    else:
                        mask = None
                    kbs.append(mk_kb(blk, mask))
                sb_attention(SQT, [sqt_b[i] for i in range(qa, qb)], qa * 128, (qb - qa) * 128, kbs, SOT, sot_b, tmp,
                             hook=(conv_hook if gi in conv_sched else None))
            if gi in conv_sched:
                conv_step(1000)
                conv_done.update(conv_sched[gi])
            if gi == 3:
                prefetch_sample_v(0)
        else:
            kt_rot = Rot([3, 5])

            def k_steps(s):
                steps = []
                for q8 in range(8):
                    ki = q8 % 2

                    def dma_step(s=s, q8=q8, ki=ki):
                        for bb in range(2):
                            P.dma("pool", lambda e, bb=bb: e.dma_start(
                                out=KSTG[ki][:, bb, :].rearrange("p (h d) -> p h d", h=8),
                                in_=ck[s, :, (q8 * 2 + bb) * 128:(q8 * 2 + bb + 1) * 128, :].rearrange("h p d -> p h d")),
                                writes=[kstg_b[ki]])

                    def tr_step(s=s, q8=q8, ki=ki):
                        for bb in range(2):
                            blk = s * 16 + q8 * 2 + bb
                            tb = kt_rot.next()
                            P.pe(lambda e, tb=tb, bb=bb: transposes(e, tb, KSTG[ki][:, bb, :], 128, 4),
                                 reads=[kstg_b[ki], c_b], writes=[ps_b[tb]])
                            dst = KT[:, :, blk * 128:(blk + 1) * 128]
                            srcp = PSB[:, tb, 0:512].rearrange("p (a b) -> p a b", a=4)
                            P.dve(lambda e, dst=dst, srcp=srcp: e.tensor_copy(out=dst, in_=srcp),
                                  reads=[ps_b[tb]], writes=[kt_b[blk]])
                    steps.append(dma_step)
                    if q8 >= 1:
                        steps.append(prev_tr)
                    prev_tr = tr_step
                steps.append(prev_tr)
                return steps

            def kbs_for(s):
                kbs = []
                for blk in range(16):
                    gb = s * 16 + blk
                    kbs.append({"nk": 128,
                                "kt": (lambda c, hp, gb=gb: KT[:, c, gb * 128:(gb + 1) * 128]),
                                "v": (lambda c, gb=gb: VS[:, gb, c * 128:(c + 1) * 128]),
                                "bufs": [kt_b[gb], vs_b[gb]], "mask": None})
                kbs.append({"nk": 128,
                            "kt": (lambda c, hp, s=s: KTOWN[:, c, s * 128:(s + 1) * 128]),
                            "v": (lambda c, s=s: VOWN[:, s, c * 128:(c + 1) * 128]),
                            "bufs": [ktown_b[s], vown_b[s]], "mask": (128, 4 * 128)})
                return kbs
            for st_ in k_steps(tiles[0]["seq"]):
                st_()
            pend = k_steps(tiles[1]["seq"])
            cnt_ = [0]

            def k_hook():
                cnt_[0] += 1
                if cnt_[0] % 2 == 0 and pend:
                    pend.pop(0)()
            sb_attention(SQT, [sqt_b[0]], tiles[0]["col0"], 128, kbs_for(tiles[0]["seq"]), SOT, sot_b, tmp, hook=k_hook)
            while pend:
                pend.pop(0)()
            sb_attention(SQT, [sqt_b[1]], tiles[1]["col0"], 128, kbs_for(tiles[1]["seq"]), SOT, sot_b, tmp)
        if stop_after == "sb":
            return
        fence()
        AR.off = mark
        load_ln(0)
        sis = [wload(wview(w_o, cbk * 512)) for cbk in range(2)]
        xbi = {}

        def _wo_mm(li, t):
            ntok, col0 = t["ntok"], t["col0"]
            for cbk in range(2):
                bk = pj_rot.next()

                def f(e, bk=bk, cbk=cbk, ntok=ntok, col0=col0):
                    ins = None
                    for kc in range(8):
                        ins = e.matmul(PS[0:ntok, bk, :], MIXT[:, kc, col0:col0 + ntok], WS[sis[cbk]][:, kc, :],
                                       start=(kc == 0), stop=(kc == 7))
                    return ins
                P.pe(f, reads=[mixt_b[li], sot_b, ws_b[sis[cbk]]], writes=[ps_b[bk]])
                resid_add(li, ntok, cbk, bk)
        _wo_mm(0, tiles[0])
        for li, t in enumerate(tiles):
            if li + 1 < nt:
                _wo_mm(li + 1, tiles[li + 1])
            xbi[li] = layer_norm(li, t["ntok"])
            to_xt(xbi[li], li, t["ntok"], t["col0"])
        if stop_after == "ln1":
            return
        fence()
        AR.reset()
        deep_b = (not is_s) and ntot <= 512
        QMT = AR.take([128, 8, 512 if ntot <= 512 else 640], BF16)
        if deep_b:
            OTs = [AR.take([128, 8, 128], BF16) for _ in range(2)]
        else:
            OTs = [AR.take([128, 8, 128], BF16)] * 2
        PF = AR.take([128, D], F32)
        MKTM_S = PF.bitcast(BF16).rearrange("p (a b) -> p a b", a=2)
        PBFs = [AR.take([128, D], BF16) for _ in range(2)]
        PTs = [AR.take([128, 8, 128], BF16) for _ in range(2)]
        MM["MKT"] = AR.take([128, 8, 256], BF16)
        MM["MV"] = AR.take([128, 2, D], BF16)
        MKT, MV = MM["MKT"], MM["MV"]
        qmt_b = Buf("qmt")
        ots_b = [Buf("ot0"), Buf("ot1")] if deep_b else [Buf("ot0")] * 2
        pf_b = Buf("pf")
        pbfs_b = [Buf("pbf0"), Buf("pbf1")]
        pts_b = [Buf("pt0"), Buf("pt1")]
        mktm_s_b = pf_b
        load_ln(2)
        if not is_s:
            mem_load(mk_o, mv_o, MKTM_S, mktm_s_b, [mko_b])
        if gi == 3:
            prefetch_sample_v(1)
        pieces = [(c0, min(512, ntot - c0)) for c0 in range(0, ntot, 512)]
        for cbk in range(2):
            si = wload(wview(w_q, cbk * 512))
            for fc in range(4):
                fch = cbk * 4 + fc
                pb = [0, 1] if fc % 2 == 0 else [2, 3]

                def f(e, si=si, fc=fc, pb=pb):
                    ins = None
                    for pi, (c0, n) in enumerate(pieces):
                        for kc in range(8):
                            ins = e.matmul(PS[:, pb[pi], 0:n], WS[si][:, kc, fc * 128:(fc + 1) * 128],
                                           XT[:, kc, c0:c0 + n], start=(kc == 0), stop=(kc == 7))
                    return ins
                P.pe(f, reads=[xt_b[i] for i in range(nt)] + [ws_b[si]], writes=[ps_b[pb[i]] for i in range(len(pieces))])

                def g(e, fch=fch, pb=pb):
                    ins = None
                    for pi, (c0, n) in enumerate(pieces):
                        ins = e.activation(out=QMT[:, fch, c0:c0 + n], in_=PS[:, pb[pi], 0:n], func=AF.Identity)
                    return ins
                P.act(g, reads=[ps_b[pb[i]] for i in range(len(pieces))], writes=[qmt_b])
        sis = [wload(wview(w_om, cbk * 512)) for cbk in range(2)]

        def _b_tile(li, t):
            ntok, col0 = t["ntok"], t["col0"]
            so, sm_b = small_next()
            if is_s:
                mem_load(cmk[t["seq"]], cmv[t["seq"]], MKTM_S, mktm_s_b, [])
            oi = li % 2
            OT, ot_bb = OTs[oi], ots_b[oi]
            PBF, pbf_b, PT, pt_b = PBFs[oi], pbfs_b[oi], PTs[oi], pts_b[oi]

            def f_s(e, ntok=ntok, col0=col0):
                ins = None
                for h in range(4):
                    for hf in range(2):
                        ins = e.matmul(PS[0:ntok, 4 + h // 2, (h % 2) * 256:(h % 2) * 256 + 256],
                                       QMT[:, 2 * h + hf, col0:col0 + ntok], MKT[:, 2 * h + hf, :],
                                       start=(hf == 0), stop=(hf == 1))
                return ins
            P.pe(f_s, reads=[qmt_b, mkt_b], writes=[ps_b[4], ps_b[5]])
            sc3 = PS[0:ntok, 4:6, :].rearrange("p a (b m) -> p (a b) m", b=2)
            mx = SMALL[0:ntok, so + 24:so + 28]
            nmx = SMALL[0:ntok, so + 28:so + 32]

            P.chain("dve", [lambda e, sc3=sc3, mx=mx: e.tensor_reduce(out=mx, in_=sc3, axis=AX.X, op=ALU.max),
                            lambda e, mx=mx, nmx=nmx: e.tensor_scalar(out=nmx, in0=mx, scalar1=-1.0 / 16.0,
                                                                      scalar2=None, op0=ALU.mult)],
                    reads=[ps_b[4], ps_b[5]], writes=[sm_b])

            def f_e(e, ntok=ntok):
                ins = None
                for h in range(4):
                    ins = e.activation(out=PF[0:ntok, h * 256:(h + 1) * 256],
                                       in_=PS[0:ntok, 4 + h // 2, (h % 2) * 256:(h % 2) * 256 + 256],
                                       func=AF.Exp, bias=SMALL[0:ntok, so + 28 + h:so + 29 + h], scale=1.0 / 16.0,
                                       accum_out=SMALL[0:ntok, so + 32 + h:so + 33 + h])
                return ins
            P.act(f_e, reads=[ps_b[4], ps_b[5], sm_b], writes=[pf_b, sm_b])

            def f_n(e, ntok=ntok):
                ins = None
                for h in range(4):
                    ins = e.tensor_scalar(out=PBF[0:ntok, h * 256:(h + 1) * 256], in0=PF[0:ntok, h * 256:(h + 1) * 256],
                                          scalar1=SMALL[0:ntok, so + 36 + h:so + 37 + h], scalar2=None, op0=ALU.mult)
                return ins
            P.chain("dve", [lambda e, ntok=ntok: e.reciprocal(out=SMALL[0:ntok, so + 36:so + 40], in_=SMALL[0:ntok, so + 32:so + 36]),
                            f_n], reads=[pf_b, sm_b], writes=[pbf_b, sm_b])
            yield
            tb = tp_rot.next()
            P.pe(lambda e, tb=tb, ntok=ntok: transposes(e, tb, PBF, ntok, 8), reads=[pbf_b, c_b], writes=[ps_b[tb]])
            srcp = PSB[:, tb, :].rearrange("p (a b) -> p a b", a=8)[:, :, 0:ntok]
            P.act(lambda e, srcp=srcp, ntok=ntok, PT=PT: e.activation(out=PT[:, :, 0:ntok], in_=srcp, func=AF.Identity),
                  reads=[ps_b[tb]], writes=[pt_b])

            def f_pv(e, ntok=ntok):
                ins = None
                for h in range(4):
                    for hf in range(2):
                        och = 2 * h + hf
                        for mc in range(2):
                            ins = e.matmul(PS[:, 6 + och // 4, (och % 4) * 128:(och % 4) * 128 + ntok],
                                           MV[:, mc, h * 256 + hf * 128:h * 256 + hf * 128 + 128],
                                           PT[:, 2 * h + mc, 0:ntok], start=(mc == 0), stop=(mc == 1))
                return ins
            P.pe(f_pv, reads=[pt_b, mv_b], writes=[ps_b[6], ps_b[7]])
            dst = OT[:, :, 0:ntok]
            srcp2 = PS[:, 6:8, :].rearrange("p a (b m) -> p (a b) m", b=4)[:, :, 0:ntok]
            P.act(lambda e, dst=dst, srcp2=srcp2: e.activation(out=dst, in_=srcp2, func=AF.Identity),
                  reads=[ps_b[6], ps_b[7]], writes=[ot_bb])
            yield
            for cbk in range(2):
                bk = pj_rot.next()

                def f(e, bk=bk, cbk=cbk, ntok=ntok, OT=OT):
                    ins = None
                    for kc in range(8):
                        ins = e.matmul(PS[0:ntok, bk, :], OT[:, kc, 0:ntok], WS[sis[cbk]][:, kc, :],
                                       start=(kc == 0), stop=(kc == 7))
                    return ins
                P.pe(f, reads=[ot_bb, ws_b[sis[cbk]]], writes=[ps_b[bk]])
                resid_add(li, ntok, cbk, bk)
            xbi[li] = layer_norm(li, ntok)
            to_xt(xbi[li], li, ntok, col0)
        gens = [_b_tile(li, t) for li, t in enumerate(tiles)]

        def adv(g_):
            try:
                next(g_)
            except StopIteration:
                pass
        if is_s:
            for g_ in gens:
                for _ in g_:
                    pass
        elif deep_b:
            for k in range(-2, nt):
                if 0 <= k + 2 < nt:
                    adv(gens[k + 2])
                if 0 <= k + 1 < nt:
                    adv(gens[k + 1])
                if 0 <= k:
                    adv(gens[k])
        else:
            next(gens[0])
            for li in range(nt):
                if li + 1 < nt:
                    next(gens[li + 1])
                for _ in gens[li]:
                    pass
        if stop_after == "ln2":
            return
        fence()
        AR.reset()
        ncw_c = 640 if ntot > 512 else 512
        ndb = 1 if ntot > 512 else 2
        HT = AR.take([128, NCH, ncw_c], BF16)
        UB = [AR.take([128, ncw_c + 8], F32) for _ in range(2 * ndb)]
        CB = [AR.take([128, ncw_c], F32) for _ in range(2 * ndb)]
        SA = AR.take([128, ncw_c], F32)
        ht_b = [Buf("ht%d" % i) for i in range(NCH)]
        ub_b = [Buf("ub%d" % i) for i in range(2 * ndb)]
        cb_b = [Buf("cb%d" % i) for i in range(2 * ndb)]
        sa_b = Buf("sa")
        if is_s:
            segs = [(t["col0"], t["nreal"], 2 + li * 36) for li, t in enumerate(tiles)]
        else:
            segs = [(0, ntot, 2)]
        pair_rot = Rot([(0, 1), (2, 3), (4, 5), (6, 7)])
        if gi == 3:
            prefetch_sample_v(2)
        for pg in range(6):
            nch = 4 if pg < 5 else 2
            sa_i = wload(wview(w_up, pg * 512, ncols=nch * 128), ncols=nch * 128)
            sg_i = wload(wview(w_up, DFF + pg * 512, ncols=nch * 128), ncols=nch * 128)
            for ci in range(nch):
                chunk_a = pg * 4 + ci
                for which, (si, chunk) in enumerate(((sa_i, chunk_a), (sg_i, NCH + chunk_a))):
                    pb = pair_rot.next()

                    def f(e, si=si, ci=ci, pb=pb):
                        ins = None
                        for pi, (c0, n) in enumerate(pieces):
                            for kc in range(8):
                                ins = e.matmul(PS[:, pb[pi], 0:n], WS[si][:, kc, ci * 128:(ci + 1) * 128],
                                               XT[:, kc, c0:c0 + n], start=(kc == 0), stop=(kc == 7))
                        return ins
                    pbufs = [ps_b[pb[i]] for i in range(len(pieces))]
                    P.pe(f, reads=[xt_b[i] for i in range(nt)] + [ws_b[si]], writes=pbufs)
                    bi_ = which + 2 * (chunk_a % ndb)
                    U, Cb = UB[bi_], CB[bi_]

                    def f_cp(e, pb=pb, U=U, Cb=Cb, chunk=chunk):
                        ins = None
                        for (c0, n, uo) in segs:
                            for pi, (p0, pn) in enumerate(pieces):
                                lo, hi = max(c0, p0), min(c0 + n, p0 + pn)
                                if lo >= hi:
                                    continue
                                e.activation(out=U[:, uo + lo - c0:uo + hi - c0], in_=PS[:, pb[pi], lo - p0:hi - p0],
                                             func=AF.Identity)
                                ins = e.activation(out=Cb[:, lo:hi], in_=PS[:, pb[pi], lo - p0:hi - p0],
                                                   func=AF.Identity, scale=CONVP[:, 2, chunk:chunk + 1],
                                                   bias=CONVP[:, 3, chunk:chunk + 1])
                        return ins
                    P.act(f_cp, reads=pbufs + [c_b], writes=[ub_b[bi_], cb_b[bi_]])

                    def f_cv0(e, U=U, chunk=chunk):
                        ins = e.tensor_copy(out=U[:, 0:2], in_=USTATE[:, chunk, :])
                        if gi == 0:
                            ins = e.tensor_scalar(out=U[:, 2 + 126:2 + 128], in0=U[:, 2 + 126:2 + 128],
                                                  scalar1=FLAG[:, 0:1], scalar2=None, op0=ALU.mult)
                        return ins

                    def f_cv1(e, U=U, Cb=Cb, chunk=chunk):
                        ins = None
                        for sgi, (c0, n, uo) in enumerate(segs):
                            ins = e.scalar_tensor_tensor(out=Cb[:, c0:c0 + n], in0=U[:, uo - 1:uo - 1 + n],
                                                         scalar=CONVP[:, 1, chunk:chunk + 1], in1=Cb[:, c0:c0 + n],
                                                         op0=ALU.mult, op1=ALU.add)
                        return ins

                    def f_cv2(e, U=U, Cb=Cb, chunk=chunk):
                        ins = None
                        for sgi, (c0, n, uo) in enumerate(segs):
                            e.scalar_tensor_tensor(out=Cb[:, c0:c0 + n], in0=U[:, uo - 2:uo - 2 + n],
                                                   scalar=CONVP[:, 0, chunk:chunk + 1], in1=Cb[:, c0:c0 + n],
                                                   op0=ALU.mult, op1=ALU.add)
                            if is_s:
                                ins = e.tensor_copy(out=USTS[:, sgi, chunk, :], in_=U[:, uo + n - 2:uo + n])
                            else:
                                ins = e.tensor_copy(out=USTATE[:, chunk, :], in_=U[:, uo + n - 2:uo + n])
                        return ins
                    if is_s:
                        def f_st(e, U=U, chunk=chunk):
                            ins = None
                            for sgi, (c0, n, uo) in enumerate(segs):
                                ins = e.tensor_copy(out=U[:, uo - 2:uo], in_=USTS_IN[:, sgi, chunk, :])
                            return ins
                        P.dve(f_st, reads=[usin_b], writes=[ub_b[bi_]])
                    P.chain("dve", ([] if is_s else [f_cv0]) + [f_cv1, f_cv2],
                            reads=[ub_b[bi_], cb_b[bi_], us_b, c_b], writes=[cb_b[bi_], us_b, ub_b[bi_]])
                ia_, ig_ = 2 * (chunk_a % ndb), 1 + 2 * (chunk_a % ndb)
                P.act(lambda e, ia_=ia_: e.activation(out=SA[:, 0:ntot], in_=CB[ia_][:, 0:ntot], func=AF.Silu),
                      reads=[cb_b[ia_]], writes=[sa_b])
                P.dve(lambda e, chunk_a=chunk_a, ig_=ig_: e.tensor_tensor(out=HT[:, chunk_a, 0:ntot], in0=SA[:, 0:ntot],
                                                                         in1=CB[ig_][:, 0:ntot], op=ALU.mult),
                      reads=[sa_b, cb_b[ig_]], writes=[ht_b[chunk_a]])
        with nc.allow_non_contiguous_dma(reason="feature-major conv state scatter"):
            if is_s:
                for sgi, t in enumerate(tiles):
                    for r in range(2):
                        P.dma("sp", lambda e, sgi=sgi, t=t, r=r: e.dma_start(
                            out=conv_s[t["seq"], r, :].rearrange("(c p) -> p c", p=128), in_=USTS[:, sgi, :, r],
                            allow_slow_non_contiguous=True),
                            reads=[us_b])
            elif tiles[-1].get("last"):
                for r in range(2):
                    P.dma("sp", lambda e, r=r: e.dma_start(out=conv_o[r, :].rearrange("(c p) -> p c", p=128),
                                                           in_=USTATE[:, :, r], allow_slow_non_contiguous=True),
                          reads=[us_b])
        if stop_after == "conv":
            return
        if gi == 3:
            prefetch_sample_v(3)
        for cbk in range(2):
            sl = [wload(wview(w_dn, cbk * 512, nk=8, k0=0)), wload(wview(w_dn, cbk * 512, nk=8, k0=8)),
                  wload(wview(w_dn, cbk * 512, nk=6, k0=16), nk=6)]
            for li, t in enumerate(tiles):
                ntok, col0 = t["ntok"], t["col0"]
                bk = pj_rot.next()

                def f(e, bk=bk, ntok=ntok, col0=col0, sl=sl):
                    ins = None
                    for ch in range(NCH):
                        ins = e.matmul(PS[0:ntok, bk, :], HT[:, ch, col0:col0 + ntok], WS[sl[ch // 8]][:, ch % 8, :],
                                       start=(ch == 0), stop=(ch == NCH - 1))
                    return ins
                P.pe(f, reads=ht_b + [ws_b[i] for i in sl], writes=[ps_b[bk]])
                resid_add(li, ntok, cbk, bk)
        fence()
        AR.reset()
        load_ln(4)
        for li, t in enumerate(tiles):
            ntok = t["ntok"]
            layer_norm(li, ntok, want_xb=False)
            if is_s:
                dsto = y_s[t["seq"]]
            elif t["otile"] is not None:
                dsto = y_o[t["otile"] * 128:(t["otile"] + 1) * 128, :]
            else:
                continue
            P.dma("sp", lambda e, dsto=dsto, li=li, nr=t["nreal"]: e.dma_start(out=dsto, in_=X[0:nr, li, :]),
                  reads=[x_b[li]])

    USTS_IN = T("USTS_IN", [128, 2, 44, 2], F32)
    USTS = T("USTS", [128, 2, 44, 2], F32)
    usin_b = Buf("usts_in")
    with nc.allow_non_contiguous_dma(reason="feature-major conv state gather"):
        for s in range(2):
            for r in range(2):
                P.dma("sp", lambda e, s=s, r=r: e.dma_start(out=USTS_IN[:, s, :, r],
                                                            in_=scv[s, r, :].rearrange("(c p) -> p c", p=128),
                                                            allow_slow_non_contiguous=True),
                      writes=[usin_b], extra=False)

    mem_phase_prompt()
    if stop_after != "mem":
        history_phase()
        groups = [[0, 1, 2, 3, 4], [5, 6, 7, 8], [9, 10, 11, 12], [13, 14, 15, 16]]
        for gi, g in enumerate(groups):
            tiles = []
            for li, ti in enumerate(g):
                tiles.append({"ntok": 128, "nreal": 128, "col0": li * 128, "kind": "p", "blk": NH_TILES + ti,
                              "otile": (ti - 1) if ti >= 1 else None,
                              "xsrc": xo[ti * 128:(ti + 1) * 128, :], "rope": ropeo[ti],
                              "last": ti == 16})
            main_group(gi, tiles)
            if stop_after in ("g0",) or (stop_after is not None and stop_after not in ("prompt",)):
                break
        if stop_after is None:
            tiles = [{"ntok": 128, "nreal": 32, "col0": s * 128, "kind": "s", "seq": s, "otile": None,
                      "xsrc": xs[s], "rope": ropes} for s in range(2)]
            main_group(4, tiles)
    P.emit(nc, st)
    st.close()
    return nc


def _rope_table(pos):
    half = 64
    inv = (1.0 / (np.float32(10000.0) ** (np.arange(half, dtype=np.float32) / np.float32(half)))).astype(np.float32)
    ang = pos.astype(np.float32)[:, None] * inv[None, :]
    c = np.cos(ang).astype(np.float32)
    s = np.sin(ang).astype(np.float32)
    return np.concatenate([c, c, -s, s], axis=1).astype(np.float32)


def _consts():
    ident = np.eye(128, dtype=np.float32)
    j = np.arange(128)[:, None]
    s_ = np.arange(128)[None, :]
    negu = -(j >= s_).astype(np.float32)
    nego = -np.ones((128, 128), np.float32)
    diag = np.where(j < s_, 0.0, NEG).astype(np.float32)
    negm = np.concatenate([np.full((128, 512), NEG, np.float32), diag], axis=1)
    m_ = np.arange(128)
    m01h = (m_[None, :] >= m_[:, None]).astype(np.float32)
    cb16 = np.concatenate([ident, negu, nego, negm, m01h, m01h, m01h, m01h], axis=1)
    g = (1.0 - 2.0 ** (-5.0 - np.arange(4, dtype=np.float64)))
    m = np.arange(128)
    kd = np.zeros((128, 4), np.float32)
    epsc = np.zeros((128, 4), np.float32)
    for h in range(4):
        kd[:, h] = (128.0 ** -0.5) * g[h] ** (-(m + 1.0))
        epsc[:, h] = RMS_EPS / (g[h] ** (2.0 * (m + 1.0)))
    cf32 = np.concatenate([kd, epsc], axis=1).astype(np.float32)
    return cb16, cf32


_NC_CACHE = {}


def kernel(x_prompt, x_sample, cache_sb_k, cache_sb_v, state_ret, state_ffn_conv,
           cache_mem_k, cache_mem_v, mem_prompt,
           w_in, w_o, ln1_g, ln1_b, w_q_mem, w_k_mem, w_v_mem, w_o_mem, ln2_g, ln2_b,
           w_up, conv_w, conv_b, w_down, ln3_g, ln3_b, _stop_after=None):
    f = lambda a: np.ascontiguousarray(np.asarray(a, dtype=np.float32))
    x_prompt, x_sample = f(x_prompt), f(x_sample)
    cache_sb_k, cache_sb_v = f(cache_sb_k), f(cache_sb_v)
    state_ret, state_ffn_conv = f(state_ret), f(state_ffn_conv)
    cache_mem_k, cache_mem_v, mem_prompt = f(cache_mem_k), f(cache_mem_v), f(mem_prompt)
    cb16, cf32 = _consts()
    lnp = np.stack([f(ln1_g)[0], f(ln1_b)[0], f(ln2_g)[0], f(ln2_b)[0], f(ln3_g)[0], f(ln3_b)[0]])
    convp = np.concatenate([f(conv_w)[0], f(conv_b)], axis=0)
    shared = {"w_in": f(w_in)[0], "w_o": f(w_o)[0], "w_q": f(w_q_mem)[0], "w_k": f(w_k_mem)[0],
              "w_v": f(w_v_mem)[0], "w_om": f(w_o_mem)[0], "w_up": f(w_up)[0], "w_dn": f(w_down)[0],
              "lnp": np.ascontiguousarray(lnp), "convp": np.ascontiguousarray(convp),
              "cb16": cb16, "cf32": cf32, "ropes": _rope_table(2048 + np.arange(32))}
    in_maps = []
    for c in range(8):
        b, half = c // 2, c % 2
        m = dict(shared)
        if half == 0:
            m["xh"] = np.zeros((NH_TILES * 128, D), np.float32)
            m["xo"] = np.concatenate([np.zeros((128, D), np.float32), x_prompt[b, 0:2048]], axis=0)
            pos_o = np.concatenate([np.zeros(128), np.arange(2048)])
            pos_h = np.zeros(NH_TILES * 128)
            m["flag"] = np.zeros((128, 1), np.float32)
        else:
            m["xh"] = x_prompt[b, 0:NH_TILES * 128]
            m["xo"] = x_prompt[b, NH_TILES * 128:4096]
            pos_o = NH_TILES * 128 + np.arange(NO_TILES * 128)
            pos_h = np.arange(NH_TILES * 128)
            m["flag"] = np.ones((128, 1), np.float32)
        m["ropeo"] = _rope_table(pos_o).reshape(NO_TILES, 128, 256)
        m["ropeh"] = _rope_table(pos_h).reshape(NH_TILES, 128, 256)
        m["xs"] = x_sample[2 * c:2 * c + 2]
        m["memp"] = mem_prompt[b]
        m["ck"] = cache_sb_k[0, 2 * c:2 * c + 2]
        m["cv"] = cache_sb_v[0, 2 * c:2 * c + 2]
        m["sr"] = state_ret[0, 2 * c:2 * c + 2]
        m["scv"] = state_ffn_conv[0, 2 * c:2 * c + 2]
        m["cmk"] = cache_mem_k[0, 2 * c:2 * c + 2]
        m["cmv"] = cache_mem_v[0, 2 * c:2 * c + 2]
        in_maps.append({k: np.ascontiguousarray(v, dtype=np.float32) for k, v in m.items()})
    key = _stop_after
    if key not in _NC_CACHE:
        _NC_CACHE[key] = build_program(_stop_after)
    nc = _NC_CACHE[key]
    res = run_bass_kernel_spmd(nc, in_maps, core_ids=list(range(8)))
    R = res.results
    y_prompt = np.zeros((4, 4096, D), np.float32)
    nk = np.zeros((1, 4, 8, 4096, 64), np.float32)
    nv = np.zeros((1, 4, 8, 4096, 64), np.float32)
    for c in range(8):
        b, half = c // 2, c % 2
        y_prompt[b, half * 2048:(half + 1) * 2048] = R[c]["y_o"]
        nk[0, b, :, half * 2048:(half + 1) * 2048] = R[c]["k_o"]
        nv[0, b, :, half * 2048:(half + 1) * 2048] = R[c]["v_o"]
    y_sample = np.concatenate([R[c]["y_s"] for c in range(8)], axis=0)
    sret_p = np.stack([R[2 * b + 1]["sret_o"] for b in range(4)])[None]
    conv_p = np.stack([R[2 * b + 1]["conv_o"] for b in range(4)])[None]
    mk_p = np.stack([R[2 * b]["mk_o"] for b in range(4)])[None]
    mv_p = np.stack([R[2 * b]["mv_o"] for b in range(4)])[None]
    ks = np.concatenate([R[c]["k_s"] for c in range(8)], axis=0)[None]
    vs = np.concatenate([R[c]["v_s"] for c in range(8)], axis=0)[None]
    sret_s = np.concatenate([R[c]["sret_s"] for c in range(8)], axis=0)[None]
    conv_s = np.concatenate([R[c]["conv_s"] for c in range(8)], axis=0)[None]
    return (y_prompt, y_sample, nk, nv, sret_p, conv_p, mk_p, mv_p, ks, vs, sret_s, conv_s)
```
